# Optimizing a Trainium2 kernel written in Bass

```python
import math
import jax, jax.numpy as jnp
from jax import lax
import numpy as np

D_MODEL = 1024
BATCH = 8
SEQ = 4096
DEPTH = 4

N_MIXERS = 2
SSM_GROUP = 16
SSM_GROUPS = D_MODEL // SSM_GROUP
SSM_STATE = 64
CONV_WIDTH = 3
FFN_HIDDEN = ((8 * D_MODEL + 3 * 256 - 1) // (3 * 256)) * 256
N_SSM_LAYERS = (DEPTH + 1) // 2
N_CONV_LAYERS = DEPTH // 2
RMS_EPS = 1e-6
DT_MIN = 1e-3
DT_MAX = 1e-1

kernel_name = "hybrid_s5_shortconv_adaln"


def rms_norm(x, g):
    x32 = x.astype(jnp.float32)
    y = x32 * lax.rsqrt(jnp.mean(x32 * x32, axis=-1, keepdims=True) + RMS_EPS)
    return (y * g.astype(jnp.float32)).astype(x.dtype)


def modulate(h, shift, scale):
    return h * (1.0 + scale) + shift


def _ssm_combine(e1, e2):
    a1r, a1i, b1r, b1i = e1
    a2r, a2i, b2r, b2i = e2
    ar = a2r * a1r - a2i * a1i
    ai = a2r * a1i + a2i * a1r
    br = a2r * b1r - a2i * b1i + b2r
    bi = a2r * b1i + a2i * b1r + b2i
    return (ar, ai, br, bi)


def s5_mixer(h, a_re, a_im, log_step, b_re, b_im, c_re, c_im, d, w_out):
    bsz, seqlen, _ = h.shape
    f32 = jnp.float32
    u = h.astype(f32).reshape(bsz, seqlen, SSM_GROUPS, SSM_GROUP)
    lr = jnp.minimum(a_re.astype(f32), -1e-4)
    li = a_im.astype(f32)
    dt = jnp.exp(log_step.astype(f32))[:, None]
    mag = jnp.exp(lr * dt)
    abr = mag * jnp.cos(li * dt)
    abi = mag * jnp.sin(li * dt)
    den = lr * lr + li * li
    qr = ((abr - 1.0) * lr + abi * li) / den
    qi = (abi * lr - (abr - 1.0) * li) / den
    br, bim = b_re.astype(f32), b_im.astype(f32)
    bbar_re = qr[..., None] * br - qi[..., None] * bim
    bbar_im = qr[..., None] * bim + qi[..., None] * br
    bu_re = jnp.einsum('blgh,gph->blgp', u, bbar_re)
    bu_im = jnp.einsum('blgh,gph->blgp', u, bbar_im)
    a_r = jnp.broadcast_to(abr[None, None], (1, seqlen, SSM_GROUPS, SSM_STATE))
    a_i = jnp.broadcast_to(abi[None, None], (1, seqlen, SSM_GROUPS, SSM_STATE))
    _, _, xr, xi = lax.associative_scan(_ssm_combine, (a_r, a_i, bu_re, bu_im), axis=1)
    y = (jnp.einsum('ghp,blgp->blgh', c_re.astype(f32), xr)
         - jnp.einsum('ghp,blgp->blgh', c_im.astype(f32), xi))
    y = y.reshape(bsz, seqlen, D_MODEL) + d.astype(f32) * h.astype(f32)
    y = jax.nn.gelu(y).astype(h.dtype)
    val, gate = jnp.split(y @ w_out, 2, axis=-1)
    return val * jax.nn.sigmoid(gate)


def short_conv_mixer(h, w_in, conv_w, w_out):
    bg, cg, v = jnp.split(h @ w_in, 3, axis=-1)
    cv = cg * v
    conv = lax.conv_general_dilated(
        cv, conv_w[:, None, :].astype(cv.dtype), window_strides=(1,),
        padding=[(CONV_WIDTH - 1, 0)], dimension_numbers=('NWC', 'WIO', 'NWC'),
        feature_group_count=D_MODEL)
    return (bg * conv) @ w_out


def swiglu(h, w_in, w_out):
    g, u = jnp.split(h @ w_in, 2, axis=-1)
    return (jax.nn.silu(g) * u) @ w_out


def setup_inputs(seed: int = 0) -> dict:
    key = jax.random.key(seed)
    ks = jax.random.split(key, 24)
    D, G, P, H, F = D_MODEL, SSM_GROUPS, SSM_STATE, SSM_GROUP, FFN_HIDDEN
    nrm = jax.random.normal
    x = nrm(ks[0], (BATCH, SEQ, D), jnp.float32)
    c = nrm(ks[1], (BATCH, D), jnp.float32)
    norm1_g = 1.0 + 0.02 * nrm(ks[2], (DEPTH, D), jnp.float32)
    norm2_g = 1.0 + 0.02 * nrm(ks[3], (DEPTH, D), jnp.float32)
    w_ada = 0.5 * D ** -0.5 * nrm(ks[4], (DEPTH, D, 6 * D), jnp.float32)
    b_ada = 0.01 * nrm(ks[5], (DEPTH, 6 * D), jnp.float32)
    n_idx = jnp.arange(P, dtype=jnp.float32)
    ssm_a_re = -0.5 + 0.01 * nrm(ks[6], (N_SSM_LAYERS, G, P), jnp.float32)
    ssm_a_im = math.pi * n_idx + 0.01 * nrm(ks[7], (N_SSM_LAYERS, G, P), jnp.float32)
    ssm_log_step = jax.random.uniform(ks[8], (N_SSM_LAYERS, G), jnp.float32,
                                      math.log(DT_MIN), math.log(DT_MAX))
    ssm_b_re = (2 * H) ** -0.5 * nrm(ks[9], (N_SSM_LAYERS, G, P, H), jnp.float32)
    ssm_b_im = (2 * H) ** -0.5 * nrm(ks[10], (N_SSM_LAYERS, G, P, H), jnp.float32)
    ssm_c_re = (2 * P) ** -0.5 * nrm(ks[11], (N_SSM_LAYERS, G, H, P), jnp.float32)
    ssm_c_im = (2 * P) ** -0.5 * nrm(ks[12], (N_SSM_LAYERS, G, H, P), jnp.float32)
    ssm_d = nrm(ks[13], (N_SSM_LAYERS, D), jnp.float32)
    ssm_w_out = D ** -0.5 * nrm(ks[14], (N_SSM_LAYERS, D, 2 * D), jnp.float32)
    conv_w_in = D ** -0.5 * nrm(ks[15], (N_CONV_LAYERS, D, 3 * D), jnp.float32)
    conv_w = CONV_WIDTH ** -0.5 * nrm(ks[16], (N_CONV_LAYERS, CONV_WIDTH, D), jnp.float32)
    conv_w_out = D ** -0.5 * nrm(ks[17], (N_CONV_LAYERS, D, D), jnp.float32)
    w_ffn_in = D ** -0.5 * nrm(ks[18], (DEPTH, D, 2 * F), jnp.float32)
    w_ffn_out = F ** -0.5 * nrm(ks[19], (DEPTH, F, D), jnp.float32)
    final_g = 1.0 + 0.02 * nrm(ks[20], (D,), jnp.float32)
    return {"x": x, "c": c, "norm1_g": norm1_g, "norm2_g": norm2_g,
            "w_ada": w_ada, "b_ada": b_ada,
            "ssm_a_re": ssm_a_re, "ssm_a_im": ssm_a_im, "ssm_log_step": ssm_log_step,
            "ssm_b_re": ssm_b_re, "ssm_b_im": ssm_b_im,
            "ssm_c_re": ssm_c_re, "ssm_c_im": ssm_c_im,
            "ssm_d": ssm_d, "ssm_w_out": ssm_w_out,
            "conv_w_in": conv_w_in, "conv_w": conv_w, "conv_w_out": conv_w_out,
            "w_ffn_in": w_ffn_in, "w_ffn_out": w_ffn_out, "final_g": final_g}


def reference(x, c, norm1_g, norm2_g, w_ada, b_ada,
              ssm_a_re, ssm_a_im, ssm_log_step, ssm_b_re, ssm_b_im,
              ssm_c_re, ssm_c_im, ssm_d, ssm_w_out,
              conv_w_in, conv_w, conv_w_out,
              w_ffn_in, w_ffn_out, final_g):
    c_act = jax.nn.silu(c)
    for i in range(DEPTH):
        mods = c_act @ w_ada[i] + b_ada[i]
        sh1, sc1, g1, sh2, sc2, g2 = [m[:, None, :] for m in jnp.split(mods, 6, axis=-1)]
        h = modulate(rms_norm(x, norm1_g[i]), sh1, sc1)
        j = i // N_MIXERS
        if i % N_MIXERS == 0:
            mix = s5_mixer(h, ssm_a_re[j], ssm_a_im[j], ssm_log_step[j],
                           ssm_b_re[j], ssm_b_im[j], ssm_c_re[j], ssm_c_im[j],
                           ssm_d[j], ssm_w_out[j])
        else:
            mix = short_conv_mixer(h, conv_w_in[j], conv_w[j], conv_w_out[j])
        x = x + g1 * mix
        h = modulate(rms_norm(x, norm2_g[i]), sh2, sc2)
        x = x + g2 * swiglu(h, w_ffn_in[i], w_ffn_out[i])
    return rms_norm(x, final_g)
```

```python
import contextlib
import math
import os
DBG = int(os.environ.get('KDBG', '0'))
OPT = int(os.environ.get('KOPT', '2'))
import numpy as np
import concourse.bass as bass
import concourse.mybir as mybir
from concourse.bass_utils import run_bass_kernel_spmd

F32 = mybir.dt.float32
BF16 = mybir.dt.bfloat16
ALU = mybir.AluOpType
AF = mybir.ActivationFunctionType

D = 1024
KC = 8
TT = 1024
SEQ = 4096
FF = 2816
FC = 22
EPS = 1e-6


class Sem:
    def __init__(self, name):
        self.name = name
        self.handle = None
        self.count = 0


class Q:
    def __init__(self, prog, name):
        self.prog = prog
        self.name = name
        self.items = []
        self.sem = prog.sem("q_" + name)
        self.waited = {}
        self.pending = []
        self.last = None

    def wait(self, ev):
        if ev is None:
            return
        s, v = ev
        if self.waited.get(s, 0) >= v:
            return
        self.waited[s] = v
        self.items.append(lambda e, s=s, v=v: e.wait_ge(s.handle, v))


class Prog:
    def __init__(self, nc):
        self.nc = nc
        self.sems = []
        self.q = {}
        for n in ["sync", "scalar", "vector", "gpsimd", "tensor"]:
            self.q[n] = Q(self, n)
        self.sync = self.q["sync"]
        self.act = self.q["scalar"]
        self.dve = self.q["vector"]
        self.pool = self.q["gpsimd"]
        self.pe = self.q["tensor"]
        self.state = {}
        self.dma_events = []

    def sem(self, name):
        s = Sem(name)
        self.sems.append(s)
        return s

    def _st(self, k):
        st = self.state.get(k)
        if st is None:
            st = {"w": None, "r": []}
            self.state[k] = st
        return st

    def _deps(self, q, r, w):
        for k in r:
            q.wait(self._st(k)["w"])
        for k in w:
            st = self._st(k)
            q.wait(st["w"])
            for ev in st["r"]:
                q.wait(ev)

    def _commit(self, ev, r, w):
        for k in r:
            self._st(k)["r"].append(ev)
        for k in w:
            self.state[k] = {"w": ev, "r": []}

    def op(self, q, fn, r=(), w=(), inc=True):
        self._deps(q, r, w)
        if inc:
            s = q.sem
            s.count += 1
            ev = (s, s.count)
            q.items.append(lambda e, fn=fn, s=s: fn(e).then_inc(s.handle, 1))
            for (pr, pw) in q.pending:
                self._commit(ev, pr, pw)
            q.pending = []
            self._commit(ev, r, w)
            q.last = ev
            return ev
        q.items.append(lambda e, fn=fn: fn(e))
        q.pending.append((tuple(r), tuple(w)))
        return None

    def dma(self, q, out, in_, sem, r=(), w=()):
        self._deps(q, r, w)
        sem.count += 16
        ev = (sem, sem.count)
        q.items.append(lambda e, out=out, in_=in_, sem=sem: e.dma_start(out=out, in_=in_).then_inc(sem.handle, 16))
        self._commit(ev, r, w)
        self.dma_events.append(ev)
        return ev

    def dma_multi(self, q, pairs, sem, r=(), w=()):
        self._deps(q, r, w)
        for (out, in_) in pairs:
            sem.count += 16
            q.items.append(lambda e, out=out, in_=in_, sem=sem: e.dma_start(out=out, in_=in_).then_inc(sem.handle, 16))
        ev = (sem, sem.count)
        self._commit(ev, r, w)
        return ev

    def barrier(self, queues=None):
        qs = queues or [self.pe, self.act, self.dve, self.pool]
        evs = [q.last for q in qs if q.last is not None]
        for q in qs:
            for ev in evs:
                if ev[0] is not q.sem:
                    q.wait(ev)

    def emit(self):
        nc = self.nc
        with contextlib.ExitStack() as st:
            for s in self.sems:
                s.handle = st.enter_context(nc.semaphore(s.name))
            block = st.enter_context(nc.Block())
            for n in ["sync", "scalar", "vector", "gpsimd", "tensor"]:
                items = self.q[n].items

                def body(eng, items=items):
                    for it in items:
                        it(eng)
                getattr(block, n)(body)


class Ring:
    def __init__(self, P, name, bufs):
        self.P = P
        self.bufs = bufs
        self.sems = [P.sem(f"{name}{i}") for i in range(len(bufs))]
        self.sems_hw = [P.sem(f"{name}h{i}") for i in range(len(bufs))]
        self.keys = [(name, i) for i in range(len(bufs))]
        self.i = 0

    def next(self, hw=False):
        i = self.i % len(self.bufs)
        self.i += 1
        return self.bufs[i], (self.sems_hw[i] if hw else self.sems[i]), self.keys[i]


def build(ntiles=4, layers=(0, 1, 2, 3), do_final=True):
    nc = bass.Bass("TRN2", target_bir_lowering=False)
    P = Prog(nc)
    pe, act, dve, pool, sync = P.pe, P.act, P.dve, P.pool, P.sync

    def din(name, shape, dt=F32):
        return nc.dram_tensor(name, list(shape), dt, kind="ExternalInput").ap()

    x_d = din("x", [SEQ, D])
    c_d = din("c", [128, 8])
    n1g_d = din("n1g", [128, 4, 8])
    n2g_d = din("n2g", [128, 4, 8])
    fing_d = din("fing", [128, 8])
    wada_d = din("w_ada", [4, D, 6 * D])
    bada_d = din("b_ada", [128, 4, 48])
    lre_d = din("lre", [128, 2, 32])
    lim_d = din("lim", [128, 2, 32])
    lst_d = din("lst", [128, 2, 32])
    bre_d = din("bre", [128, 2, 512])
    bim_d = din("bim", [128, 2, 512])
    crt_d = din("crt", [128, 2, 512])
    cit_d = din("cit", [128, 2, 512])
    dd_d = din("dd", [128, 2, 8])
    swo_d = din("ssm_w_out", [2, D, 2 * D])
    cwi_d = din("conv_w_in", [2, D, 3 * D])
    cw_d = din("cw", [128, 2, 8, 3])
    cwo_d = din("conv_w_out", [2, D, D])
    wfi_d = din("w_ffn_in", [4, D, 2 * FF])
    wfo_d = din("w_ffn_out", [4, FF, D])
    y_d = nc.dram_tensor("y", [SEQ, D], F32, kind="ExternalOutput").ap()
    wbd = nc.dram_tensor("wbd", [2, 8, 128, 2048], BF16, kind="Internal").ap()
    cbd = nc.dram_tensor("cbd", [2, 8, 128, 2048], BF16, kind="Internal").ap()
    wtd = nc.dram_tensor("wtd", [2, 8, 128, 1024], BF16, kind="Internal").ap()

    with contextlib.ExitStack() as st:
        def sb(name, shape, dt=F32):
            return st.enter_context(nc.sbuf_tensor("sb_" + name, list(shape), dt))

        xT = sb("xT", [128, 8, 1024])
        hb = sb("hb", [128, 8, 1024], BF16)
        hprev = sb("hprev", [128, 2, 8, 8], BF16)
        scr = sb("scr", [128, 16448])
        wA = [sb(f"wA{i}", [128, 4096], BF16) for i in range(3)]
        wB = [sb(f"wB{i}", [128, 5632], BF16) for i in range(2)]
        xin = sb("xin", [128, 2, 1024])
        sq = sb("sq", [128, 8, 512], BF16)
        rstd = sb("rstd", [128, 2, 512])
        tmpf = sb("tmpf", [128, 4, 512])
        gel = sb("gel", [128, 2, 1024])
        ident = sb("ident", [128, 128])
        ones_bf = sb("ones_bf", [128, 128], BF16)
        mods = sb("mods", [128, 4, 48])
        gsc = sb("gsc", [128, 4, 2, 8])
        n1g = sb("n1g", [128, 4, 8])
        n2g = sb("n2g", [128, 4, 8])
        fing = sb("fing", [128, 8])
        fing32 = sb("fing32", [128, 8])
        bada = sb("bada", [128, 4, 48])
        cin = sb("cin", [128, 8])
        cact = sb("cact", [128, 8])
        cw = sb("cw", [128, 2, 8, 3])
        dd = sb("dd", [128, 2, 8])
        halo = sb("halo", [128, 2, 8, 8])
        sstate = sb("sstate", [128, 2, 64])
        Ca = sb("Ca", [128, 2, 64])
        Cb = sb("Cb", [128, 2, 64])
        T1 = sb("T1", [128, 64])
        T2 = sb("T2", [128, 64])
        T3 = sb("T3", [128, 64])
        ps = [st.enter_context(nc.psum_tensor(f"ps{i}", [128, 512], F32)) for i in range(8)]
        PSK = [("ps", i) for i in range(8)]

        ringA = Ring(P, "wA", wA)
        ringB = Ring(P, "wB", wB)
        s_small = P.sem("small")
        s_small2 = P.sem("small2")
        s_xin = [P.sem("xin0"), P.sem("xin1")]
        s_out = [P.sem("out0"), P.sem("out1")]
        s_scr = [P.sem("scrd0"), P.sem("scrd1"), P.sem("scrd2")]

        small = [(cin, c_d, "cin"), (n1g, n1g_d, "n1g"), (n2g, n2g_d, "n2g"), (fing, fing_d, "fing"),
                 (bada, bada_d, "bada"), (cw, cw_d, "cw"), (dd, dd_d, "dd")]
        P.dma_multi(sync, [(t[:], d_) for (t, d_, _) in small], s_small, w=[k_ for (_, _, k_) in small])

        P.op(pool, lambda g: g.memset(ident[:], 1.0), w=["ident"])
        P.op(pool, lambda g: g.affine_select(out=ident[:], in_=ident[:], pattern=[[-1, 128]],
                                             compare_op=ALU.is_equal, fill=0.0, base=0, channel_multiplier=1),
             r=["ident"], w=["ident"])
        P.op(pool, lambda g: g.memset(ones_bf[:], 1.0), w=["ones"])
        P.op(pool, lambda g: g.memset(hprev[:], 0.0), w=["hprev"])
        P.op(pool, lambda g: g.memset(halo[:], 0.0), w=["halo"])
        P.op(pool, lambda g: g.memset(sstate[:], 0.0), w=["sstate"])

        P.op(act, lambda a: a.activation(out=cact[:], in_=cin[:], func=AF.Silu), r=["cin"], w=["cact"])
        adab = [xin[:, :, :].rearrange("p a (k n) -> p (a k) n", k=4),
                sq[:, :, :].rearrange("p k n -> p (k n)").bitcast(F32).rearrange("p (k n) -> p k n", k=8)]
        s_ada = [P.sem("ada0"), P.sem("ada1")]

        def ada_gen():
            nb = 0
            for li in range(4):
                wv = wada_d[li].rearrange("(k p) n -> p k n", p=128)
                pb = 4 + li
                for blk in range(24):
                    b = nb % 2
                    nb += 1
                    akeys = [[("xin", 0), ("xin", 1)], ["sq"]][b]
                    P.dma(sync, adab[b], wv[:, :, blk * 256:(blk + 1) * 256], s_ada[b], w=akeys)
                    for jj in range(2):
                        j = blk * 2 + jj
                        for k in range(8):
                            P.op(pe, lambda e, pb=pb, j=j, jj=jj, k=k, b=b: e.matmul(
                                ps[pb][:, j:j + 1], lhsT=adab[b][:, k, jj * 128:(jj + 1) * 128], rhs=cact[:, k:k + 1],
                                start=(k == 0), stop=(k == 7)),
                                r=akeys + ["cact"], w=[PSK[pb]], inc=(k == 7 and jj == 1))
                    yield
                P.op(dve, lambda v, li=li, pb=pb: v.tensor_tensor(out=mods[:, li, :], in0=ps[pb][:, 0:48], in1=bada[:, li, :],
                                                                  op=ALU.add), r=[PSK[pb], "bada"], w=["mods"])
                for which, ng, ngk in ((0, n1g, "n1g"), (1, n2g, "n2g")):
                    sc = mods[:, li, (1 + 3 * which) * 8:(2 + 3 * which) * 8]
                    P.op(dve, lambda v, li=li, which=which, ng=ng, sc=sc: v.scalar_tensor_tensor(
                        out=gsc[:, li, which, :], in0=sc, scalar=1.0, in1=ng[:, li, :], op0=ALU.add, op1=ALU.mult),
                        r=["mods", ngk], w=["gsc"])
                    P.op(dve, lambda v, li=li, which=which: v.tensor_scalar(
                        out=gsc[:, li, which, :], in0=gsc[:, li, which, :], scalar1=32.0, scalar2=None, op0=ALU.mult),
                        r=["gsc"], w=["gsc"])
            P.op(dve, lambda v: v.tensor_scalar(out=fing32[:], in0=fing[:], scalar1=32.0, scalar2=None, op0=ALU.mult),
                 r=["fing"], w=["fing32"])
        ada_it = ada_gen()
        if not (OPT & 1):
            for _ in ada_it:
                pass
            P.barrier()

        ssm_layers = [l for l in layers if l % 2 == 0]
        if ssm_layers and not (DBG & 2):
            _ssm_prologue(nc, P, st, locals())
        for _ in ada_it:
            pass
        P.barrier()

        env = dict(locals())
        for ti in range(ntiles):
            _load_tile(env, ti)
            for li in layers:
                bar = (lambda: None) if (OPT & 2) else (lambda: P.barrier([pe, act, dve]))
                _norm_mod(env, li, 0)
                bar()
                if li % 2 == 0:
                    _ssm_mixer(env, li, ti)
                else:
                    _conv_mixer(env, li, ti)
                bar()
                _norm_mod(env, li, 1)
                bar()
                _ffn(env, li)
                bar()
            _final_store(env, ti, do_final)
            bar()
        for s in s_out:
            if s.count:
                sync.wait((s, s.count))
        P.emit()
    return nc


def _ssm_prologue(nc, P, st, E):
    pe, act, dve, pool, sync = P.pe, P.act, P.dve, P.pool, P.sync
    scr, hb, ps, PSK, ident = E["scr"], E["hb"], E["ps"], E["PSK"], E["ident"]
    Ca, Cb = E["Ca"], E["Cb"]
    wbd, cbd, wtd = E["wbd"], E["cbd"], E["wtd"]
    s_scr = E["s_scr"]
    s_small = E["s_small"]

    def sb(name, shape, dt=F32):
        return st.enter_context(nc.sbuf_tensor("sp_" + name, list(shape), dt))

    class V:
        def __init__(self, name, ap):
            self.name = name
            self.ap = ap

        def __getitem__(self, k):
            return self.ap[k]

    t32c = {}

    def t32(nm):
        if nm not in t32c:
            t32c[nm] = sb("t_" + nm, [128, 32])
        return t32c[nm]

    xTf = E["xT"][:, :, :].rearrange("p k n -> p (k n)")
    gelf = E["gel"][:, :, :].rearrange("p a n -> p (a n)")
    tmpff = E["tmpf"][:, :, :].rearrange("p a n -> p (a n)")
    CBall = scr[:, 0:8192].bitcast(BF16).rearrange("p (a s r c) -> p a s r c", a=32, s=8, r=2)
    WBall = scr[:, 8192:16384].bitcast(BF16).rearrange("p (k s r c) -> p k s r c", k=8, s=8, r=2)
    WTall = hb[:, :, :].rearrange("p k (t c) -> p k t c", t=8)
    lre = sb("lre", [128, 2, 32]); lim = sb("lim", [128, 2, 32]); lst = sb("lst", [128, 2, 32])
    bre = V("bre", xTf[:, 0:1024].rearrange("p (j n) -> p j n", j=2))
    bim = V("bim", xTf[:, 1024:2048].rearrange("p (j n) -> p j n", j=2))
    crt = V("crt", xTf[:, 2048:3072].rearrange("p (j n) -> p j n", j=2))
    cit = V("cit", xTf[:, 3072:4096].rearrange("p (j n) -> p j n", j=2))
    Bpr = V("Bpr", xTf[:, 4096:5120].rearrange("p (a c) -> p a c", a=32))
    Bpi = V("Bpi", xTf[:, 5120:6144].rearrange("p (a c) -> p a c", a=32))
    Cpr = V("Cpr", xTf[:, 6144:7168].rearrange("p (a c) -> p a c", a=32))
    nCpi = V("nCpi", xTf[:, 7168:8192].rearrange("p (a c) -> p a c", a=32))
    Bbr = V("Bbr", gelf[:, 0:512].rearrange("p (a c) -> p a c", a=32))
    Bbi = V("Bbi", gelf[:, 512:1024].rearrange("p (a c) -> p a c", a=32))
    U1 = V("U1", gelf[:, 1024:1536].rearrange("p (a c) -> p a c", a=32))
    U2 = V("U2", gelf[:, 1536:2048].rearrange("p (a c) -> p a c", a=32))
    nCr = V("nCr", tmpff[:, 0:512].rearrange("p (a c) -> p a c", a=32))
    nCi = V("nCi", tmpff[:, 512:1024].rearrange("p (a c) -> p a c", a=32))
    P.dma_multi(sync, [(t[:], d_) for t, d_ in ((lre, E["lre_d"]), (lim, E["lim_d"]), (lst, E["lst_d"]), (bre, E["bre_d"]),
                                                  (bim, E["bim_d"]), (crt, E["crt_d"]), (cit, E["cit_d"]))], E["s_small2"],
                w=["lre", "lim", "lst", "bre", "bim", "crt", "cit"])

    def tt(out, a, b, op, r, w, q=None):
        return P.op(q or dve, lambda v: v.tensor_tensor(out=out, in0=a, in1=b, op=op), r=r, w=w)

    def bc(a):
        return a.unsqueeze(2).to_broadcast([128, 32, 16])

    P.op(pool, lambda g: g.memset(WTall, 0.0), w=["WTall"])
    tctr = [0]
    def do_layer(j):
        P.op(pool, lambda g: g.memset(scr[:, 0:8192], 0.0), w=["CBall"])
        for t in (Bpr, Bpi, Cpr, nCpi):
            P.op(pool, lambda g, t=t: g.memset(t[:], 0.0), w=[t.name])
        dt_ = t32("dt"); lr = t32("lr"); ph = t32("ph"); lrd = t32("lrd")
        s8 = t32("s8"); c8 = t32("c8"); m8 = t32("m8")
        P.op(act, lambda a: a.activation(out=dt_[:], in_=lst[:, j, :], func=AF.Exp), r=["lst"], w=[dt_.name])
        P.op(dve, lambda v: v.tensor_scalar(out=lr[:], in0=lre[:, j, :], scalar1=-1e-4, scalar2=None, op0=ALU.min),
             r=["lre"], w=[lr.name])
        li_ = lim[:, j, :]
        tt(ph[:], li_, dt_[:], ALU.mult, ["lim", dt_.name], [ph.name])
        tt(lrd[:], lr[:], dt_[:], ALU.mult, [lr.name, dt_.name], [lrd.name])
        P.op(act, lambda a: a.activation(out=s8[:], in_=ph[:], func=AF.Sin, scale=0.125), r=[ph.name], w=[s8.name])
        P.op(act, lambda a: a.activation(out=c8[:], in_=ph[:], func=AF.Sin, scale=-0.125, bias=math.pi / 2),
             r=[ph.name], w=[c8.name])
        P.op(act, lambda a: a.activation(out=m8[:], in_=lrd[:], func=AF.Exp, scale=0.125), r=[lrd.name], w=[m8.name])
        ar = t32("ar"); ai = t32("ai")
        tt(ar[:], m8[:], c8[:], ALU.mult, [m8.name, c8.name], [ar.name])
        tt(ai[:], m8[:], s8[:], ALU.mult, [m8.name, s8.name], [ai.name])
        for it in range(3):
            q1 = t32("q1"); q2 = t32("q2"); q3 = t32("q3"); nr = t32(f"nr{it}"); ni = t32(f"ni{it}")
            tt(q1[:], ar[:], ar[:], ALU.mult, [ar.name], [q1.name])
            tt(q2[:], ai[:], ai[:], ALU.mult, [ai.name], [q2.name])
            tt(q3[:], ar[:], ai[:], ALU.mult, [ar.name, ai.name], [q3.name])
            tt(nr[:], q1[:], q2[:], ALU.subtract, [q1.name, q2.name], [nr.name])
            tt(ni[:], q3[:], q3[:], ALU.add, [q3.name], [ni.name])
            ar, ai = nr, ni
        pw = [None] * 9
        one = t32("one"); zero = t32("zero")
        P.op(pool, lambda g: g.memset(one[:], 1.0), w=[one.name])
        P.op(pool, lambda g: g.memset(zero[:], 0.0), w=[zero.name])
        pw[0] = (one, zero)
        pw[1] = (ar, ai)

        def cmul(xr, xi, yr, yi, n):
            a1 = t32("a1"); a2 = t32("a2"); a3 = t32("a3"); a4 = t32("a4"); zr = t32(f"zr{n}"); zi = t32(f"zi{n}")
            tt(a1[:], xr[:], yr[:], ALU.mult, [xr.name, yr.name], [a1.name])
            tt(a2[:], xi[:], yi[:], ALU.mult, [xi.name, yi.name], [a2.name])
            tt(a3[:], xr[:], yi[:], ALU.mult, [xr.name, yi.name], [a3.name])
            tt(a4[:], xi[:], yr[:], ALU.mult, [xi.name, yr.name], [a4.name])
            tt(zr[:], a1[:], a2[:], ALU.subtract, [a1.name, a2.name], [zr.name])
            tt(zi[:], a3[:], a4[:], ALU.add, [a3.name, a4.name], [zi.name])
            return zr, zi
        for n in range(2, 9):
            pw[n] = cmul(pw[n - 1][0], pw[n - 1][1], ar, ai, n)
        a8r, a8i = pw[8]
        P.op(dve, lambda v: v.tensor_copy(out=Ca[:, j, 0:32], in_=a8r[:]), r=[a8r.name], w=["Ca"])
        P.op(dve, lambda v: v.tensor_copy(out=Ca[:, j, 32:64], in_=a8r[:]), r=[a8r.name], w=["Ca"])
        P.op(dve, lambda v: v.tensor_copy(out=Cb[:, j, 32:64], in_=a8i[:]), r=[a8i.name], w=["Cb"])
        P.op(dve, lambda v: v.tensor_scalar(out=Cb[:, j, 0:32], in0=a8i[:], scalar1=-1.0, scalar2=None, op0=ALU.mult),
             r=[a8i.name], w=["Cb"])
        den = t32("den"); d2 = t32("d2"); rden = t32("rden"); am1 = t32("am1")
        tt(den[:], lr[:], lr[:], ALU.mult, [lr.name], [den.name])
        tt(d2[:], li_, li_, ALU.mult, ["lim"], [d2.name])
        tt(den[:], den[:], d2[:], ALU.add, [den.name, d2.name], [den.name])
        P.op(dve, lambda v: v.reciprocal(out=rden[:], in_=den[:]), r=[den.name], w=[rden.name])
        P.op(dve, lambda v: v.tensor_scalar(out=am1[:], in0=ar[:], scalar1=-1.0, scalar2=None, op0=ALU.add),
             r=[ar.name], w=[am1.name])
        e1 = t32("e1"); e2 = t32("e2"); qr = t32("qr"); qi = t32("qi")
        tt(e1[:], am1[:], lr[:], ALU.mult, [am1.name, lr.name], [e1.name])
        tt(e2[:], ai[:], li_, ALU.mult, [ai.name, "lim"], [e2.name])
        tt(e1[:], e1[:], e2[:], ALU.add, [e1.name, e2.name], [e1.name])
        tt(qr[:], e1[:], rden[:], ALU.mult, [e1.name, rden.name], [qr.name])
        e3 = t32("e3"); e4 = t32("e4")
        tt(e3[:], ai[:], lr[:], ALU.mult, [ai.name, lr.name], [e3.name])
        tt(e4[:], am1[:], li_, ALU.mult, [am1.name, "lim"], [e4.name])
        tt(e3[:], e3[:], e4[:], ALU.subtract, [e3.name, e4.name], [e3.name])
        tt(qi[:], e3[:], rden[:], ALU.mult, [e3.name, rden.name], [qi.name])
        B_r = bre[:, j, :].rearrange("p (a h) -> p a h", h=16)
        B_i = bim[:, j, :].rearrange("p (a h) -> p a h", h=16)
        tt(U1[:], B_r, bc(qr[:]), ALU.mult, ["bre", qr.name], ["U1"])
        tt(U2[:], B_i, bc(qi[:]), ALU.mult, ["bim", qi.name], ["U2"])
        tt(Bbr[:], U1[:], U2[:], ALU.subtract, ["U1", "U2"], ["Bbr"])
        tt(U1[:], B_i, bc(qr[:]), ALU.mult, ["bim", qr.name], ["U1"])
        tt(U2[:], B_r, bc(qi[:]), ALU.mult, ["bre", qi.name], ["U2"])
        tt(Bbi[:], U1[:], U2[:], ALU.add, ["U1", "U2"], ["Bbi"])
        C_r = crt[:, j, :].rearrange("p (a h) -> p a h", h=16)
        C_i = cit[:, j, :].rearrange("p (a h) -> p a h", h=16)
        for (lo, hi, c0) in ((0, 64, 0), (64, 128, 16)):
            P.op(dve, lambda v, lo=lo, hi=hi, c0=c0: v.tensor_copy(out=Cpr[lo:hi, :, c0:c0 + 16], in_=C_r[lo:hi]),
                 r=["crt"], w=["Cpr"])
            P.op(dve, lambda v, lo=lo, hi=hi, c0=c0: v.tensor_scalar(
                out=nCpi[lo:hi, :, c0:c0 + 16], in0=C_i[lo:hi], scalar1=-1.0, scalar2=None, op0=ALU.mult),
                r=["cit"], w=["nCpi"])
        P.op(dve, lambda v: v.tensor_scalar(out=nCr[:], in0=C_r, scalar1=-1.0, scalar2=None, op0=ALU.mult),
             r=["crt"], w=["nCr"])
        P.op(dve, lambda v: v.tensor_scalar(out=nCi[:], in0=C_i, scalar1=-1.0, scalar2=None, op0=ALU.mult),
             r=["cit"], w=["nCi"])
        def do_s(s):
            pr, pi_ = pw[7 - s]
            tt(U1[:], Bbr[:], bc(pr[:]), ALU.mult, ["Bbr", pr.name], ["U1"])
            tt(U2[:], Bbi[:], bc(pi_[:]), ALU.mult, ["Bbi", pi_.name], ["U2"])
            for (lo, hi, c0) in ((0, 64, 0), (64, 128, 16)):
                tt(Bpr[lo:hi, :, c0:c0 + 16], U1[lo:hi], U2[lo:hi], ALU.subtract, ["U1", "U2"], ["Bpr"])
            tt(U1[:], Bbi[:], bc(pr[:]), ALU.mult, ["Bbi", pr.name], ["U1"])
            tt(U2[:], Bbr[:], bc(pi_[:]), ALU.mult, ["Bbr", pi_.name], ["U2"])
            for (lo, hi, c0) in ((0, 64, 0), (64, 128, 16)):
                tt(Bpi[lo:hi, :, c0:c0 + 16], U1[lo:hi], U2[lo:hi], ALU.add, ["U1", "U2"], ["Bpi"])
            tau = 7 - s
            for k in range(8 if not (DBG & 4) else 0):
                bk = tctr[0] % 4
                tctr[0] += 1
                for ri, Bp in enumerate((Bpr, Bpi)):
                    src = Bp[:, 4 * k:4 * k + 4, :].rearrange("p a c -> p (a c)")
                    P.op(pe, lambda e, src=src, bk=bk, ri=ri: e.transpose(
                        out=ps[bk][:, ri * 128:(ri + 1) * 128], in_=src, identity=ident[:]),
                        r=[Bp.name, "ident"], w=[PSK[bk]], inc=False)
                srcr = Bpr[:, 4 * k:4 * k + 4, :].rearrange("p a c -> p (a c)")
                srci = Bpi[:, 4 * k:4 * k + 4, :].rearrange("p a c -> p (a c)")
                cr = Cpr[:, 4 * k:4 * k + 4, :].rearrange("p a c -> p (a c)")
                ci = nCpi[:, 4 * k:4 * k + 4, :].rearrange("p a c -> p (a c)")
                P.op(pe, lambda e, bk=bk, srcr=srcr, cr=cr: e.matmul(ps[bk][:, 256:384], lhsT=srcr, rhs=cr,
                                                                     start=True, stop=False),
                     r=["Bpr", "Cpr"], w=[PSK[bk]], inc=False)
                P.op(pe, lambda e, bk=bk, srci=srci, ci=ci: e.matmul(ps[bk][:, 256:384], lhsT=srci, rhs=ci,
                                                                     start=False, stop=True),
                     r=["Bpi", "nCpi"], w=[PSK[bk]], inc=True)
                P.op(act, lambda a, bk=bk, k=k, s=s: a.activation(
                    out=WBall[:, k, s, :, :], in_=ps[bk][:, 0:256].rearrange("p (r c) -> p r c", r=2), func=AF.Copy),
                    r=[PSK[bk]], w=["WBall"])
                for jq in range(4):
                    P.op(dve, lambda v, bk=bk, k=k, tau=tau, jq=jq: v.tensor_copy(
                        out=WTall[32 * jq:32 * jq + 32, k, tau, 32 * jq:32 * jq + 32],
                        in_=ps[bk][32 * jq:32 * jq + 32, 256 + 32 * jq:256 + 32 * jq + 32]),
                        r=[], w=["WTall", PSK[bk]])
            qr_, qi_ = pw[s + 1]
            tt(U1[:], C_r, bc(qr_[:]), ALU.mult, ["crt", qr_.name], ["U1"])
            tt(U2[:], C_i, bc(qi_[:]), ALU.mult, ["cit", qi_.name], ["U2"])
            for (lo, hi, c0) in ((0, 64, 0), (64, 128, 16)):
                tt(CBall[lo:hi, :, s, 0, c0:c0 + 16], U1[lo:hi], U2[lo:hi], ALU.subtract, ["U1", "U2"], ["CBall"])
            tt(U1[:], nCr[:], bc(qi_[:]), ALU.mult, ["nCr", qi_.name], ["U1"])
            tt(U2[:], nCi[:], bc(qr_[:]), ALU.mult, ["nCi", qr_.name], ["U2"])
            for (lo, hi, c0) in ((0, 64, 0), (64, 128, 16)):
                tt(CBall[lo:hi, :, s, 1, c0:c0 + 16], U1[lo:hi], U2[lo:hi], ALU.add, ["U1", "U2"], ["CBall"])
        for s_ in range(8):
            do_s(s_)
            for _ in range(6):
                next(E["ada_it"], None)
        if DBG & 8:
            return
        P.dma(sync, wbd[j].rearrange("k p n -> p k n"), scr[:, 8192:16384].bitcast(BF16).rearrange("p (k n) -> p k n", k=8),
              s_scr[0], r=["WBall"], w=[("wbd", j)])
        P.dma(sync, cbd[j].rearrange("k p n -> p k n"), scr[:, 0:8192].bitcast(BF16).rearrange("p (k n) -> p k n", k=8),
              s_scr[1], r=["CBall"], w=[("cbd", j)])
        P.dma(sync, wtd[j].rearrange("k p n -> p k n"), hb[:, :, :], s_scr[2], r=["WTall"], w=[("wtd", j)])
    for j_ in range(2):
        if 2 * j_ in E["layers"]:
            do_layer(j_)
    for sm in s_scr:
        if sm.count:
            for q in (pe, act, dve, pool, sync):
                q.wait((sm, sm.count))


def _load_tile(E, ti):
    P = E["P"]; pe, act, dve, sync = P.pe, P.act, P.dve, P.sync
    xT, xin, ps, PSK, ident = E["xT"], E["xin"], E["ps"], E["PSK"], E["ident"]
    xTv = xT[:, :, :].rearrange("p k (s c) -> p k s c", s=8)
    for blk in range(8):
        b = blk % 2
        r0 = ti * TT + blk * 128
        P.dma(sync, xin[:, b, :], E["x_d"][r0:r0 + 128, :], E["s_xin"][b], w=[("xin", b)])
        for half in range(2):
            bk = (blk * 2 + half) % 4
            for kk in range(4):
                k = half * 4 + kk
                P.op(pe, lambda e, b=b, bk=bk, k=k, kk=kk: e.transpose(
                    out=ps[bk][:, kk * 128:(kk + 1) * 128], in_=xin[:, b, k * 128:(k + 1) * 128], identity=ident[:]),
                    r=[("xin", b), "ident"], w=[PSK[bk]], inc=(kk == 3))
            q = act if half == 0 else dve
            src = ps[bk][:, :].rearrange("p (k jc s) -> p k s jc", k=4, jc=16, s=8)
            dst = xTv[:, half * 4:half * 4 + 4, :, 16 * blk:16 * blk + 16]
            if q is act:
                P.op(q, lambda a, src=src, dst=dst: a.activation(out=dst, in_=src, func=AF.Copy), r=[PSK[bk]], w=["xT"])
            else:
                P.op(q, lambda v, src=src, dst=dst: v.tensor_copy(out=dst, in_=src), r=[PSK[bk]], w=["xT"])


def _final_store(E, ti, do_final):
    P = E["P"]; pe, act, dve, sync = P.pe, P.act, P.dve, P.sync
    xT, xin, ps, PSK, ident, scr = E["xT"], E["xin"], E["ps"], E["PSK"], E["ident"], E["scr"]
    if do_final:
        _norm_mod(E, None, 2)
        src_all = scr[:, 0:8192].rearrange("p (k s c) -> p k s c", k=8, s=8)
        skey = "fout"
    else:
        src_all = xT[:, :, :].rearrange("p k (s c) -> p k s c", s=8)
        skey = "xT"
    for blk in range(8):
        b = blk % 2
        for half in range(2):
            a = (blk * 2 + half) % 2
            bk = (blk * 2 + half) % 4
            stg = scr[:, 8192 + a * 512: 8192 + (a + 1) * 512]
            src = src_all[:, half * 4:half * 4 + 4, :, 16 * blk:16 * blk + 16]
            dstv = stg.rearrange("p (k jc s) -> p k s jc", k=4, jc=16, s=8)
            if half == 0:
                P.op(act, lambda e, src=src, dstv=dstv: e.activation(out=dstv, in_=src, func=AF.Copy),
                     r=[skey], w=[("stg", a)])
            else:
                P.op(dve, lambda e, src=src, dstv=dstv: e.tensor_copy(out=dstv, in_=src), r=[skey], w=[("stg", a)])
            for kk in range(4):
                P.op(pe, lambda e, stg=stg, bk=bk, kk=kk: e.transpose(
                    out=ps[bk][:, kk * 128:(kk + 1) * 128], in_=stg[:, kk * 128:(kk + 1) * 128], identity=ident[:]),
                    r=[("stg", a), "ident"], w=[PSK[bk]], inc=(kk == 3))
            dsto = xin[:, b, half * 512:(half + 1) * 512]
            if half == 0:
                P.op(dve, lambda e, bk=bk, dsto=dsto: e.tensor_copy(out=dsto, in_=ps[bk][:, :]),
                     r=[PSK[bk]], w=[("xin", b)])
            else:
                P.op(act, lambda e, bk=bk, dsto=dsto: e.activation(out=dsto, in_=ps[bk][:, :], func=AF.Copy),
                     r=[PSK[bk]], w=[("xin", b)])
        r0 = ti * TT + blk * 128
        P.dma(sync, E["y_d"][r0:r0 + 128, :], xin[:, b, :], E["s_out"][b], r=[("xin", b)])


def _norm_mod(E, li, which):
    P = E["P"]; pe, act, dve = P.pe, P.act, P.dve
    xT, hb, sq, rstd, tmpf, ps, PSK = E["xT"], E["hb"], E["sq"], E["rstd"], E["tmpf"], E["ps"], E["PSK"]
    ones_bf, gsc, mods, fing32, scr = E["ones_bf"], E["gsc"], E["mods"], E["fing32"], E["scr"]
    fout = scr[:, 0:8192].rearrange("p (k n) -> p k n", k=8)
    for tb in range(2):
        sl = slice(tb * 512, (tb + 1) * 512)
        bk = 6 + tb
        P.op(act, lambda a, sl=sl: a.activation(out=sq[:], in_=xT[:, :, sl], func=AF.Square), r=["xT"], w=["sq"])
        for k in range(8):
            P.op(pe, lambda e, k=k, bk=bk: e.matmul(ps[bk][:, :], lhsT=ones_bf[:], rhs=sq[:, k, :],
                                                    start=(k == 0), stop=(k == 7)),
                 r=["sq", "ones"], w=[PSK[bk]], inc=(k == 7))
        P.op(act, lambda a, tb=tb, bk=bk: a.activation(out=rstd[:, tb, :], in_=ps[bk][:, :], func=AF.Sqrt,
                                                       bias=D * EPS, scale=1.0),
             r=[PSK[bk]], w=[("rstd", tb)])
        P.op(dve, lambda v, tb=tb: v.reciprocal(out=rstd[:, tb, :], in_=rstd[:, tb, :]),
             r=[("rstd", tb)], w=[("rstd", tb)])
        for k in range(8):
            if which == 2:
                P.op(dve, lambda v, k=k, sl=sl, tb=tb: v.scalar_tensor_tensor(
                    out=fout[:, k, sl], in0=xT[:, k, sl], scalar=fing32[:, k:k + 1], in1=rstd[:, tb, :],
                    op0=ALU.mult, op1=ALU.mult), r=["xT", ("rstd", tb), "fing32"], w=["fout"])
                continue
            tbuf = k % 4
            P.op(dve, lambda v, k=k, sl=sl, tb=tb, tbuf=tbuf: v.scalar_tensor_tensor(
                out=tmpf[:, tbuf, :], in0=xT[:, k, sl], scalar=gsc[:, li, which, k:k + 1], in1=rstd[:, tb, :],
                op0=ALU.mult, op1=ALU.mult), r=["xT", ("rstd", tb), "gsc"], w=[("tmpf", tbuf)])
            shc = (3 * which) * 8 + k
            P.op(act, lambda a, k=k, sl=sl, tbuf=tbuf, shc=shc: a.activation(
                out=hb[:, k, sl], in_=tmpf[:, tbuf, :], func=AF.Identity, bias=mods[:, li, shc:shc + 1], scale=1.0),
                r=[("tmpf", tbuf), "mods"], w=["hb"])


def _ffn(E, li):
    P = E["P"]; pe, act, dve, pool = P.pe, P.act, P.dve, P.pool
    xT, hb, scr, tmpf, ps, PSK, mods = E["xT"], E["hb"], E["scr"], E["tmpf"], E["ps"], E["PSK"], E["mods"]
    ringA, ringB = E["ringA"], E["ringB"]
    actb = scr[:, 0:11264].bitcast(BF16).rearrange("p (j n) -> p j n", j=FC)
    wi = E["wfi_d"][li].rearrange("(k p) n -> p k n", p=128)
    wo = E["wfo_d"][li].rearrange("(j p) n -> p j n", p=128)
    cnt = 0
    for grp in range(11):
        buf, sem, key = ringA.next()
        bv = buf[:, :].rearrange("p (g k n) -> p g k n", g=2, k=8)
        P.dma_multi(pool, [(bv[:, 0], wi[:, :, grp * 256:(grp + 1) * 256]),
                           (bv[:, 1], wi[:, :, FF + grp * 256:FF + (grp + 1) * 256])], sem, w=[key])
        for jj in range(2):
            j = grp * 2 + jj
            for tb in range(2):
                sl = slice(tb * 512, (tb + 1) * 512)
                gb = cnt % 2
                ub = 2 + cnt % 2
                cnt += 1
                for k in range(8):
                    P.op(pe, lambda e, bv=bv, jj=jj, k=k, sl=sl, gb=gb: e.matmul(
                        ps[gb][:, :], lhsT=bv[:, 0, k, jj * 128:(jj + 1) * 128], rhs=hb[:, k, sl],
                        start=(k == 0), stop=(k == 7)), r=[key, "hb"], w=[PSK[gb]], inc=(k == 7))
                for k in range(8):
                    P.op(pe, lambda e, bv=bv, jj=jj, k=k, sl=sl, ub=ub: e.matmul(
                        ps[ub][:, :], lhsT=bv[:, 1, k, jj * 128:(jj + 1) * 128], rhs=hb[:, k, sl],
                        start=(k == 0), stop=(k == 7)), r=[key, "hb"], w=[PSK[ub]], inc=(k == 7))
                tbuf = cnt % 4
                P.op(act, lambda a, gb=gb, tbuf=tbuf: a.activation(out=tmpf[:, tbuf, :], in_=ps[gb][:, :], func=AF.Silu),
                     r=[PSK[gb]], w=[("tmpf", tbuf)])
                P.op(dve, lambda v, ub=ub, tbuf=tbuf, j=j, sl=sl: v.tensor_tensor(
                    out=actb[:, j, sl], in0=ps[ub][:, :], in1=tmpf[:, tbuf, :], op=ALU.mult),
                    r=[PSK[ub], ("tmpf", tbuf)], w=[("actb", j)])
    cnt = 0
    for mg in range(4):
        buf, sem, key = ringB.next()
        bv = buf[:, :].rearrange("p (j n) -> p j n", j=FC)
        P.dma(pool, bv, wo[:, :, mg * 256:(mg + 1) * 256], sem, w=[key])
        for mm in range(2):
            m = mg * 2 + mm
            for tb in range(2):
                sl = slice(tb * 512, (tb + 1) * 512)
                ob = 4 + cnt % 2
                cnt += 1
                for j in range(FC):
                    P.op(pe, lambda e, bv=bv, mm=mm, j=j, sl=sl, ob=ob: e.matmul(
                        ps[ob][:, :], lhsT=bv[:, j, mm * 128:(mm + 1) * 128], rhs=actb[:, j, sl],
                        start=(j == 0), stop=(j == FC - 1)), r=[key, ("actb", j)], w=[PSK[ob]], inc=(j == FC - 1))
                gc = 5 * 8 + m
                P.op(dve, lambda v, ob=ob, m=m, sl=sl, gc=gc: v.scalar_tensor_tensor(
                    out=xT[:, m, sl], in0=ps[ob][:, :], scalar=mods[:, li, gc:gc + 1], in1=xT[:, m, sl],
                    op0=ALU.mult, op1=ALU.add), r=[PSK[ob], "mods", "xT"], w=["xT"])


def _conv_mixer(E, li, ti):
    P = E["P"]; pe, act, dve, pool = P.pe, P.act, P.dve, P.pool
    xT, hb, scr, tmpf, gel, ps, PSK, mods = E["xT"], E["hb"], E["scr"], E["tmpf"], E["gel"], E["ps"], E["PSK"], E["mods"]
    cw, halo = E["cw"], E["halo"]
    ringA, ringB = E["ringA"], E["ringB"]
    jl = li // 2
    mbuf = scr[:, 0:4096].bitcast(BF16).rearrange("p (k n) -> p k n", k=8)
    cvb = [scr[:, 4096 + i * 1032: 4096 + (i + 1) * 1032].rearrange("p (s c) -> p s c", s=8) for i in range(2)]
    accb = [scr[:, 6400 + i * 1024: 6400 + (i + 1) * 1024] for i in range(2)]
    wi = E["cwi_d"][jl].rearrange("(k p) n -> p k n", p=128)
    wo = E["cwo_d"][jl].rearrange("(k p) n -> p k n", p=128)
    for m in range(8):
        buf, sem, key = ringA.next()
        bv = buf[:, 0:3072].rearrange("p (g k n) -> p g k n", g=3, k=8)
        P.dma_multi(pool, [(bv[:, g], wi[:, :, g * D + m * 128: g * D + (m + 1) * 128]) for g in range(3)], sem, w=[key])
        cv = cvb[m % 2]
        acc = accb[m % 2]
        bgb = gel[:, m % 2, :]
        ck, ak, bk_ = ("cvb", m % 2), ("acc", m % 2), ("gel", m % 2)
        for tb in range(2):
            sl = slice(tb * 512, (tb + 1) * 512)
            base = 3 * ((2 * m + tb) % 2)
            for g in range(3):
                for k in range(8):
                    P.op(pe, lambda e, bv=bv, g=g, k=k, sl=sl, base=base: e.matmul(
                        ps[base + g][:, :], lhsT=bv[:, g, k, :], rhs=hb[:, k, sl], start=(k == 0), stop=(k == 7)),
                        r=[key, "hb"], w=[PSK[base + g]], inc=(k == 7))
            tbuf = (2 * m + tb) % 4
            P.op(act, lambda a, base=base, sl=sl, bgb=bgb: a.activation(out=bgb[:, sl], in_=ps[base][:, :], func=AF.Copy),
                 r=[PSK[base]], w=[bk_])
            P.op(act, lambda a, base=base, tbuf=tbuf: a.activation(out=tmpf[:, tbuf, :], in_=ps[base + 1][:, :], func=AF.Copy),
                 r=[PSK[base + 1]], w=[("tmpf", tbuf)])
            P.op(dve, lambda v, base=base, tbuf=tbuf, cv=cv, tb=tb: v.tensor_tensor(
                out=cv[:, 4 * tb:4 * tb + 4, 1:129], in0=ps[base + 2][:, :].rearrange("p (s c) -> p s c", s=4),
                in1=tmpf[:, tbuf, :].rearrange("p (s c) -> p s c", s=4), op=ALU.mult),
                r=[PSK[base + 2], ("tmpf", tbuf)], w=[ck])
        P.op(dve, lambda v, cv=cv, m=m: v.tensor_copy(out=cv[:, :, 0], in_=halo[:, jl, m, :]), r=["halo", ck], w=[ck])
        P.op(dve, lambda v, cv=cv, m=m: v.tensor_copy(out=halo[:, jl, m, :], in_=cv[:, :, 128]), r=[ck, "halo"], w=["halo"])
        a3 = acc.rearrange("p (s c) -> p s c", s=8)
        w0 = cw[:, jl, m, 0:1]; w1 = cw[:, jl, m, 1:2]; w2 = cw[:, jl, m, 2:3]
        P.op(dve, lambda v, a3=a3, cv=cv, w2=w2: v.tensor_scalar(out=a3, in0=cv[:, :, 1:129], scalar1=w2, scalar2=None,
                                                                op0=ALU.mult), r=[ck, "cw"], w=[ak])
        for (o_, i_, wv) in ((a3[:, 1:8, :], cv[:, 0:7, 1:129], w1), (a3[:, 0:1, :], cv[:, 7:8, 0:128], w1),
                             (a3[:, 2:8, :], cv[:, 0:6, 1:129], w0), (a3[:, 0:2, :], cv[:, 6:8, 0:128], w0)):
            P.op(dve, lambda v, o_=o_, i_=i_, wv=wv: v.scalar_tensor_tensor(
                out=o_, in0=i_, scalar=wv, in1=o_, op0=ALU.mult, op1=ALU.add), r=[ck, ak, "cw"], w=[ak])
        P.op(dve, lambda v, acc=acc, bgb=bgb, m=m: v.tensor_tensor(out=mbuf[:, m, :], in0=acc, in1=bgb, op=ALU.mult),
             r=[ak, bk_], w=[("mbuf", m)])
    cnt = 0
    for mg in range(4):
        buf, sem, key = ringB.next()
        bv = buf[:, 0:2048].rearrange("p (k n) -> p k n", k=8)
        P.dma(pool, bv, wo[:, :, mg * 256:(mg + 1) * 256], sem, w=[key])
        for mm in range(2):
            m = mg * 2 + mm
            for tb in range(2):
                sl = slice(tb * 512, (tb + 1) * 512)
                ob = 6 + cnt % 2
                cnt += 1
                for k in range(8):
                    P.op(pe, lambda e, bv=bv, mm=mm, k=k, sl=sl, ob=ob: e.matmul(
                        ps[ob][:, :], lhsT=bv[:, k, mm * 128:(mm + 1) * 128], rhs=mbuf[:, k, sl],
                        start=(k == 0), stop=(k == 7)), r=[key, ("mbuf", k)], w=[PSK[ob]], inc=(k == 7))
                gc = 2 * 8 + m
                P.op(dve, lambda v, ob=ob, m=m, sl=sl, gc=gc: v.scalar_tensor_tensor(
                    out=xT[:, m, sl], in0=ps[ob][:, :], scalar=mods[:, li, gc:gc + 1], in1=xT[:, m, sl],
                    op0=ALU.mult, op1=ALU.add), r=[PSK[ob], "mods", "xT"], w=["xT"])


def _ssm_mixer(E, li, ti):
    P = E["P"]; pe, act, dve, pool, sync = P.pe, P.act, P.dve, P.pool, P.sync
    xT, hb, scr, tmpf, gel, ps, PSK, mods = E["xT"], E["hb"], E["scr"], E["tmpf"], E["gel"], E["ps"], E["PSK"], E["mods"]
    hprev, sstate, Ca, Cb, dd = E["hprev"], E["sstate"], E["Ca"], E["Cb"], E["dd"]
    T1, T2, T3 = E["T1"], E["T2"], E["T3"]
    ringA = E["ringA"]
    jl = li // 2
    XS = scr[:, 0:8256].rearrange("p (r c) -> p r c", r=64)
    XSb = scr[:, 8256:12352].bitcast(BF16).rearrange("p (r c) -> p r c", r=64)
    ybuf = scr[:, 12352:16448].bitcast(BF16).rearrange("p (k n) -> p k n", k=8)
    hv = hb[:, :, :].rearrange("p k (s c) -> p k s c", s=8)
    P.op(dve, lambda v: v.tensor_copy(out=XS[:, :, 0], in_=sstate[:, jl, :]), r=["sstate"], w=["XS"])
    for k in range(8):
        buf, sem, key = ringA.next(hw=True)
        bv = buf[:, 0:2048].rearrange("p (s r c) -> p s r c", s=8, r=2)
        P.dma(sync, buf[:, 0:2048], E["wbd"][jl, k], sem, r=[("wbd", jl)], w=[key])
        for ri in range(2):
            for s in range(8):
                for jq in range(4):
                    p0 = 32 * jq
                    P.op(pe, lambda e, bv=bv, p0=p0, ri=ri, s=s, k=k, jq=jq: e.matmul(
                        ps[ri * 4 + jq][:, 0:128], lhsT=bv[p0:p0 + 32, s, ri, :],
                        rhs=hv[p0:p0 + 32, k, s, :], start=(s == 0), stop=(s == 7), tile_position=(p0, 0)),
                        r=[key, "hb"], w=[PSK[ri * 4 + jq]], inc=(s == 7 and jq == 3))
            for jq in range(4):
                pi_ = 4 * k + jq
                dst = XS[:, ri * 32 + pi_, 1:129]
                src = ps[ri * 4 + jq][:, 0:128]
                if jq % 2 == 0:
                    P.op(act, lambda a, dst=dst, src=src: a.activation(out=dst, in_=src, func=AF.Copy),
                         r=[PSK[ri * 4 + jq]], w=["XS"])
                else:
                    P.op(dve, lambda v, dst=dst, src=src: v.tensor_copy(out=dst, in_=src), r=[PSK[ri * 4 + jq]], w=["XS"])
    def mm(out, lhsT, rhs, rk, start=False, bank=0, inc=False, tp=None):
        if tp is None:
            P.op(pe, lambda e: e.matmul(out, lhsT=lhsT, rhs=rhs, start=start, stop=inc, skip_group_check=True),
                 r=rk, w=[PSK[bank]], inc=inc)
        else:
            P.op(pe, lambda e: e.matmul(out, lhsT=lhsT, rhs=rhs, start=start, stop=inc, skip_group_check=True,
                                        tile_position=tp),
                 r=rk, w=[PSK[bank]], inc=inc)

    def near(k, wtv, yb, rk):
        for hf in range(2):
            mm(ps[yb[hf]][:, :], wtv[:, 0, :], hb[:, k, hf * 512:(hf + 1) * 512], rk, start=True, bank=yb[hf])
        for tau in range(1, 8):
            lo = tau * 128
            if lo < 512:
                mm(ps[yb[0]][:, lo:512], wtv[:, tau, :], hb[:, k, 0:512 - lo], rk, bank=yb[0])
            lo2 = max(512, lo)
            mm(ps[yb[1]][:, lo2 - 512:512], wtv[:, tau, :], hb[:, k, lo2 - lo:1024 - lo], rk, bank=yb[1])
            for s in range(tau):
                hf, sc = s // 4, s % 4
                src_s = s + 8 - tau
                mm(ps[yb[hf]][:, sc * 128 + 1:sc * 128 + 128], wtv[:, tau, :],
                   hb[:, k, src_s * 128:src_s * 128 + 127], rk, bank=yb[hf])
                mm(ps[yb[hf]][:, sc * 128:sc * 128 + 1], wtv[:, tau, :], hprev[:, jl, k, src_s:src_s + 1], rk, bank=yb[hf])

    def far(k, cbv, yb, rk):
        for s in range(8):
            hf, sc = s // 4, s % 4
            for ri in range(2):
                for jq in range(4):
                    pi_ = 4 * k + jq
                    last = (jq == 3 and s in (3, 7) and ri == 1)
                    mm(ps[yb[hf]][32 * jq:32 * jq + 32, sc * 128:(sc + 1) * 128], cbv[:, jq, s, ri, :],
                       XSb[:, ri * 32 + pi_, :], rk, bank=yb[hf], inc=last, tp=(0, 32 * jq))

    def evac(k, yb):
        for hf in range(2):
            sl = slice(hf * 512, (hf + 1) * 512)
            g_ = gel[:, hf, 0:512]
            P.op(dve, lambda v, hf=hf, sl=sl, k=k, g_=g_, yb=yb: v.scalar_tensor_tensor(
                out=g_, in0=hb[:, k, sl], scalar=dd[:, jl, k:k + 1], in1=ps[yb[hf]][:, :],
                op0=ALU.mult, op1=ALU.add), r=["hb", "dd", PSK[yb[hf]]], w=[("gel", hf)])
            P.op(act, lambda a, sl=sl, k=k, g_=g_: a.activation(out=ybuf[:, k, sl], in_=g_, func=AF.Gelu_apprx_tanh),
                 r=[("gel", hf)], w=[("ybuf", k)])

    bufW, semW, keyW = ringA.next(hw=True)
    P.dma(sync, bufW[:, :].rearrange("p (k n) -> p k n", k=4), E["wtd"][jl, 0:4].rearrange("k p n -> p k n"), semW,
          r=[("wtd", jl)], w=[keyW])
    for k in range(4):
        wtv = bufW[:, k * 1024:(k + 1) * 1024].rearrange("p (t c) -> p t c", t=8)
        near(k, wtv, [2 * k, 2 * k + 1], [keyW, "hb", "hprev"])
    xs_t = XS.tensor
    pstep = XS.ap[0][0]
    for c in range(128 if not (DBG & 1) else 0):
        cur = XS[:, :, c]
        nxt = XS[:, :, c + 1]
        swp = bass.AP(xs_t, XS.offset + 32 * 129 + c, [[pstep, 128], [-32 * 129, 2], [129, 32]])
        P.op(dve, lambda v, cur=cur: v.tensor_tensor(out=T1[:], in0=cur, in1=Ca[:, jl, :], op=ALU.mult),
             r=["XS", "Ca"], w=["T1"])
        P.op(dve, lambda v, swp=swp: v.tensor_tensor(out=T2[:].rearrange("p (r c) -> p r c", r=2), in0=swp,
                                                     in1=Cb[:, jl, :].rearrange("p (r c) -> p r c", r=2), op=ALU.mult),
             r=["XS", "Cb"], w=["T2"])
        P.op(dve, lambda v: v.tensor_tensor(out=T3[:], in0=T1[:], in1=T2[:], op=ALU.add), r=["T1", "T2"], w=["T3"])
        P.op(dve, lambda v, nxt=nxt: v.tensor_tensor(out=nxt, in0=T3[:], in1=nxt, op=ALU.add), r=["T3", "XS"], w=["XS"])
    P.op(dve, lambda v: v.tensor_copy(out=sstate[:, jl, :], in_=XS[:, :, 128]), r=["XS"], w=["sstate"])
    for q4 in range(4):
        P.op(act, lambda a, q4=q4: a.activation(out=XSb[:, 16 * q4:16 * q4 + 16, :], in_=XS[:, 16 * q4:16 * q4 + 16, 0:128],
                                                 func=AF.Copy), r=["XS"], w=["XSb"])
    for k in range(8):
        buf, sem, key = ringA.next(hw=True)
        cbv = buf[:, 0:2048].rearrange("p (a s r c) -> p a s r c", a=4, s=8, r=2)
        yb = [2 * (k % 4), 2 * (k % 4) + 1]
        if k < 4:
            P.dma(sync, buf[:, 0:2048], E["cbd"][jl, k], sem, r=[("cbd", jl)], w=[key])
        else:
            wtv = buf[:, 2048:3072].rearrange("p (t c) -> p t c", t=8)
            P.dma_multi(sync, [(buf[:, 0:2048], E["cbd"][jl, k]), (buf[:, 2048:3072], E["wtd"][jl, k])], sem,
                        r=[("cbd", jl), ("wtd", jl)], w=[key])
            near(k, wtv, yb, [key, "hb", "hprev"])
        far(k, cbv, yb, [key, "XSb"])
        evac(k, yb)
    P.op(dve, lambda v: v.tensor_copy(out=hprev[:, jl, :, :], in_=hv[:, :, :, 127]), r=["hb", "hprev"], w=["hprev"])
    wo = E["swo_d"][jl].rearrange("(k p) n -> p k n", p=128)
    cnt = 0
    for mg in range(4):
        buf, sem, key = ringA.next()
        bv = buf[:, :].rearrange("p (g k n) -> p g k n", g=2, k=8)
        P.dma_multi(pool, [(bv[:, 0], wo[:, :, mg * 256:(mg + 1) * 256]),
                           (bv[:, 1], wo[:, :, D + mg * 256:D + (mg + 1) * 256])], sem, w=[key])
        for mm_ in range(2):
            m = mg * 2 + mm_
            for tb in range(2):
                sl = slice(tb * 512, (tb + 1) * 512)
                vb = 4 + cnt % 2
                gb = 6 + cnt % 2
                cnt += 1
                for g, bnk in ((0, vb), (1, gb)):
                    for k in range(8):
                        P.op(pe, lambda e, bv=bv, g=g, mm_=mm_, k=k, sl=sl, bnk=bnk: e.matmul(
                            ps[bnk][:, :], lhsT=bv[:, g, k, mm_ * 128:(mm_ + 1) * 128], rhs=ybuf[:, k, sl],
                            start=(k == 0), stop=(k == 7)), r=[key, ("ybuf", k)], w=[PSK[bnk]], inc=(k == 7))
                tbuf = cnt % 4
                P.op(act, lambda a, gb=gb, tbuf=tbuf: a.activation(out=tmpf[:, tbuf, :], in_=ps[gb][:, :], func=AF.Sigmoid),
                     r=[PSK[gb]], w=[("tmpf", tbuf)])
                P.op(dve, lambda v, vb=vb, tbuf=tbuf: v.tensor_tensor(out=tmpf[:, tbuf, :], in0=ps[vb][:, :],
                                                                      in1=tmpf[:, tbuf, :], op=ALU.mult),
                     r=[PSK[vb], ("tmpf", tbuf)], w=[("tmpf", tbuf)])
                gc = 2 * 8 + m
                P.op(dve, lambda v, tbuf=tbuf, m=m, sl=sl, gc=gc: v.scalar_tensor_tensor(
                    out=xT[:, m, sl], in0=tmpf[:, tbuf, :], scalar=mods[:, li, gc:gc + 1], in1=xT[:, m, sl],
                    op0=ALU.mult, op1=ALU.add), r=[("tmpf", tbuf), "mods", "xT"], w=["xT"])


def _prep_inputs(inp, b):
    f = np.float32
    g = lambda a: np.ascontiguousarray(np.asarray(a, dtype=f))

    def pk(v):
        v = np.asarray(v, dtype=f)
        lead = v.shape[:-1]
        return g(np.moveaxis(v.reshape(lead + (8, 128)), -1, 0))

    def gp(a):
        a = np.asarray(a, dtype=f).reshape(2, 32, 2, 64)
        return g(a.transpose(2, 3, 0, 1).reshape(128, 2, 32))
    m = {}
    m["x"] = g(inp["x"][b])
    m["c"] = pk(inp["c"][b])
    m["n1g"] = pk(inp["norm1_g"])
    m["n2g"] = pk(inp["norm2_g"])
    m["fing"] = pk(inp["final_g"])
    m["w_ada"] = g(inp["w_ada"])
    m["b_ada"] = g(np.asarray(inp["b_ada"], dtype=f).reshape(4, 48, 128).transpose(2, 0, 1))
    m["lre"] = gp(inp["ssm_a_re"])
    m["lim"] = gp(inp["ssm_a_im"])
    ls = np.broadcast_to(np.asarray(inp["ssm_log_step"], dtype=f)[:, :, None], (2, 64, 64))
    m["lst"] = gp(ls)
    for nm, src in (("bre", "ssm_b_re"), ("bim", "ssm_b_im")):
        a = np.asarray(inp[src], dtype=f).reshape(2, 32, 2, 64, 16)
        m[nm] = g(a.transpose(2, 3, 0, 1, 4).reshape(128, 2, 512))
    for nm, src in (("crt", "ssm_c_re"), ("cit", "ssm_c_im")):
        a = np.asarray(inp[src], dtype=f).reshape(2, 32, 2, 16, 64)
        m[nm] = g(a.transpose(2, 4, 0, 1, 3).reshape(128, 2, 512))
    m["dd"] = pk(inp["ssm_d"])
    m["ssm_w_out"] = g(inp["ssm_w_out"])
    m["conv_w_in"] = g(inp["conv_w_in"])
    cwv = np.asarray(inp["conv_w"], dtype=f).reshape(2, 3, 8, 128)
    m["cw"] = g(cwv.transpose(3, 0, 2, 1))
    m["conv_w_out"] = g(inp["conv_w_out"])
    m["w_ffn_in"] = g(inp["w_ffn_in"])
    m["w_ffn_out"] = g(inp["w_ffn_out"])
    return m


_NC_CACHE = {}


def kernel(**inputs):
    if "full" not in _NC_CACHE:
        _NC_CACHE["full"] = build()
    nc = _NC_CACHE["full"]
    n = 8
    in_maps = [_prep_inputs(inputs, b) for b in range(n)]
    res = run_bass_kernel_spmd(nc, in_maps, core_ids=list(range(n)))
    out = np.stack([np.asarray(r["y"], dtype=np.float32) for r in res.results], axis=0)
    return out
```

```python
import contextlib
import math
import os
DBG = int(os.environ.get('KDBG', '0'))
OPT = int(os.environ.get('KOPT', '0'))
import numpy as np
import concourse.bass as bass
import concourse.mybir as mybir
from concourse.bass_utils import run_bass_kernel_spmd

F32 = mybir.dt.float32
BF16 = mybir.dt.bfloat16
ALU = mybir.AluOpType
AF = mybir.ActivationFunctionType

D = 1024
KC = 8
TT = 1024
SEQ = 4096
FF = 2816
FC = 22
EPS = 1e-6


class Sem:
    def __init__(self, name):
        self.name = name
        self.handle = None
        self.count = 0


class Q:
    def __init__(self, prog, name):
        self.prog = prog
        self.name = name
        self.items = []
        self.sem = prog.sem("q_" + name)
        self.waited = {}
        self.pending = []
        self.last = None

    def wait(self, ev):
        if ev is None:
            return
        s, v = ev
        if self.waited.get(s, 0) >= v:
            return
        self.waited[s] = v
        self.items.append(lambda e, s=s, v=v: e.wait_ge(s.handle, v))


class Prog:
    def __init__(self, nc):
        self.nc = nc
        self.sems = []
        self.q = {}
        for n in ["sync", "scalar", "vector", "gpsimd", "tensor"]:
            self.q[n] = Q(self, n)
        self.sync = self.q["sync"]
        self.act = self.q["scalar"]
        self.dve = self.q["vector"]
        self.pool = self.q["gpsimd"]
        self.pe = self.q["tensor"]
        self.state = {}
        self.dma_events = []

    def sem(self, name):
        s = Sem(name)
        self.sems.append(s)
        return s

    def _st(self, k):
        st = self.state.get(k)
        if st is None:
            st = {"w": None, "r": []}
            self.state[k] = st
        return st

    def _deps(self, q, r, w):
        for k in r:
            q.wait(self._st(k)["w"])
        for k in w:
            st = self._st(k)
            q.wait(st["w"])
            for ev in st["r"]:
                q.wait(ev)

    def _commit(self, ev, r, w):
        for k in r:
            self._st(k)["r"].append(ev)
        for k in w:
            self.state[k] = {"w": ev, "r": []}

    def op(self, q, fn, r=(), w=(), inc=True):
        self._deps(q, r, w)
        if inc:
            s = q.sem
            s.count += 1
            ev = (s, s.count)
            q.items.append(lambda e, fn=fn, s=s: fn(e).then_inc(s.handle, 1))
            for (pr, pw) in q.pending:
                self._commit(ev, pr, pw)
            q.pending = []
            self._commit(ev, r, w)
            q.last = ev
            return ev
        q.items.append(lambda e, fn=fn: fn(e))
        q.pending.append((tuple(r), tuple(w)))
        return None

    def dma(self, q, out, in_, sem, r=(), w=()):
        self._deps(q, r, w)
        sem.count += 16
        ev = (sem, sem.count)
        q.items.append(lambda e, out=out, in_=in_, sem=sem: e.dma_start(out=out, in_=in_).then_inc(sem.handle, 16))
        self._commit(ev, r, w)
        self.dma_events.append(ev)
        return ev

    def dma_multi(self, q, pairs, sem, r=(), w=()):
        self._deps(q, r, w)
        for (out, in_) in pairs:
            sem.count += 16
            q.items.append(lambda e, out=out, in_=in_, sem=sem: e.dma_start(out=out, in_=in_).then_inc(sem.handle, 16))
        ev = (sem, sem.count)
        self._commit(ev, r, w)
        return ev

    def barrier(self, queues=None):
        qs = queues or [self.pe, self.act, self.dve, self.pool]
        evs = [q.last for q in qs if q.last is not None]
        for q in qs:
            for ev in evs:
                if ev[0] is not q.sem:
                    q.wait(ev)

    def emit(self):
        nc = self.nc
        with contextlib.ExitStack() as st:
            for s in self.sems:
                s.handle = st.enter_context(nc.semaphore(s.name))
            block = st.enter_context(nc.Block())
            for n in ["sync", "scalar", "vector", "gpsimd", "tensor"]:
                items = self.q[n].items

                def body(eng, items=items):
                    for it in items:
                        it(eng)
                getattr(block, n)(body)


class Ring:
    def __init__(self, P, name, bufs):
        self.P = P
        self.bufs = bufs
        self.sems = [P.sem(f"{name}{i}") for i in range(len(bufs))]
        self.sems_hw = [P.sem(f"{name}h{i}") for i in range(len(bufs))]
        self.keys = [(name, i) for i in range(len(bufs))]
        self.i = 0

    def next(self, hw=False):
        i = self.i % len(self.bufs)
        self.i += 1
        return self.bufs[i], (self.sems_hw[i] if hw else self.sems[i]), self.keys[i]


def build(ntiles=4, layers=(0, 1, 2, 3), do_final=True):
    nc = bass.Bass("TRN2", target_bir_lowering=False)
    P = Prog(nc)
    pe, act, dve, pool, sync = P.pe, P.act, P.dve, P.pool, P.sync

    def din(name, shape, dt=F32):
        return nc.dram_tensor(name, list(shape), dt, kind="ExternalInput").ap()

    x_d = din("x", [SEQ, D])
    c_d = din("c", [128, 8])
    n1g_d = din("n1g", [128, 4, 8])
    n2g_d = din("n2g", [128, 4, 8])
    fing_d = din("fing", [128, 8])
    wada_d = din("w_ada", [4, D, 6 * D])
    bada_d = din("b_ada", [128, 4, 48])
    lre_d = din("lre", [128, 2, 32])
    lim_d = din("lim", [128, 2, 32])
    lst_d = din("lst", [128, 2, 32])
    bre_d = din("bre", [128, 2, 512])
    bim_d = din("bim", [128, 2, 512])
    crt_d = din("crt", [128, 2, 512])
    cit_d = din("cit", [128, 2, 512])
    dd_d = din("dd", [128, 2, 8])
    swo_d = din("ssm_w_out", [2, D, 2 * D])
    cwi_d = din("conv_w_in", [2, D, 3 * D])
    cw_d = din("cw", [128, 2, 8, 3])
    cwo_d = din("conv_w_out", [2, D, D])
    wfi_d = din("w_ffn_in", [4, D, 2 * FF])
    wfo_d = din("w_ffn_out", [4, FF, D])
    y_d = nc.dram_tensor("y", [SEQ, D], F32, kind="ExternalOutput").ap()
    wbd = nc.dram_tensor("wbd", [2, 8, 128, 2048], BF16, kind="Internal").ap()
    cbd = nc.dram_tensor("cbd", [2, 8, 128, 2048], BF16, kind="Internal").ap()
    wtd = nc.dram_tensor("wtd", [2, 8, 128, 1024], BF16, kind="Internal").ap()

    with contextlib.ExitStack() as st:
        def sb(name, shape, dt=F32):
            return st.enter_context(nc.sbuf_tensor("sb_" + name, list(shape), dt))

        xT = sb("xT", [128, 8, 1024])
        hb = sb("hb", [128, 8, 1024], BF16)
        hprev = sb("hprev", [128, 2, 8, 8], BF16)
        scr = sb("scr", [128, 16448])
        wA = [sb(f"wA{i}", [128, 4096], BF16) for i in range(3)]
        wB = [sb(f"wB{i}", [128, 5632], BF16) for i in range(2)]
        xin = sb("xin", [128, 2, 1024])
        sq = sb("sq", [128, 8, 512], BF16)
        rstd = sb("rstd", [128, 2, 512])
        tmpf = sb("tmpf", [128, 4, 512])
        gel = sb("gel", [128, 2, 1024])
        ident = sb("ident", [128, 128])
        ones_bf = sb("ones_bf", [128, 128], BF16)
        mods = sb("mods", [128, 4, 48])
        gsc = sb("gsc", [128, 4, 2, 8])
        n1g = sb("n1g", [128, 4, 8])
        n2g = sb("n2g", [128, 4, 8])
        fing = sb("fing", [128, 8])
        fing32 = sb("fing32", [128, 8])
        bada = sb("bada", [128, 4, 48])
        cin = sb("cin", [128, 8])
        cact = sb("cact", [128, 8])
        cw = sb("cw", [128, 2, 8, 3])
        dd = sb("dd", [128, 2, 8])
        halo = sb("halo", [128, 2, 8, 8])
        sstate = sb("sstate", [128, 2, 64])
        Ca = sb("Ca", [128, 2, 64])
        Cb = sb("Cb", [128, 2, 64])
        T1 = sb("T1", [128, 64])
        T2 = sb("T2", [128, 64])
        T3 = sb("T3", [128, 64])
        ps = [st.enter_context(nc.psum_tensor(f"ps{i}", [128, 512], F32)) for i in range(8)]
        PSK = [("ps", i) for i in range(8)]

        ringA = Ring(P, "wA", wA)
        ringB = Ring(P, "wB", wB)
        s_small = P.sem("small")
        s_small2 = P.sem("small2")
        s_xin = [P.sem("xin0"), P.sem("xin1")]
        s_out = [P.sem("out0"), P.sem("out1")]
        s_scr = [P.sem("scrd0"), P.sem("scrd1"), P.sem("scrd2")]

        small = [(cin, c_d, "cin"), (n1g, n1g_d, "n1g"), (n2g, n2g_d, "n2g"), (fing, fing_d, "fing"),
                 (bada, bada_d, "bada"), (cw, cw_d, "cw"), (dd, dd_d, "dd")]
        P.dma_multi(sync, [(t[:], d_) for (t, d_, _) in small], s_small, w=[k_ for (_, _, k_) in small])

        P.op(pool, lambda g: g.memset(ident[:], 1.0), w=["ident"])
        P.op(pool, lambda g: g.affine_select(out=ident[:], in_=ident[:], pattern=[[-1, 128]],
                                             compare_op=ALU.is_equal, fill=0.0, base=0, channel_multiplier=1),
             r=["ident"], w=["ident"])
        P.op(pool, lambda g: g.memset(ones_bf[:], 1.0), w=["ones"])
        P.op(pool, lambda g: g.memset(hprev[:], 0.0), w=["hprev"])
        P.op(pool, lambda g: g.memset(halo[:], 0.0), w=["halo"])
        P.op(pool, lambda g: g.memset(sstate[:], 0.0), w=["sstate"])

        P.op(act, lambda a: a.activation(out=cact[:], in_=cin[:], func=AF.Silu), r=["cin"], w=["cact"])
        adab = [xin[:, :, :].rearrange("p a (k n) -> p (a k) n", k=4),
                sq[:, :, :].rearrange("p k n -> p (k n)").bitcast(F32).rearrange("p (k n) -> p k n", k=8)]
        s_ada = [P.sem("ada0"), P.sem("ada1")]

        def ada_gen():
            nb = 0
            for li in range(4):
                wv = wada_d[li].rearrange("(k p) n -> p k n", p=128)
                pb = 4 + li
                for blk in range(24):
                    b = nb % 2
                    nb += 1
                    akeys = [[("xin", 0), ("xin", 1)], ["sq"]][b]
                    P.dma(sync, adab[b], wv[:, :, blk * 256:(blk + 1) * 256], s_ada[b], w=akeys)
                    for jj in range(2):
                        j = blk * 2 + jj
                        for k in range(8):
                            P.op(pe, lambda e, pb=pb, j=j, jj=jj, k=k, b=b: e.matmul(
                                ps[pb][:, j:j + 1], lhsT=adab[b][:, k, jj * 128:(jj + 1) * 128], rhs=cact[:, k:k + 1],
                                start=(k == 0), stop=(k == 7)),
                                r=akeys + ["cact"], w=[PSK[pb]], inc=(k == 7 and jj == 1))
                    yield
                P.op(dve, lambda v, li=li, pb=pb: v.tensor_tensor(out=mods[:, li, :], in0=ps[pb][:, 0:48], in1=bada[:, li, :],
                                                                  op=ALU.add), r=[PSK[pb], "bada"], w=["mods"])
                for which, ng, ngk in ((0, n1g, "n1g"), (1, n2g, "n2g")):
                    sc = mods[:, li, (1 + 3 * which) * 8:(2 + 3 * which) * 8]
                    P.op(dve, lambda v, li=li, which=which, ng=ng, sc=sc: v.scalar_tensor_tensor(
                        out=gsc[:, li, which, :], in0=sc, scalar=1.0, in1=ng[:, li, :], op0=ALU.add, op1=ALU.mult),
                        r=["mods", ngk], w=["gsc"])
                    P.op(dve, lambda v, li=li, which=which: v.tensor_scalar(
                        out=gsc[:, li, which, :], in0=gsc[:, li, which, :], scalar1=32.0, scalar2=None, op0=ALU.mult),
                        r=["gsc"], w=["gsc"])
            P.op(dve, lambda v: v.tensor_scalar(out=fing32[:], in0=fing[:], scalar1=32.0, scalar2=None, op0=ALU.mult),
                 r=["fing"], w=["fing32"])
        ada_it = ada_gen()
        if not (OPT & 1):
            for _ in ada_it:
                pass
            P.barrier()

        ssm_layers = [l for l in layers if l % 2 == 0]
        if ssm_layers and not (DBG & 2):
            _ssm_prologue(nc, P, st, locals())
        for _ in ada_it:
            pass
        P.barrier()

        env = dict(locals())
        for ti in range(ntiles):
            _load_tile(env, ti)
            for li in layers:
                bar = (lambda: None) if (OPT & 2) else (lambda: P.barrier([pe, act, dve]))
                _norm_mod(env, li, 0)
                bar()
                if li % 2 == 0:
                    _ssm_mixer(env, li, ti)
                else:
                    _conv_mixer(env, li, ti)
                bar()
                _norm_mod(env, li, 1)
                bar()
                _ffn(env, li)
                bar()
            _final_store(env, ti, do_final)
            bar()
        for s in s_out:
            if s.count:
                sync.wait((s, s.count))
        P.emit()
    return nc


def _ssm_prologue(nc, P, st, E):
    pe, act, dve, pool, sync = P.pe, P.act, P.dve, P.pool, P.sync
    scr, hb, ps, PSK, ident = E["scr"], E["hb"], E["ps"], E["PSK"], E["ident"]
    Ca, Cb = E["Ca"], E["Cb"]
    wbd, cbd, wtd = E["wbd"], E["cbd"], E["wtd"]
    s_scr = E["s_scr"]
    s_small = E["s_small"]

    def sb(name, shape, dt=F32):
        return st.enter_context(nc.sbuf_tensor("sp_" + name, list(shape), dt))

    class V:
        def __init__(self, name, ap):
            self.name = name
            self.ap = ap

        def __getitem__(self, k):
            return self.ap[k]

    t32c = {}

    def t32(nm):
        if nm not in t32c:
            t32c[nm] = sb("t_" + nm, [128, 32])
        return t32c[nm]

    xTf = E["xT"][:, :, :].rearrange("p k n -> p (k n)")
    gelf = E["gel"][:, :, :].rearrange("p a n -> p (a n)")
    tmpff = E["tmpf"][:, :, :].rearrange("p a n -> p (a n)")
    CBall = scr[:, 0:8192].bitcast(BF16).rearrange("p (a s r c) -> p a s r c", a=32, s=8, r=2)
    WBall = scr[:, 8192:16384].bitcast(BF16).rearrange("p (k s r c) -> p k s r c", k=8, s=8, r=2)
    WTall = hb[:, :, :].rearrange("p k (t c) -> p k t c", t=8)
    lre = sb("lre", [128, 2, 32]); lim = sb("lim", [128, 2, 32]); lst = sb("lst", [128, 2, 32])
    bre = V("bre", xTf[:, 0:1024].rearrange("p (j n) -> p j n", j=2))
    bim = V("bim", xTf[:, 1024:2048].rearrange("p (j n) -> p j n", j=2))
    crt = V("crt", xTf[:, 2048:3072].rearrange("p (j n) -> p j n", j=2))
    cit = V("cit", xTf[:, 3072:4096].rearrange("p (j n) -> p j n", j=2))
    Bpr = V("Bpr", xTf[:, 4096:5120].rearrange("p (a c) -> p a c", a=32))
    Bpi = V("Bpi", xTf[:, 5120:6144].rearrange("p (a c) -> p a c", a=32))
    Cpr = V("Cpr", xTf[:, 6144:7168].rearrange("p (a c) -> p a c", a=32))
    nCpi = V("nCpi", xTf[:, 7168:8192].rearrange("p (a c) -> p a c", a=32))
    Bbr = V("Bbr", gelf[:, 0:512].rearrange("p (a c) -> p a c", a=32))
    Bbi = V("Bbi", gelf[:, 512:1024].rearrange("p (a c) -> p a c", a=32))
    U1 = V("U1", gelf[:, 1024:1536].rearrange("p (a c) -> p a c", a=32))
    U2 = V("U2", gelf[:, 1536:2048].rearrange("p (a c) -> p a c", a=32))
    nCr = V("nCr", tmpff[:, 0:512].rearrange("p (a c) -> p a c", a=32))
    nCi = V("nCi", tmpff[:, 512:1024].rearrange("p (a c) -> p a c", a=32))
    P.dma_multi(sync, [(t[:], d_) for t, d_ in ((lre, E["lre_d"]), (lim, E["lim_d"]), (lst, E["lst_d"]), (bre, E["bre_d"]),
                                                  (bim, E["bim_d"]), (crt, E["crt_d"]), (cit, E["cit_d"]))], E["s_small2"],
                w=["lre", "lim", "lst", "bre", "bim", "crt", "cit"])

    def tt(out, a, b, op, r, w, q=None):
        return P.op(q or dve, lambda v: v.tensor_tensor(out=out, in0=a, in1=b, op=op), r=r, w=w)

    def bc(a):
        return a.unsqueeze(2).to_broadcast([128, 32, 16])

    P.op(pool, lambda g: g.memset(WTall, 0.0), w=["WTall"])
    tctr = [0]
    def do_layer(j):
        P.op(pool, lambda g: g.memset(scr[:, 0:8192], 0.0), w=["CBall"])
        for t in (Bpr, Bpi, Cpr, nCpi):
            P.op(pool, lambda g, t=t: g.memset(t[:], 0.0), w=[t.name])
        dt_ = t32("dt"); lr = t32("lr"); ph = t32("ph"); lrd = t32("lrd")
        s8 = t32("s8"); c8 = t32("c8"); m8 = t32("m8")
        P.op(act, lambda a: a.activation(out=dt_[:], in_=lst[:, j, :], func=AF.Exp), r=["lst"], w=[dt_.name])
        P.op(dve, lambda v: v.tensor_scalar(out=lr[:], in0=lre[:, j, :], scalar1=-1e-4, scalar2=None, op0=ALU.min),
             r=["lre"], w=[lr.name])
        li_ = lim[:, j, :]
        tt(ph[:], li_, dt_[:], ALU.mult, ["lim", dt_.name], [ph.name])
        tt(lrd[:], lr[:], dt_[:], ALU.mult, [lr.name, dt_.name], [lrd.name])
        P.op(act, lambda a: a.activation(out=s8[:], in_=ph[:], func=AF.Sin, scale=0.125), r=[ph.name], w=[s8.name])
        P.op(act, lambda a: a.activation(out=c8[:], in_=ph[:], func=AF.Sin, scale=-0.125, bias=math.pi / 2),
             r=[ph.name], w=[c8.name])
        P.op(act, lambda a: a.activation(out=m8[:], in_=lrd[:], func=AF.Exp, scale=0.125), r=[lrd.name], w=[m8.name])
        ar = t32("ar"); ai = t32("ai")
        tt(ar[:], m8[:], c8[:], ALU.mult, [m8.name, c8.name], [ar.name])
        tt(ai[:], m8[:], s8[:], ALU.mult, [m8.name, s8.name], [ai.name])
        for it in range(3):
            q1 = t32("q1"); q2 = t32("q2"); q3 = t32("q3"); nr = t32(f"nr{it}"); ni = t32(f"ni{it}")
            tt(q1[:], ar[:], ar[:], ALU.mult, [ar.name], [q1.name])
            tt(q2[:], ai[:], ai[:], ALU.mult, [ai.name], [q2.name])
            tt(q3[:], ar[:], ai[:], ALU.mult, [ar.name, ai.name], [q3.name])
            tt(nr[:], q1[:], q2[:], ALU.subtract, [q1.name, q2.name], [nr.name])
            tt(ni[:], q3[:], q3[:], ALU.add, [q3.name], [ni.name])
            ar, ai = nr, ni
        pw = [None] * 9
        one = t32("one"); zero = t32("zero")
        P.op(pool, lambda g: g.memset(one[:], 1.0), w=[one.name])
        P.op(pool, lambda g: g.memset(zero[:], 0.0), w=[zero.name])
        pw[0] = (one, zero)
        pw[1] = (ar, ai)

        def cmul(xr, xi, yr, yi, n):
            a1 = t32("a1"); a2 = t32("a2"); a3 = t32("a3"); a4 = t32("a4"); zr = t32(f"zr{n}"); zi = t32(f"zi{n}")
            tt(a1[:], xr[:], yr[:], ALU.mult, [xr.name, yr.name], [a1.name])
            tt(a2[:], xi[:], yi[:], ALU.mult, [xi.name, yi.name], [a2.name])
            tt(a3[:], xr[:], yi[:], ALU.mult, [xr.name, yi.name], [a3.name])
            tt(a4[:], xi[:], yr[:], ALU.mult, [xi.name, yr.name], [a4.name])
            tt(zr[:], a1[:], a2[:], ALU.subtract, [a1.name, a2.name], [zr.name])
            tt(zi[:], a3[:], a4[:], ALU.add, [a3.name, a4.name], [zi.name])
            return zr, zi
        for n in range(2, 9):
            pw[n] = cmul(pw[n - 1][0], pw[n - 1][1], ar, ai, n)
        a8r, a8i = pw[8]
        P.op(dve, lambda v: v.tensor_copy(out=Ca[:, j, 0:32], in_=a8r[:]), r=[a8r.name], w=["Ca"])
        P.op(dve, lambda v: v.tensor_copy(out=Ca[:, j, 32:64], in_=a8r[:]), r=[a8r.name], w=["Ca"])
        P.op(dve, lambda v: v.tensor_copy(out=Cb[:, j, 32:64], in_=a8i[:]), r=[a8i.name], w=["Cb"])
        P.op(dve, lambda v: v.tensor_scalar(out=Cb[:, j, 0:32], in0=a8i[:], scalar1=-1.0, scalar2=None, op0=ALU.mult),
             r=[a8i.name], w=["Cb"])
        den = t32("den"); d2 = t32("d2"); rden = t32("rden"); am1 = t32("am1")
        tt(den[:], lr[:], lr[:], ALU.mult, [lr.name], [den.name])
        tt(d2[:], li_, li_, ALU.mult, ["lim"], [d2.name])
        tt(den[:], den[:], d2[:], ALU.add, [den.name, d2.name], [den.name])
        P.op(dve, lambda v: v.reciprocal(out=rden[:], in_=den[:]), r=[den.name], w=[rden.name])
        P.op(dve, lambda v: v.tensor_scalar(out=am1[:], in0=ar[:], scalar1=-1.0, scalar2=None, op0=ALU.add),
             r=[ar.name], w=[am1.name])
        e1 = t32("e1"); e2 = t32("e2"); qr = t32("qr"); qi = t32("qi")
        tt(e1[:], am1[:], lr[:], ALU.mult, [am1.name, lr.name], [e1.name])
        tt(e2[:], ai[:], li_, ALU.mult, [ai.name, "lim"], [e2.name])
        tt(e1[:], e1[:], e2[:], ALU.add, [e1.name, e2.name], [e1.name])
        tt(qr[:], e1[:], rden[:], ALU.mult, [e1.name, rden.name], [qr.name])
        e3 = t32("e3"); e4 = t32("e4")
        tt(e3[:], ai[:], lr[:], ALU.mult, [ai.name, lr.name], [e3.name])
        tt(e4[:], am1[:], li_, ALU.mult, [am1.name, "lim"], [e4.name])
        tt(e3[:], e3[:], e4[:], ALU.subtract, [e3.name, e4.name], [e3.name])
        tt(qi[:], e3[:], rden[:], ALU.mult, [e3.name, rden.name], [qi.name])
        B_r = bre[:, j, :].rearrange("p (a h) -> p a h", h=16)
        B_i = bim[:, j, :].rearrange("p (a h) -> p a h", h=16)
        tt(U1[:], B_r, bc(qr[:]), ALU.mult, ["bre", qr.name], ["U1"])
        tt(U2[:], B_i, bc(qi[:]), ALU.mult, ["bim", qi.name], ["U2"])
        tt(Bbr[:], U1[:], U2[:], ALU.subtract, ["U1", "U2"], ["Bbr"])
        tt(U1[:], B_i, bc(qr[:]), ALU.mult, ["bim", qr.name], ["U1"])
        tt(U2[:], B_r, bc(qi[:]), ALU.mult, ["bre", qi.name], ["U2"])
        tt(Bbi[:], U1[:], U2[:], ALU.add, ["U1", "U2"], ["Bbi"])
        C_r = crt[:, j, :].rearrange("p (a h) -> p a h", h=16)
        C_i = cit[:, j, :].rearrange("p (a h) -> p a h", h=16)
        for (lo, hi, c0) in ((0, 64, 0), (64, 128, 16)):
            P.op(dve, lambda v, lo=lo, hi=hi, c0=c0: v.tensor_copy(out=Cpr[lo:hi, :, c0:c0 + 16], in_=C_r[lo:hi]),
                 r=["crt"], w=["Cpr"])
            P.op(dve, lambda v, lo=lo, hi=hi, c0=c0: v.tensor_scalar(
                out=nCpi[lo:hi, :, c0:c0 + 16], in0=C_i[lo:hi], scalar1=-1.0, scalar2=None, op0=ALU.mult),
                r=["cit"], w=["nCpi"])
        P.op(dve, lambda v: v.tensor_scalar(out=nCr[:], in0=C_r, scalar1=-1.0, scalar2=None, op0=ALU.mult),
             r=["crt"], w=["nCr"])
        P.op(dve, lambda v: v.tensor_scalar(out=nCi[:], in0=C_i, scalar1=-1.0, scalar2=None, op0=ALU.mult),
             r=["cit"], w=["nCi"])
        def do_s(s):
            pr, pi_ = pw[7 - s]
            tt(U1[:], Bbr[:], bc(pr[:]), ALU.mult, ["Bbr", pr.name], ["U1"])
            tt(U2[:], Bbi[:], bc(pi_[:]), ALU.mult, ["Bbi", pi_.name], ["U2"])
            for (lo, hi, c0) in ((0, 64, 0), (64, 128, 16)):
                tt(Bpr[lo:hi, :, c0:c0 + 16], U1[lo:hi], U2[lo:hi], ALU.subtract, ["U1", "U2"], ["Bpr"])
            tt(U1[:], Bbi[:], bc(pr[:]), ALU.mult, ["Bbi", pr.name], ["U1"])
            tt(U2[:], Bbr[:], bc(pi_[:]), ALU.mult, ["Bbr", pi_.name], ["U2"])
            for (lo, hi, c0) in ((0, 64, 0), (64, 128, 16)):
                tt(Bpi[lo:hi, :, c0:c0 + 16], U1[lo:hi], U2[lo:hi], ALU.add, ["U1", "U2"], ["Bpi"])
            tau = 7 - s
            for k in range(8 if not (DBG & 4) else 0):
                bk = tctr[0] % 4
                tctr[0] += 1
                for ri, Bp in enumerate((Bpr, Bpi)):
                    src = Bp[:, 4 * k:4 * k + 4, :].rearrange("p a c -> p (a c)")
                    P.op(pe, lambda e, src=src, bk=bk, ri=ri: e.transpose(
                        out=ps[bk][:, ri * 128:(ri + 1) * 128], in_=src, identity=ident[:]),
                        r=[Bp.name, "ident"], w=[PSK[bk]], inc=False)
                srcr = Bpr[:, 4 * k:4 * k + 4, :].rearrange("p a c -> p (a c)")
                srci = Bpi[:, 4 * k:4 * k + 4, :].rearrange("p a c -> p (a c)")
                cr = Cpr[:, 4 * k:4 * k + 4, :].rearrange("p a c -> p (a c)")
                ci = nCpi[:, 4 * k:4 * k + 4, :].rearrange("p a c -> p (a c)")
                P.op(pe, lambda e, bk=bk, srcr=srcr, cr=cr: e.matmul(ps[bk][:, 256:384], lhsT=srcr, rhs=cr,
                                                                     start=True, stop=False),
                     r=["Bpr", "Cpr"], w=[PSK[bk]], inc=False)
                P.op(pe, lambda e, bk=bk, srci=srci, ci=ci: e.matmul(ps[bk][:, 256:384], lhsT=srci, rhs=ci,
                                                                     start=False, stop=True),
                     r=["Bpi", "nCpi"], w=[PSK[bk]], inc=True)
                P.op(act, lambda a, bk=bk, k=k, s=s: a.activation(
                    out=WBall[:, k, s, :, :], in_=ps[bk][:, 0:256].rearrange("p (r c) -> p r c", r=2), func=AF.Copy),
                    r=[PSK[bk]], w=["WBall"])
                for jq in range(4):
                    P.op(dve, lambda v, bk=bk, k=k, tau=tau, jq=jq: v.tensor_copy(
                        out=WTall[32 * jq:32 * jq + 32, k, tau, 32 * jq:32 * jq + 32],
                        in_=ps[bk][32 * jq:32 * jq + 32, 256 + 32 * jq:256 + 32 * jq + 32]),
                        r=[], w=["WTall", PSK[bk]])
            qr_, qi_ = pw[s + 1]
            tt(U1[:], C_r, bc(qr_[:]), ALU.mult, ["crt", qr_.name], ["U1"])
            tt(U2[:], C_i, bc(qi_[:]), ALU.mult, ["cit", qi_.name], ["U2"])
            for (lo, hi, c0) in ((0, 64, 0), (64, 128, 16)):
                tt(CBall[lo:hi, :, s, 0, c0:c0 + 16], U1[lo:hi], U2[lo:hi], ALU.subtract, ["U1", "U2"], ["CBall"])
            tt(U1[:], nCr[:], bc(qi_[:]), ALU.mult, ["nCr", qi_.name], ["U1"])
            tt(U2[:], nCi[:], bc(qr_[:]), ALU.mult, ["nCi", qr_.name], ["U2"])
            for (lo, hi, c0) in ((0, 64, 0), (64, 128, 16)):
                tt(CBall[lo:hi, :, s, 1, c0:c0 + 16], U1[lo:hi], U2[lo:hi], ALU.add, ["U1", "U2"], ["CBall"])
        for s_ in range(8):
            do_s(s_)
            for _ in range(6):
                next(E["ada_it"], None)
        if DBG & 8:
            return
        P.dma(sync, wbd[j].rearrange("k p n -> p k n"), scr[:, 8192:16384].bitcast(BF16).rearrange("p (k n) -> p k n", k=8),
              s_scr[0], r=["WBall"], w=[("wbd", j)])
        P.dma(sync, cbd[j].rearrange("k p n -> p k n"), scr[:, 0:8192].bitcast(BF16).rearrange("p (k n) -> p k n", k=8),
              s_scr[1], r=["CBall"], w=[("cbd", j)])
        P.dma(sync, wtd[j].rearrange("k p n -> p k n"), hb[:, :, :], s_scr[2], r=["WTall"], w=[("wtd", j)])
    for j_ in range(2):
        if 2 * j_ in E["layers"]:
            do_layer(j_)
    for sm in s_scr:
        if sm.count:
            for q in (pe, act, dve, pool, sync):
                q.wait((sm, sm.count))


def _load_tile(E, ti):
    P = E["P"]; pe, act, dve, sync = P.pe, P.act, P.dve, P.sync
    xT, xin, ps, PSK, ident = E["xT"], E["xin"], E["ps"], E["PSK"], E["ident"]
    xTv = xT[:, :, :].rearrange("p k (s c) -> p k s c", s=8)
    for blk in range(8):
        b = blk % 2
        r0 = ti * TT + blk * 128
        P.dma(sync, xin[:, b, :], E["x_d"][r0:r0 + 128, :], E["s_xin"][b], w=[("xin", b)])
        for half in range(2):
            bk = (blk * 2 + half) % 4
            for kk in range(4):
                k = half * 4 + kk
                P.op(pe, lambda e, b=b, bk=bk, k=k, kk=kk: e.transpose(
                    out=ps[bk][:, kk * 128:(kk + 1) * 128], in_=xin[:, b, k * 128:(k + 1) * 128], identity=ident[:]),
                    r=[("xin", b), "ident"], w=[PSK[bk]], inc=(kk == 3))
            q = act if half == 0 else dve
            src = ps[bk][:, :].rearrange("p (k jc s) -> p k s jc", k=4, jc=16, s=8)
            dst = xTv[:, half * 4:half * 4 + 4, :, 16 * blk:16 * blk + 16]
            if q is act:
                P.op(q, lambda a, src=src, dst=dst: a.activation(out=dst, in_=src, func=AF.Copy), r=[PSK[bk]], w=["xT"])
            else:
                P.op(q, lambda v, src=src, dst=dst: v.tensor_copy(out=dst, in_=src), r=[PSK[bk]], w=["xT"])


def _final_store(E, ti, do_final):
    P = E["P"]; pe, act, dve, sync = P.pe, P.act, P.dve, P.sync
    xT, xin, ps, PSK, ident, scr = E["xT"], E["xin"], E["ps"], E["PSK"], E["ident"], E["scr"]
    if do_final:
        _norm_mod(E, None, 2)
        src_all = scr[:, 0:8192].rearrange("p (k s c) -> p k s c", k=8, s=8)
        skey = "fout"
    else:
        src_all = xT[:, :, :].rearrange("p k (s c) -> p k s c", s=8)
        skey = "xT"
    for blk in range(8):
        b = blk % 2
        for half in range(2):
            a = (blk * 2 + half) % 2
            bk = (blk * 2 + half) % 4
            stg = scr[:, 8192 + a * 512: 8192 + (a + 1) * 512]
            src = src_all[:, half * 4:half * 4 + 4, :, 16 * blk:16 * blk + 16]
            dstv = stg.rearrange("p (k jc s) -> p k s jc", k=4, jc=16, s=8)
            if half == 0:
                P.op(act, lambda e, src=src, dstv=dstv: e.activation(out=dstv, in_=src, func=AF.Copy),
                     r=[skey], w=[("stg", a)])
            else:
                P.op(dve, lambda e, src=src, dstv=dstv: e.tensor_copy(out=dstv, in_=src), r=[skey], w=[("stg", a)])
            for kk in range(4):
                P.op(pe, lambda e, stg=stg, bk=bk, kk=kk: e.transpose(
                    out=ps[bk][:, kk * 128:(kk + 1) * 128], in_=stg[:, kk * 128:(kk + 1) * 128], identity=ident[:]),
                    r=[("stg", a), "ident"], w=[PSK[bk]], inc=(kk == 3))
            dsto = xin[:, b, half * 512:(half + 1) * 512]
            if half == 0:
                P.op(dve, lambda e, bk=bk, dsto=dsto: e.tensor_copy(out=dsto, in_=ps[bk][:, :]),
                     r=[PSK[bk]], w=[("xin", b)])
            else:
                P.op(act, lambda e, bk=bk, dsto=dsto: e.activation(out=dsto, in_=ps[bk][:, :], func=AF.Copy),
                     r=[PSK[bk]], w=[("xin", b)])
        r0 = ti * TT + blk * 128
        P.dma(sync, E["y_d"][r0:r0 + 128, :], xin[:, b, :], E["s_out"][b], r=[("xin", b)])


def _norm_mod(E, li, which):
    P = E["P"]; pe, act, dve = P.pe, P.act, P.dve
    xT, hb, sq, rstd, tmpf, ps, PSK = E["xT"], E["hb"], E["sq"], E["rstd"], E["tmpf"], E["ps"], E["PSK"]
    ones_bf, gsc, mods, fing32, scr = E["ones_bf"], E["gsc"], E["mods"], E["fing32"], E["scr"]
    fout = scr[:, 0:8192].rearrange("p (k n) -> p k n", k=8)
    for tb in range(2):
        sl = slice(tb * 512, (tb + 1) * 512)
        bk = 6 + tb
        P.op(act, lambda a, sl=sl: a.activation(out=sq[:], in_=xT[:, :, sl], func=AF.Square), r=["xT"], w=["sq"])
        for k in range(8):
            P.op(pe, lambda e, k=k, bk=bk: e.matmul(ps[bk][:, :], lhsT=ones_bf[:], rhs=sq[:, k, :],
                                                    start=(k == 0), stop=(k == 7)),
                 r=["sq", "ones"], w=[PSK[bk]], inc=(k == 7))
        P.op(act, lambda a, tb=tb, bk=bk: a.activation(out=rstd[:, tb, :], in_=ps[bk][:, :], func=AF.Sqrt,
                                                       bias=D * EPS, scale=1.0),
             r=[PSK[bk]], w=[("rstd", tb)])
        P.op(dve, lambda v, tb=tb: v.reciprocal(out=rstd[:, tb, :], in_=rstd[:, tb, :]),
             r=[("rstd", tb)], w=[("rstd", tb)])
        for k in range(8):
            if which == 2:
                P.op(dve, lambda v, k=k, sl=sl, tb=tb: v.scalar_tensor_tensor(
                    out=fout[:, k, sl], in0=xT[:, k, sl], scalar=fing32[:, k:k + 1], in1=rstd[:, tb, :],
                    op0=ALU.mult, op1=ALU.mult), r=["xT", ("rstd", tb), "fing32"], w=["fout"])
                continue
            tbuf = k % 4
            P.op(dve, lambda v, k=k, sl=sl, tb=tb, tbuf=tbuf: v.scalar_tensor_tensor(
                out=tmpf[:, tbuf, :], in0=xT[:, k, sl], scalar=gsc[:, li, which, k:k + 1], in1=rstd[:, tb, :],
                op0=ALU.mult, op1=ALU.mult), r=["xT", ("rstd", tb), "gsc"], w=[("tmpf", tbuf)])
            shc = (3 * which) * 8 + k
            P.op(act, lambda a, k=k, sl=sl, tbuf=tbuf, shc=shc: a.activation(
                out=hb[:, k, sl], in_=tmpf[:, tbuf, :], func=AF.Identity, bias=mods[:, li, shc:shc + 1], scale=1.0),
                r=[("tmpf", tbuf), "mods"], w=["hb"])


def _ffn(E, li):
    P = E["P"]; pe, act, dve, pool = P.pe, P.act, P.dve, P.pool
    xT, hb, scr, tmpf, ps, PSK, mods = E["xT"], E["hb"], E["scr"], E["tmpf"], E["ps"], E["PSK"], E["mods"]
    ringA, ringB = E["ringA"], E["ringB"]
    actb = scr[:, 0:11264].bitcast(BF16).rearrange("p (j n) -> p j n", j=FC)
    wi = E["wfi_d"][li].rearrange("(k p) n -> p k n", p=128)
    wo = E["wfo_d"][li].rearrange("(j p) n -> p j n", p=128)
    cnt = 0
    for grp in range(11):
        buf, sem, key = ringA.next()
        bv = buf[:, :].rearrange("p (g k n) -> p g k n", g=2, k=8)
        P.dma_multi(pool, [(bv[:, 0], wi[:, :, grp * 256:(grp + 1) * 256]),
                           (bv[:, 1], wi[:, :, FF + grp * 256:FF + (grp + 1) * 256])], sem, w=[key])
        for jj in range(2):
            j = grp * 2 + jj
            for tb in range(2):
                sl = slice(tb * 512, (tb + 1) * 512)
                gb = cnt % 2
                ub = 2 + cnt % 2
                cnt += 1
                for k in range(8):
                    P.op(pe, lambda e, bv=bv, jj=jj, k=k, sl=sl, gb=gb: e.matmul(
                        ps[gb][:, :], lhsT=bv[:, 0, k, jj * 128:(jj + 1) * 128], rhs=hb[:, k, sl],
                        start=(k == 0), stop=(k == 7)), r=[key, "hb"], w=[PSK[gb]], inc=(k == 7))
                for k in range(8):
                    P.op(pe, lambda e, bv=bv, jj=jj, k=k, sl=sl, ub=ub: e.matmul(
                        ps[ub][:, :], lhsT=bv[:, 1, k, jj * 128:(jj + 1) * 128], rhs=hb[:, k, sl],
                        start=(k == 0), stop=(k == 7)), r=[key, "hb"], w=[PSK[ub]], inc=(k == 7))
                tbuf = cnt % 4
                P.op(act, lambda a, gb=gb, tbuf=tbuf: a.activation(out=tmpf[:, tbuf, :], in_=ps[gb][:, :], func=AF.Silu),
                     r=[PSK[gb]], w=[("tmpf", tbuf)])
                P.op(dve, lambda v, ub=ub, tbuf=tbuf, j=j, sl=sl: v.tensor_tensor(
                    out=actb[:, j, sl], in0=ps[ub][:, :], in1=tmpf[:, tbuf, :], op=ALU.mult),
                    r=[PSK[ub], ("tmpf", tbuf)], w=[("actb", j)])
    cnt = 0
    for mg in range(4):
        buf, sem, key = ringB.next()
        bv = buf[:, :].rearrange("p (j n) -> p j n", j=FC)
        P.dma(pool, bv, wo[:, :, mg * 256:(mg + 1) * 256], sem, w=[key])
        for mm in range(2):
            m = mg * 2 + mm
            for tb in range(2):
                sl = slice(tb * 512, (tb + 1) * 512)
                ob = 4 + cnt % 2
                cnt += 1
                for j in range(FC):
                    P.op(pe, lambda e, bv=bv, mm=mm, j=j, sl=sl, ob=ob: e.matmul(
                        ps[ob][:, :], lhsT=bv[:, j, mm * 128:(mm + 1) * 128], rhs=actb[:, j, sl],
                        start=(j == 0), stop=(j == FC - 1)), r=[key, ("actb", j)], w=[PSK[ob]], inc=(j == FC - 1))
                gc = 5 * 8 + m
                P.op(dve, lambda v, ob=ob, m=m, sl=sl, gc=gc: v.scalar_tensor_tensor(
                    out=xT[:, m, sl], in0=ps[ob][:, :], scalar=mods[:, li, gc:gc + 1], in1=xT[:, m, sl],
                    op0=ALU.mult, op1=ALU.add), r=[PSK[ob], "mods", "xT"], w=["xT"])


def _conv_mixer(E, li, ti):
    P = E["P"]; pe, act, dve, pool = P.pe, P.act, P.dve, P.pool
    xT, hb, scr, tmpf, gel, ps, PSK, mods = E["xT"], E["hb"], E["scr"], E["tmpf"], E["gel"], E["ps"], E["PSK"], E["mods"]
    cw, halo = E["cw"], E["halo"]
    ringA, ringB = E["ringA"], E["ringB"]
    jl = li // 2
    mbuf = scr[:, 0:4096].bitcast(BF16).rearrange("p (k n) -> p k n", k=8)
    cvb = [scr[:, 4096 + i * 1032: 4096 + (i + 1) * 1032].rearrange("p (s c) -> p s c", s=8) for i in range(2)]
    accb = [scr[:, 6400 + i * 1024: 6400 + (i + 1) * 1024] for i in range(2)]
    wi = E["cwi_d"][jl].rearrange("(k p) n -> p k n", p=128)
    wo = E["cwo_d"][jl].rearrange("(k p) n -> p k n", p=128)
    for m in range(8):
        buf, sem, key = ringA.next()
        bv = buf[:, 0:3072].rearrange("p (g k n) -> p g k n", g=3, k=8)
        P.dma_multi(pool, [(bv[:, g], wi[:, :, g * D + m * 128: g * D + (m + 1) * 128]) for g in range(3)], sem, w=[key])
        cv = cvb[m % 2]
        acc = accb[m % 2]
        bgb = gel[:, m % 2, :]
        ck, ak, bk_ = ("cvb", m % 2), ("acc", m % 2), ("gel", m % 2)
        for tb in range(2):
            sl = slice(tb * 512, (tb + 1) * 512)
            base = 3 * ((2 * m + tb) % 2)
            for g in range(3):
                for k in range(8):
                    P.op(pe, lambda e, bv=bv, g=g, k=k, sl=sl, base=base: e.matmul(
                        ps[base + g][:, :], lhsT=bv[:, g, k, :], rhs=hb[:, k, sl], start=(k == 0), stop=(k == 7)),
                        r=[key, "hb"], w=[PSK[base + g]], inc=(k == 7))
            tbuf = (2 * m + tb) % 4
            P.op(act, lambda a, base=base, sl=sl, bgb=bgb: a.activation(out=bgb[:, sl], in_=ps[base][:, :], func=AF.Copy),
                 r=[PSK[base]], w=[bk_])
            P.op(act, lambda a, base=base, tbuf=tbuf: a.activation(out=tmpf[:, tbuf, :], in_=ps[base + 1][:, :], func=AF.Copy),
                 r=[PSK[base + 1]], w=[("tmpf", tbuf)])
            P.op(dve, lambda v, base=base, tbuf=tbuf, cv=cv, tb=tb: v.tensor_tensor(
                out=cv[:, 4 * tb:4 * tb + 4, 1:129], in0=ps[base + 2][:, :].rearrange("p (s c) -> p s c", s=4),
                in1=tmpf[:, tbuf, :].rearrange("p (s c) -> p s c", s=4), op=ALU.mult),
                r=[PSK[base + 2], ("tmpf", tbuf)], w=[ck])
        P.op(dve, lambda v, cv=cv, m=m: v.tensor_copy(out=cv[:, :, 0], in_=halo[:, jl, m, :]), r=["halo", ck], w=[ck])
        P.op(dve, lambda v, cv=cv, m=m: v.tensor_copy(out=halo[:, jl, m, :], in_=cv[:, :, 128]), r=[ck, "halo"], w=["halo"])
        a3 = acc.rearrange("p (s c) -> p s c", s=8)
        w0 = cw[:, jl, m, 0:1]; w1 = cw[:, jl, m, 1:2]; w2 = cw[:, jl, m, 2:3]
        P.op(dve, lambda v, a3=a3, cv=cv, w2=w2: v.tensor_scalar(out=a3, in0=cv[:, :, 1:129], scalar1=w2, scalar2=None,
                                                                op0=ALU.mult), r=[ck, "cw"], w=[ak])
        for (o_, i_, wv) in ((a3[:, 1:8, :], cv[:, 0:7, 1:129], w1), (a3[:, 0:1, :], cv[:, 7:8, 0:128], w1),
                             (a3[:, 2:8, :], cv[:, 0:6, 1:129], w0), (a3[:, 0:2, :], cv[:, 6:8, 0:128], w0)):
            P.op(dve, lambda v, o_=o_, i_=i_, wv=wv: v.scalar_tensor_tensor(
                out=o_, in0=i_, scalar=wv, in1=o_, op0=ALU.mult, op1=ALU.add), r=[ck, ak, "cw"], w=[ak])
        P.op(dve, lambda v, acc=acc, bgb=bgb, m=m: v.tensor_tensor(out=mbuf[:, m, :], in0=acc, in1=bgb, op=ALU.mult),
             r=[ak, bk_], w=[("mbuf", m)])
    cnt = 0
    for mg in range(4):
        buf, sem, key = ringB.next()
        bv = buf[:, 0:2048].rearrange("p (k n) -> p k n", k=8)
        P.dma(pool, bv, wo[:, :, mg * 256:(mg + 1) * 256], sem, w=[key])
        for mm in range(2):
            m = mg * 2 + mm
            for tb in range(2):
                sl = slice(tb * 512, (tb + 1) * 512)
                ob = 6 + cnt % 2
                cnt += 1
                for k in range(8):
                    P.op(pe, lambda e, bv=bv, mm=mm, k=k, sl=sl, ob=ob: e.matmul(
                        ps[ob][:, :], lhsT=bv[:, k, mm * 128:(mm + 1) * 128], rhs=mbuf[:, k, sl],
                        start=(k == 0), stop=(k == 7)), r=[key, ("mbuf", k)], w=[PSK[ob]], inc=(k == 7))
                gc = 2 * 8 + m
                P.op(dve, lambda v, ob=ob, m=m, sl=sl, gc=gc: v.scalar_tensor_tensor(
                    out=xT[:, m, sl], in0=ps[ob][:, :], scalar=mods[:, li, gc:gc + 1], in1=xT[:, m, sl],
                    op0=ALU.mult, op1=ALU.add), r=[PSK[ob], "mods", "xT"], w=["xT"])


def _ssm_mixer(E, li, ti):
    P = E["P"]; pe, act, dve, pool, sync = P.pe, P.act, P.dve, P.pool, P.sync
    xT, hb, scr, tmpf, gel, ps, PSK, mods = E["xT"], E["hb"], E["scr"], E["tmpf"], E["gel"], E["ps"], E["PSK"], E["mods"]
    hprev, sstate, Ca, Cb, dd = E["hprev"], E["sstate"], E["Ca"], E["Cb"], E["dd"]
    T1, T2, T3 = E["T1"], E["T2"], E["T3"]
    ringA = E["ringA"]
    jl = li // 2
    XS = scr[:, 0:8256].rearrange("p (r c) -> p r c", r=64)
    XSb = scr[:, 8256:12352].bitcast(BF16).rearrange("p (r c) -> p r c", r=64)
    ybuf = scr[:, 12352:16448].bitcast(BF16).rearrange("p (k n) -> p k n", k=8)
    hv = hb[:, :, :].rearrange("p k (s c) -> p k s c", s=8)
    P.op(dve, lambda v: v.tensor_copy(out=XS[:, :, 0], in_=sstate[:, jl, :]), r=["sstate"], w=["XS"])
    for k in range(8):
        buf, sem, key = ringA.next(hw=True)
        bv = buf[:, 0:2048].rearrange("p (s r c) -> p s r c", s=8, r=2)
        P.dma(sync, buf[:, 0:2048], E["wbd"][jl, k], sem, r=[("wbd", jl)], w=[key])
        for ri in range(2):
            for s in range(8):
                for jq in range(4):
                    p0 = 32 * jq
                    P.op(pe, lambda e, bv=bv, p0=p0, ri=ri, s=s, k=k, jq=jq: e.matmul(
                        ps[ri * 4 + jq][:, 0:128], lhsT=bv[p0:p0 + 32, s, ri, :],
                        rhs=hv[p0:p0 + 32, k, s, :], start=(s == 0), stop=(s == 7), tile_position=(p0, 0)),
                        r=[key, "hb"], w=[PSK[ri * 4 + jq]], inc=(s == 7 and jq == 3))
            for jq in range(4):
                pi_ = 4 * k + jq
                dst = XS[:, ri * 32 + pi_, 1:129]
                src = ps[ri * 4 + jq][:, 0:128]
                if jq % 2 == 0:
                    P.op(act, lambda a, dst=dst, src=src: a.activation(out=dst, in_=src, func=AF.Copy),
                         r=[PSK[ri * 4 + jq]], w=["XS"])
                else:
                    P.op(dve, lambda v, dst=dst, src=src: v.tensor_copy(out=dst, in_=src), r=[PSK[ri * 4 + jq]], w=["XS"])
    def mm(out, lhsT, rhs, rk, start=False, bank=0, inc=False, tp=None):
        if tp is None:
            P.op(pe, lambda e: e.matmul(out, lhsT=lhsT, rhs=rhs, start=start, stop=inc, skip_group_check=True),
                 r=rk, w=[PSK[bank]], inc=inc)
        else:
            P.op(pe, lambda e: e.matmul(out, lhsT=lhsT, rhs=rhs, start=start, stop=inc, skip_group_check=True,
                                        tile_position=tp),
                 r=rk, w=[PSK[bank]], inc=inc)

    def near(k, wtv, yb, rk):
        for hf in range(2):
            mm(ps[yb[hf]][:, :], wtv[:, 0, :], hb[:, k, hf * 512:(hf + 1) * 512], rk, start=True, bank=yb[hf])
        for tau in range(1, 8):
            lo = tau * 128
            if lo < 512:
                mm(ps[yb[0]][:, lo:512], wtv[:, tau, :], hb[:, k, 0:512 - lo], rk, bank=yb[0])
            lo2 = max(512, lo)
            mm(ps[yb[1]][:, lo2 - 512:512], wtv[:, tau, :], hb[:, k, lo2 - lo:1024 - lo], rk, bank=yb[1])
            for s in range(tau):
                hf, sc = s // 4, s % 4
                src_s = s + 8 - tau
                mm(ps[yb[hf]][:, sc * 128 + 1:sc * 128 + 128], wtv[:, tau, :],
                   hb[:, k, src_s * 128:src_s * 128 + 127], rk, bank=yb[hf])
                mm(ps[yb[hf]][:, sc * 128:sc * 128 + 1], wtv[:, tau, :], hprev[:, jl, k, src_s:src_s + 1], rk, bank=yb[hf])

    def far(k, cbv, yb, rk):
        for s in range(8):
            hf, sc = s // 4, s % 4
            for ri in range(2):
                for jq in range(4):
                    pi_ = 4 * k + jq
                    last = (jq == 3 and s in (3, 7) and ri == 1)
                    mm(ps[yb[hf]][32 * jq:32 * jq + 32, sc * 128:(sc + 1) * 128], cbv[:, jq, s, ri, :],
                       XSb[:, ri * 32 + pi_, :], rk, bank=yb[hf], inc=last, tp=(0, 32 * jq))

    def evac(k, yb):
        for hf in range(2):
            sl = slice(hf * 512, (hf + 1) * 512)
            g_ = gel[:, hf, 0:512]
            P.op(dve, lambda v, hf=hf, sl=sl, k=k, g_=g_, yb=yb: v.scalar_tensor_tensor(
                out=g_, in0=hb[:, k, sl], scalar=dd[:, jl, k:k + 1], in1=ps[yb[hf]][:, :],
                op0=ALU.mult, op1=ALU.add), r=["hb", "dd", PSK[yb[hf]]], w=[("gel", hf)])
            P.op(act, lambda a, sl=sl, k=k, g_=g_: a.activation(out=ybuf[:, k, sl], in_=g_, func=AF.Gelu_apprx_tanh),
                 r=[("gel", hf)], w=[("ybuf", k)])

    bufW, semW, keyW = ringA.next(hw=True)
    P.dma(sync, bufW[:, :].rearrange("p (k n) -> p k n", k=4), E["wtd"][jl, 0:4].rearrange("k p n -> p k n"), semW,
          r=[("wtd", jl)], w=[keyW])
    for k in range(4):
        wtv = bufW[:, k * 1024:(k + 1) * 1024].rearrange("p (t c) -> p t c", t=8)
        near(k, wtv, [2 * k, 2 * k + 1], [keyW, "hb", "hprev"])
    xs_t = XS.tensor
    pstep = XS.ap[0][0]
    for c in range(128 if not (DBG & 1) else 0):
        cur = XS[:, :, c]
        nxt = XS[:, :, c + 1]
        swp = bass.AP(xs_t, XS.offset + 32 * 129 + c, [[pstep, 128], [-32 * 129, 2], [129, 32]])
        P.op(dve, lambda v, cur=cur: v.tensor_tensor(out=T1[:], in0=cur, in1=Ca[:, jl, :], op=ALU.mult),
             r=["XS", "Ca"], w=["T1"])
        P.op(dve, lambda v, swp=swp: v.tensor_tensor(out=T2[:].rearrange("p (r c) -> p r c", r=2), in0=swp,
                                                     in1=Cb[:, jl, :].rearrange("p (r c) -> p r c", r=2), op=ALU.mult),
             r=["XS", "Cb"], w=["T2"])
        P.op(dve, lambda v: v.tensor_tensor(out=T3[:], in0=T1[:], in1=T2[:], op=ALU.add), r=["T1", "T2"], w=["T3"])
        P.op(dve, lambda v, nxt=nxt: v.tensor_tensor(out=nxt, in0=T3[:], in1=nxt, op=ALU.add), r=["T3", "XS"], w=["XS"])
    P.op(dve, lambda v: v.tensor_copy(out=sstate[:, jl, :], in_=XS[:, :, 128]), r=["XS"], w=["sstate"])
    for q4 in range(4):
        P.op(act, lambda a, q4=q4: a.activation(out=XSb[:, 16 * q4:16 * q4 + 16, :], in_=XS[:, 16 * q4:16 * q4 + 16, 0:128],
                                                 func=AF.Copy), r=["XS"], w=["XSb"])
    for k in range(8):
        buf, sem, key = ringA.next(hw=True)
        cbv = buf[:, 0:2048].rearrange("p (a s r c) -> p a s r c", a=4, s=8, r=2)
        yb = [2 * (k % 4), 2 * (k % 4) + 1]
        if k < 4:
            P.dma(sync, buf[:, 0:2048], E["cbd"][jl, k], sem, r=[("cbd", jl)], w=[key])
        else:
            wtv = buf[:, 2048:3072].rearrange("p (t c) -> p t c", t=8)
            P.dma_multi(sync, [(buf[:, 0:2048], E["cbd"][jl, k]), (buf[:, 2048:3072], E["wtd"][jl, k])], sem,
                        r=[("cbd", jl), ("wtd", jl)], w=[key])
            near(k, wtv, yb, [key, "hb", "hprev"])
        far(k, cbv, yb, [key, "XSb"])
        evac(k, yb)
    P.op(dve, lambda v: v.tensor_copy(out=hprev[:, jl, :, :], in_=hv[:, :, :, 127]), r=["hb", "hprev"], w=["hprev"])
    wo = E["swo_d"][jl].rearrange("(k p) n -> p k n", p=128)
    cnt = 0
    for mg in range(4):
        buf, sem, key = ringA.next()
        bv = buf[:, :].rearrange("p (g k n) -> p g k n", g=2, k=8)
        P.dma_multi(pool, [(bv[:, 0], wo[:, :, mg * 256:(mg + 1) * 256]),
                           (bv[:, 1], wo[:, :, D + mg * 256:D + (mg + 1) * 256])], sem, w=[key])
        for mm_ in range(2):
            m = mg * 2 + mm_
            for tb in range(2):
                sl = slice(tb * 512, (tb + 1) * 512)
                vb = 4 + cnt % 2
                gb = 6 + cnt % 2
                cnt += 1
                for g, bnk in ((0, vb), (1, gb)):
                    for k in range(8):
                        P.op(pe, lambda e, bv=bv, g=g, mm_=mm_, k=k, sl=sl, bnk=bnk: e.matmul(
                            ps[bnk][:, :], lhsT=bv[:, g, k, mm_ * 128:(mm_ + 1) * 128], rhs=ybuf[:, k, sl],
                            start=(k == 0), stop=(k == 7)), r=[key, ("ybuf", k)], w=[PSK[bnk]], inc=(k == 7))
                tbuf = cnt % 4
                P.op(act, lambda a, gb=gb, tbuf=tbuf: a.activation(out=tmpf[:, tbuf, :], in_=ps[gb][:, :], func=AF.Sigmoid),
                     r=[PSK[gb]], w=[("tmpf", tbuf)])
                P.op(dve, lambda v, vb=vb, tbuf=tbuf: v.tensor_tensor(out=tmpf[:, tbuf, :], in0=ps[vb][:, :],
                                                                      in1=tmpf[:, tbuf, :], op=ALU.mult),
                     r=[PSK[vb], ("tmpf", tbuf)], w=[("tmpf", tbuf)])
                gc = 2 * 8 + m
                P.op(dve, lambda v, tbuf=tbuf, m=m, sl=sl, gc=gc: v.scalar_tensor_tensor(
                    out=xT[:, m, sl], in0=tmpf[:, tbuf, :], scalar=mods[:, li, gc:gc + 1], in1=xT[:, m, sl],
                    op0=ALU.mult, op1=ALU.add), r=[("tmpf", tbuf), "mods", "xT"], w=["xT"])


def _prep_inputs(inp, b):
    f = np.float32
    g = lambda a: np.ascontiguousarray(np.asarray(a, dtype=f))

    def pk(v):
        v = np.asarray(v, dtype=f)
        lead = v.shape[:-1]
        return g(np.moveaxis(v.reshape(lead + (8, 128)), -1, 0))

    def gp(a):
        a = np.asarray(a, dtype=f).reshape(2, 32, 2, 64)
        return g(a.transpose(2, 3, 0, 1).reshape(128, 2, 32))
    m = {}
    m["x"] = g(inp["x"][b])
    m["c"] = pk(inp["c"][b])
    m["n1g"] = pk(inp["norm1_g"])
    m["n2g"] = pk(inp["norm2_g"])
    m["fing"] = pk(inp["final_g"])
    m["w_ada"] = g(inp["w_ada"])
    m["b_ada"] = g(np.asarray(inp["b_ada"], dtype=f).reshape(4, 48, 128).transpose(2, 0, 1))
    m["lre"] = gp(inp["ssm_a_re"])
    m["lim"] = gp(inp["ssm_a_im"])
    ls = np.broadcast_to(np.asarray(inp["ssm_log_step"], dtype=f)[:, :, None], (2, 64, 64))
    m["lst"] = gp(ls)
    for nm, src in (("bre", "ssm_b_re"), ("bim", "ssm_b_im")):
        a = np.asarray(inp[src], dtype=f).reshape(2, 32, 2, 64, 16)
        m[nm] = g(a.transpose(2, 3, 0, 1, 4).reshape(128, 2, 512))
    for nm, src in (("crt", "ssm_c_re"), ("cit", "ssm_c_im")):
        a = np.asarray(inp[src], dtype=f).reshape(2, 32, 2, 16, 64)
        m[nm] = g(a.transpose(2, 4, 0, 1, 3).reshape(128, 2, 512))
    m["dd"] = pk(inp["ssm_d"])
    m["ssm_w_out"] = g(inp["ssm_w_out"])
    m["conv_w_in"] = g(inp["conv_w_in"])
    cwv = np.asarray(inp["conv_w"], dtype=f).reshape(2, 3, 8, 128)
    m["cw"] = g(cwv.transpose(3, 0, 2, 1))
    m["conv_w_out"] = g(inp["conv_w_out"])
    m["w_ffn_in"] = g(inp["w_ffn_in"])
    m["w_ffn_out"] = g(inp["w_ffn_out"])
    return m


_NC_CACHE = {}


def kernel(**inputs):
    if "full" not in _NC_CACHE:
        _NC_CACHE["full"] = build()
    nc = _NC_CACHE["full"]
    n = 8
    in_maps = [_prep_inputs(inputs, b) for b in range(n)]
    res = run_bass_kernel_spmd(nc, in_maps, core_ids=list(range(n)))
    out = np.stack([np.asarray(r["y"], dtype=np.float32) for r in res.results], axis=0)
    return out
```

```python
import contextlib
import math
import os
DBG = int(os.environ.get('KDBG', '0'))
OPT = int(os.environ.get('KOPT', '0'))
import numpy as np
import concourse.bass as bass
import concourse.mybir as mybir
from concourse.bass_utils import run_bass_kernel_spmd

F32 = mybir.dt.float32
BF16 = mybir.dt.bfloat16
ALU = mybir.AluOpType
AF = mybir.ActivationFunctionType

D = 1024
KC = 8
TT = 1024
SEQ = 4096
FF = 2816
FC = 22
EPS = 1e-6


class Sem:
    def __init__(self, name):
        self.name = name
        self.handle = None
        self.count = 0


class Q:
    def __init__(self, prog, name):
        self.prog = prog
        self.name = name
        self.items = []
        self.sem = prog.sem("q_" + name)
        self.waited = {}
        self.pending = []
        self.last = None

    def wait(self, ev):
        if ev is None:
            return
        s, v = ev
        if self.waited.get(s, 0) >= v:
            return
        self.waited[s] = v
        self.items.append(lambda e, s=s, v=v: e.wait_ge(s.handle, v))


class Prog:
    def __init__(self, nc):
        self.nc = nc
        self.sems = []
        self.q = {}
        for n in ["sync", "scalar", "vector", "gpsimd", "tensor"]:
            self.q[n] = Q(self, n)
        self.sync = self.q["sync"]
        self.act = self.q["scalar"]
        self.dve = self.q["vector"]
        self.pool = self.q["gpsimd"]
        self.pe = self.q["tensor"]
        self.state = {}
        self.dma_events = []

    def sem(self, name):
        s = Sem(name)
        self.sems.append(s)
        return s

    def _st(self, k):
        st = self.state.get(k)
        if st is None:
            st = {"w": None, "r": []}
            self.state[k] = st
        return st

    def _deps(self, q, r, w):
        for k in r:
            q.wait(self._st(k)["w"])
        for k in w:
            st = self._st(k)
            q.wait(st["w"])
            for ev in st["r"]:
                q.wait(ev)

    def _commit(self, ev, r, w):
        for k in r:
            self._st(k)["r"].append(ev)
        for k in w:
            self.state[k] = {"w": ev, "r": []}

    def op(self, q, fn, r=(), w=(), inc=True):
        self._deps(q, r, w)
        if inc:
            s = q.sem
            s.count += 1
            ev = (s, s.count)
            q.items.append(lambda e, fn=fn, s=s: fn(e).then_inc(s.handle, 1))
            for (pr, pw) in q.pending:
                self._commit(ev, pr, pw)
            q.pending = []
            self._commit(ev, r, w)
            q.last = ev
            return ev
        q.items.append(lambda e, fn=fn: fn(e))
        q.pending.append((tuple(r), tuple(w)))
        return None

    def dma(self, q, out, in_, sem, r=(), w=()):
        self._deps(q, r, w)
        sem.count += 16
        ev = (sem, sem.count)
        q.items.append(lambda e, out=out, in_=in_, sem=sem: e.dma_start(out=out, in_=in_).then_inc(sem.handle, 16))
        self._commit(ev, r, w)
        self.dma_events.append(ev)
        return ev

    def dma_multi(self, q, pairs, sem, r=(), w=()):
        self._deps(q, r, w)
        for (out, in_) in pairs:
            sem.count += 16
            q.items.append(lambda e, out=out, in_=in_, sem=sem: e.dma_start(out=out, in_=in_).then_inc(sem.handle, 16))
        ev = (sem, sem.count)
        self._commit(ev, r, w)
        return ev

    def barrier(self, queues=None):
        qs = queues or [self.pe, self.act, self.dve, self.pool]
        evs = [q.last for q in qs if q.last is not None]
        for q in qs:
            for ev in evs:
                if ev[0] is not q.sem:
                    q.wait(ev)

    def emit(self):
        nc = self.nc
        with contextlib.ExitStack() as st:
            for s in self.sems:
                s.handle = st.enter_context(nc.semaphore(s.name))
            block = st.enter_context(nc.Block())
            for n in ["sync", "scalar", "vector", "gpsimd", "tensor"]:
                items = self.q[n].items

                def body(eng, items=items):
                    for it in items:
                        it(eng)
                getattr(block, n)(body)


class Ring:
    def __init__(self, P, name, bufs):
        self.P = P
        self.bufs = bufs
        self.sems = [P.sem(f"{name}{i}") for i in range(len(bufs))]
        self.sems_hw = [P.sem(f"{name}h{i}") for i in range(len(bufs))]
        self.keys = [(name, i) for i in range(len(bufs))]
        self.i = 0

    def next(self, hw=False):
        i = self.i % len(self.bufs)
        self.i += 1
        return self.bufs[i], (self.sems_hw[i] if hw else self.sems[i]), self.keys[i]


def build(ntiles=4, layers=(0, 1, 2, 3), do_final=True):
    nc = bass.Bass("TRN2", target_bir_lowering=False)
    P = Prog(nc)
    pe, act, dve, pool, sync = P.pe, P.act, P.dve, P.pool, P.sync

    def din(name, shape, dt=F32):
        return nc.dram_tensor(name, list(shape), dt, kind="ExternalInput").ap()

    x_d = din("x", [SEQ, D])
    c_d = din("c", [128, 8])
    n1g_d = din("n1g", [128, 4, 8])
    n2g_d = din("n2g", [128, 4, 8])
    fing_d = din("fing", [128, 8])
    wada_d = din("w_ada", [4, D, 6 * D])
    bada_d = din("b_ada", [128, 4, 48])
    lre_d = din("lre", [128, 2, 32])
    lim_d = din("lim", [128, 2, 32])
    lst_d = din("lst", [128, 2, 32])
    bre_d = din("bre", [128, 2, 512])
    bim_d = din("bim", [128, 2, 512])
    crt_d = din("crt", [128, 2, 512])
    cit_d = din("cit", [128, 2, 512])
    dd_d = din("dd", [128, 2, 8])
    swo_d = din("ssm_w_out", [2, D, 2 * D])
    cwi_d = din("conv_w_in", [2, D, 3 * D])
    cw_d = din("cw", [128, 2, 8, 3])
    cwo_d = din("conv_w_out", [2, D, D])
    wfi_d = din("w_ffn_in", [4, D, 2 * FF])
    wfo_d = din("w_ffn_out", [4, FF, D])
    y_d = nc.dram_tensor("y", [SEQ, D], F32, kind="ExternalOutput").ap()
    wbd = nc.dram_tensor("wbd", [2, 8, 128, 2048], BF16, kind="Internal").ap()
    cbd = nc.dram_tensor("cbd", [2, 8, 128, 2048], BF16, kind="Internal").ap()
    wtd = nc.dram_tensor("wtd", [2, 8, 128, 1024], BF16, kind="Internal").ap()

    with contextlib.ExitStack() as st:
        def sb(name, shape, dt=F32):
            return st.enter_context(nc.sbuf_tensor("sb_" + name, list(shape), dt))

        xT = sb("xT", [128, 8, 1024])
        hb = sb("hb", [128, 8, 1024], BF16)
        hprev = sb("hprev", [128, 2, 8, 8], BF16)
        scr = sb("scr", [128, 16448])
        wA = [sb(f"wA{i}", [128, 4096], BF16) for i in range(3)]
        wB = [sb(f"wB{i}", [128, 5632], BF16) for i in range(2)]
        xin = sb("xin", [128, 2, 1024])
        sq = sb("sq", [128, 8, 512], BF16)
        rstd = sb("rstd", [128, 2, 512])
        tmpf = sb("tmpf", [128, 4, 512])
        gel = sb("gel", [128, 2, 1024])
        ident = sb("ident", [128, 128])
        ones_bf = sb("ones_bf", [128, 128], BF16)
        mods = sb("mods", [128, 4, 48])
        gsc = sb("gsc", [128, 4, 2, 8])
        n1g = sb("n1g", [128, 4, 8])
        n2g = sb("n2g", [128, 4, 8])
        fing = sb("fing", [128, 8])
        fing32 = sb("fing32", [128, 8])
        bada = sb("bada", [128, 4, 48])
        cin = sb("cin", [128, 8])
        cact = sb("cact", [128, 8])
        cw = sb("cw", [128, 2, 8, 3])
        dd = sb("dd", [128, 2, 8])
        halo = sb("halo", [128, 2, 8, 8])
        sstate = sb("sstate", [128, 2, 64])
        Ca = sb("Ca", [128, 2, 64])
        Cb = sb("Cb", [128, 2, 64])
        T1 = sb("T1", [128, 64])
        T2 = sb("T2", [128, 64])
        T3 = sb("T3", [128, 64])
        ps = [st.enter_context(nc.psum_tensor(f"ps{i}", [128, 512], F32)) for i in range(8)]
        PSK = [("ps", i) for i in range(8)]

        ringA = Ring(P, "wA", wA)
        ringB = Ring(P, "wB", wB)
        s_small = P.sem("small")
        s_small2 = P.sem("small2")
        s_xin = [P.sem("xin0"), P.sem("xin1")]
        s_out = [P.sem("out0"), P.sem("out1")]
        s_scr = [P.sem("scrd0"), P.sem("scrd1"), P.sem("scrd2")]

        small = [(cin, c_d, "cin"), (n1g, n1g_d, "n1g"), (n2g, n2g_d, "n2g"), (fing, fing_d, "fing"),
                 (bada, bada_d, "bada"), (cw, cw_d, "cw"), (dd, dd_d, "dd")]
        P.dma_multi(sync, [(t[:], d_) for (t, d_, _) in small], s_small, w=[k_ for (_, _, k_) in small])

        P.op(pool, lambda g: g.memset(ident[:], 1.0), w=["ident"])
        P.op(pool, lambda g: g.affine_select(out=ident[:], in_=ident[:], pattern=[[-1, 128]],
                                             compare_op=ALU.is_equal, fill=0.0, base=0, channel_multiplier=1),
             r=["ident"], w=["ident"])
        P.op(pool, lambda g: g.memset(ones_bf[:], 1.0), w=["ones"])
        P.op(pool, lambda g: g.memset(hprev[:], 0.0), w=["hprev"])
        P.op(pool, lambda g: g.memset(halo[:], 0.0), w=["halo"])
        P.op(pool, lambda g: g.memset(sstate[:], 0.0), w=["sstate"])

        P.op(act, lambda a: a.activation(out=cact[:], in_=cin[:], func=AF.Silu), r=["cin"], w=["cact"])
        adab = [xin[:, :, :].rearrange("p a (k n) -> p (a k) n", k=4),
                sq[:, :, :].rearrange("p k n -> p (k n)").bitcast(F32).rearrange("p (k n) -> p k n", k=8)]
        s_ada = [P.sem("ada0"), P.sem("ada1")]

        rowsb = rstd[0:1, :, 0:256]

        def ada_gen():
            nb = 0
            deferred = None
            for li in range(4):
                wv = wada_d[li].rearrange("(k p) n -> p k n", p=128)
                mb = 6 + li % 2

                def flush(d):
                    b_, blk_, mb_ = d
                    for jj in range(2):
                        j = blk_ * 2 + jj
                        P.op(pe, lambda e, b_=b_, jj=jj, j=j, mb_=mb_: e.transpose(
                            out=ps[mb_][:, j:j + 1], in_=rowsb[0:1, b_, jj * 128:(jj + 1) * 128], identity=ident[0:1, 0:1]),
                            r=[("rowsb", b_), "ident"], w=[PSK[mb_]], inc=(jj == 1))
                for blk in range(24):
                    b = nb % 2
                    rb = 4 + nb % 2
                    nb += 1
                    akeys = [[("xin", 0), ("xin", 1)], ["sq"]][b]
                    P.dma(sync, adab[b], wv[:, :, blk * 256:(blk + 1) * 256], s_ada[b], w=akeys)
                    for k in range(8):
                        P.op(pe, lambda e, rb=rb, k=k, b=b: e.matmul(
                            ps[rb][0:1, 0:256], lhsT=cact[:, k:k + 1], rhs=adab[b][:, k, :],
                            start=(k == 0), stop=(k == 7)),
                            r=akeys + ["cact"], w=[PSK[rb]], inc=(k == 7))
                    if deferred is not None:
                        flush(deferred)
                    P.op(act, lambda a_, rb=rb, b=b: a_.activation(out=rowsb[0:1, b, :], in_=ps[rb][0:1, 0:256], func=AF.Copy),
                         r=[PSK[rb]], w=[("rowsb", b)])
                    deferred = (b, blk, mb)
                    yield
                flush(deferred)
                deferred = None
                P.op(dve, lambda v, li=li, mb=mb: v.tensor_tensor(out=mods[:, li, :], in0=ps[mb][:, 0:48], in1=bada[:, li, :],
                                                                  op=ALU.add), r=[PSK[mb], "bada"], w=["mods"])
                for which, ng, ngk in ((0, n1g, "n1g"), (1, n2g, "n2g")):
                    sc = mods[:, li, (1 + 3 * which) * 8:(2 + 3 * which) * 8]
                    P.op(dve, lambda v, li=li, which=which, ng=ng, sc=sc: v.scalar_tensor_tensor(
                        out=gsc[:, li, which, :], in0=sc, scalar=1.0, in1=ng[:, li, :], op0=ALU.add, op1=ALU.mult),
                        r=["mods", ngk], w=["gsc"])
                    P.op(dve, lambda v, li=li, which=which: v.tensor_scalar(
                        out=gsc[:, li, which, :], in0=gsc[:, li, which, :], scalar1=32.0, scalar2=None, op0=ALU.mult),
                        r=["gsc"], w=["gsc"])
            P.op(dve, lambda v: v.tensor_scalar(out=fing32[:], in0=fing[:], scalar1=32.0, scalar2=None, op0=ALU.mult),
                 r=["fing"], w=["fing32"])
        ada_it = ada_gen()
        if not (OPT & 1):
            for _ in ada_it:
                pass
            P.barrier()

        ssm_layers = [l for l in layers if l % 2 == 0]
        if ssm_layers and not (DBG & 2):
            _ssm_prologue(nc, P, st, locals())
        for _ in ada_it:
            pass
        P.barrier()

        env = dict(locals())
        for ti in range(ntiles):
            _load_tile(env, ti)
            for li in layers:
                bar = (lambda: None) if (OPT & 2) else (lambda: P.barrier([pe, act, dve]))
                _norm_mod(env, li, 0)
                bar()
                if li % 2 == 0:
                    _ssm_mixer(env, li, ti)
                else:
                    _conv_mixer(env, li, ti)
                bar()
                _norm_mod(env, li, 1)
                bar()
                _ffn(env, li)
                bar()
            _final_store(env, ti, do_final)
            bar()
        for s in s_out:
            if s.count:
                sync.wait((s, s.count))
        P.emit()
    return nc


def _ssm_prologue(nc, P, st, E):
    pe, act, dve, pool, sync = P.pe, P.act, P.dve, P.pool, P.sync
    scr, hb, ps, PSK, ident = E["scr"], E["hb"], E["ps"], E["PSK"], E["ident"]
    Ca, Cb = E["Ca"], E["Cb"]
    wbd, cbd, wtd = E["wbd"], E["cbd"], E["wtd"]
    s_scr = E["s_scr"]
    s_small = E["s_small"]

    def sb(name, shape, dt=F32):
        return st.enter_context(nc.sbuf_tensor("sp_" + name, list(shape), dt))

    class V:
        def __init__(self, name, ap):
            self.name = name
            self.ap = ap

        def __getitem__(self, k):
            return self.ap[k]

    t32c = {}

    def t32(nm):
        if nm not in t32c:
            t32c[nm] = sb("t_" + nm, [128, 32])
        return t32c[nm]

    xTf = E["xT"][:, :, :].rearrange("p k n -> p (k n)")
    gelf = E["gel"][:, :, :].rearrange("p a n -> p (a n)")
    tmpff = E["tmpf"][:, :, :].rearrange("p a n -> p (a n)")
    CBall = scr[:, 0:8192].bitcast(BF16).rearrange("p (a s r c) -> p a s r c", a=32, s=8, r=2)
    WBall = scr[:, 8192:16384].bitcast(BF16).rearrange("p (k s r c) -> p k s r c", k=8, s=8, r=2)
    WTall = hb[:, :, :].rearrange("p k (t c) -> p k t c", t=8)
    lre = sb("lre", [128, 2, 32]); lim = sb("lim", [128, 2, 32]); lst = sb("lst", [128, 2, 32])
    bre = V("bre", xTf[:, 0:1024].rearrange("p (j n) -> p j n", j=2))
    bim = V("bim", xTf[:, 1024:2048].rearrange("p (j n) -> p j n", j=2))
    crt = V("crt", xTf[:, 2048:3072].rearrange("p (j n) -> p j n", j=2))
    cit = V("cit", xTf[:, 3072:4096].rearrange("p (j n) -> p j n", j=2))
    Bpr = V("Bpr", xTf[:, 4096:5120].rearrange("p (a c) -> p a c", a=32))
    Bpi = V("Bpi", xTf[:, 5120:6144].rearrange("p (a c) -> p a c", a=32))
    Cpr = V("Cpr", xTf[:, 6144:7168].rearrange("p (a c) -> p a c", a=32))
    nCpi = V("nCpi", xTf[:, 7168:8192].rearrange("p (a c) -> p a c", a=32))
    Bbr = V("Bbr", gelf[:, 0:512].rearrange("p (a c) -> p a c", a=32))
    Bbi = V("Bbi", gelf[:, 512:1024].rearrange("p (a c) -> p a c", a=32))
    U1 = V("U1", gelf[:, 1024:1536].rearrange("p (a c) -> p a c", a=32))
    U2 = V("U2", gelf[:, 1536:2048].rearrange("p (a c) -> p a c", a=32))
    nCr = V("nCr", tmpff[:, 0:512].rearrange("p (a c) -> p a c", a=32))
    nCi = V("nCi", tmpff[:, 512:1024].rearrange("p (a c) -> p a c", a=32))
    P.dma_multi(sync, [(t[:], d_) for t, d_ in ((lre, E["lre_d"]), (lim, E["lim_d"]), (lst, E["lst_d"]), (bre, E["bre_d"]),
                                                  (bim, E["bim_d"]), (crt, E["crt_d"]), (cit, E["cit_d"]))], E["s_small2"],
                w=["lre", "lim", "lst", "bre", "bim", "crt", "cit"])

    def tt(out, a, b, op, r, w, q=None):
        return P.op(q or dve, lambda v: v.tensor_tensor(out=out, in0=a, in1=b, op=op), r=r, w=w)

    def bc(a):
        return a.unsqueeze(2).to_broadcast([128, 32, 16])

    P.op(pool, lambda g: g.memset(WTall, 0.0), w=["WTall"])
    tctr = [0]
    def do_layer(j):
        P.op(pool, lambda g: g.memset(scr[:, 0:8192], 0.0), w=["CBall"])
        for t in (Bpr, Bpi, Cpr, nCpi):
            P.op(pool, lambda g, t=t: g.memset(t[:], 0.0), w=[t.name])
        dt_ = t32("dt"); lr = t32("lr"); ph = t32("ph"); lrd = t32("lrd")
        s8 = t32("s8"); c8 = t32("c8"); m8 = t32("m8")
        P.op(act, lambda a: a.activation(out=dt_[:], in_=lst[:, j, :], func=AF.Exp), r=["lst"], w=[dt_.name])
        P.op(dve, lambda v: v.tensor_scalar(out=lr[:], in0=lre[:, j, :], scalar1=-1e-4, scalar2=None, op0=ALU.min),
             r=["lre"], w=[lr.name])
        li_ = lim[:, j, :]
        tt(ph[:], li_, dt_[:], ALU.mult, ["lim", dt_.name], [ph.name])
        tt(lrd[:], lr[:], dt_[:], ALU.mult, [lr.name, dt_.name], [lrd.name])
        P.op(act, lambda a: a.activation(out=s8[:], in_=ph[:], func=AF.Sin, scale=0.125), r=[ph.name], w=[s8.name])
        P.op(act, lambda a: a.activation(out=c8[:], in_=ph[:], func=AF.Sin, scale=-0.125, bias=math.pi / 2),
             r=[ph.name], w=[c8.name])
        P.op(act, lambda a: a.activation(out=m8[:], in_=lrd[:], func=AF.Exp, scale=0.125), r=[lrd.name], w=[m8.name])
        ar = t32("ar"); ai = t32("ai")
        tt(ar[:], m8[:], c8[:], ALU.mult, [m8.name, c8.name], [ar.name])
        tt(ai[:], m8[:], s8[:], ALU.mult, [m8.name, s8.name], [ai.name])
        for it in range(3):
            q1 = t32("q1"); q2 = t32("q2"); q3 = t32("q3"); nr = t32(f"nr{it}"); ni = t32(f"ni{it}")
            tt(q1[:], ar[:], ar[:], ALU.mult, [ar.name], [q1.name])
            tt(q2[:], ai[:], ai[:], ALU.mult, [ai.name], [q2.name])
            tt(q3[:], ar[:], ai[:], ALU.mult, [ar.name, ai.name], [q3.name])
            tt(nr[:], q1[:], q2[:], ALU.subtract, [q1.name, q2.name], [nr.name])
            tt(ni[:], q3[:], q3[:], ALU.add, [q3.name], [ni.name])
            ar, ai = nr, ni
        pw = [None] * 9
        one = t32("one"); zero = t32("zero")
        P.op(pool, lambda g: g.memset(one[:], 1.0), w=[one.name])
        P.op(pool, lambda g: g.memset(zero[:], 0.0), w=[zero.name])
        pw[0] = (one, zero)
        pw[1] = (ar, ai)

        def cmul(xr, xi, yr, yi, n):
            a1 = t32("a1"); a2 = t32("a2"); a3 = t32("a3"); a4 = t32("a4"); zr = t32(f"zr{n}"); zi = t32(f"zi{n}")
            tt(a1[:], xr[:], yr[:], ALU.mult, [xr.name, yr.name], [a1.name])
            tt(a2[:], xi[:], yi[:], ALU.mult, [xi.name, yi.name], [a2.name])
            tt(a3[:], xr[:], yi[:], ALU.mult, [xr.name, yi.name], [a3.name])
            tt(a4[:], xi[:], yr[:], ALU.mult, [xi.name, yr.name], [a4.name])
            tt(zr[:], a1[:], a2[:], ALU.subtract, [a1.name, a2.name], [zr.name])
            tt(zi[:], a3[:], a4[:], ALU.add, [a3.name, a4.name], [zi.name])
            return zr, zi
        for n in range(2, 9):
            pw[n] = cmul(pw[n - 1][0], pw[n - 1][1], ar, ai, n)
        a8r, a8i = pw[8]
        P.op(dve, lambda v: v.tensor_copy(out=Ca[:, j, 0:32], in_=a8r[:]), r=[a8r.name], w=["Ca"])
        P.op(dve, lambda v: v.tensor_copy(out=Ca[:, j, 32:64], in_=a8r[:]), r=[a8r.name], w=["Ca"])
        P.op(dve, lambda v: v.tensor_copy(out=Cb[:, j, 32:64], in_=a8i[:]), r=[a8i.name], w=["Cb"])
        P.op(dve, lambda v: v.tensor_scalar(out=Cb[:, j, 0:32], in0=a8i[:], scalar1=-1.0, scalar2=None, op0=ALU.mult),
             r=[a8i.name], w=["Cb"])
        den = t32("den"); d2 = t32("d2"); rden = t32("rden"); am1 = t32("am1")
        tt(den[:], lr[:], lr[:], ALU.mult, [lr.name], [den.name])
        tt(d2[:], li_, li_, ALU.mult, ["lim"], [d2.name])
        tt(den[:], den[:], d2[:], ALU.add, [den.name, d2.name], [den.name])
        P.op(dve, lambda v: v.reciprocal(out=rden[:], in_=den[:]), r=[den.name], w=[rden.name])
        P.op(dve, lambda v: v.tensor_scalar(out=am1[:], in0=ar[:], scalar1=-1.0, scalar2=None, op0=ALU.add),
             r=[ar.name], w=[am1.name])
        e1 = t32("e1"); e2 = t32("e2"); qr = t32("qr"); qi = t32("qi")
        tt(e1[:], am1[:], lr[:], ALU.mult, [am1.name, lr.name], [e1.name])
        tt(e2[:], ai[:], li_, ALU.mult, [ai.name, "lim"], [e2.name])
        tt(e1[:], e1[:], e2[:], ALU.add, [e1.name, e2.name], [e1.name])
        tt(qr[:], e1[:], rden[:], ALU.mult, [e1.name, rden.name], [qr.name])
        e3 = t32("e3"); e4 = t32("e4")
        tt(e3[:], ai[:], lr[:], ALU.mult, [ai.name, lr.name], [e3.name])
        tt(e4[:], am1[:], li_, ALU.mult, [am1.name, "lim"], [e4.name])
        tt(e3[:], e3[:], e4[:], ALU.subtract, [e3.name, e4.name], [e3.name])
        tt(qi[:], e3[:], rden[:], ALU.mult, [e3.name, rden.name], [qi.name])
        B_r = bre[:, j, :].rearrange("p (a h) -> p a h", h=16)
        B_i = bim[:, j, :].rearrange("p (a h) -> p a h", h=16)
        tt(U1[:], B_r, bc(qr[:]), ALU.mult, ["bre", qr.name], ["U1"])
        tt(U2[:], B_i, bc(qi[:]), ALU.mult, ["bim", qi.name], ["U2"])
        tt(Bbr[:], U1[:], U2[:], ALU.subtract, ["U1", "U2"], ["Bbr"])
        tt(U1[:], B_i, bc(qr[:]), ALU.mult, ["bim", qr.name], ["U1"])
        tt(U2[:], B_r, bc(qi[:]), ALU.mult, ["bre", qi.name], ["U2"])
        tt(Bbi[:], U1[:], U2[:], ALU.add, ["U1", "U2"], ["Bbi"])
        C_r = crt[:, j, :].rearrange("p (a h) -> p a h", h=16)
        C_i = cit[:, j, :].rearrange("p (a h) -> p a h", h=16)
        for (lo, hi, c0) in ((0, 64, 0), (64, 128, 16)):
            P.op(dve, lambda v, lo=lo, hi=hi, c0=c0: v.tensor_copy(out=Cpr[lo:hi, :, c0:c0 + 16], in_=C_r[lo:hi]),
                 r=["crt"], w=["Cpr"])
            P.op(dve, lambda v, lo=lo, hi=hi, c0=c0: v.tensor_scalar(
                out=nCpi[lo:hi, :, c0:c0 + 16], in0=C_i[lo:hi], scalar1=-1.0, scalar2=None, op0=ALU.mult),
                r=["cit"], w=["nCpi"])
        P.op(dve, lambda v: v.tensor_scalar(out=nCr[:], in0=C_r, scalar1=-1.0, scalar2=None, op0=ALU.mult),
             r=["crt"], w=["nCr"])
        P.op(dve, lambda v: v.tensor_scalar(out=nCi[:], in0=C_i, scalar1=-1.0, scalar2=None, op0=ALU.mult),
             r=["cit"], w=["nCi"])
        def do_s(s):
            pr, pi_ = pw[7 - s]
            tt(U1[:], Bbr[:], bc(pr[:]), ALU.mult, ["Bbr", pr.name], ["U1"])
            tt(U2[:], Bbi[:], bc(pi_[:]), ALU.mult, ["Bbi", pi_.name], ["U2"])
            for (lo, hi, c0) in ((0, 64, 0), (64, 128, 16)):
                tt(Bpr[lo:hi, :, c0:c0 + 16], U1[lo:hi], U2[lo:hi], ALU.subtract, ["U1", "U2"], ["Bpr"])
            tt(U1[:], Bbi[:], bc(pr[:]), ALU.mult, ["Bbi", pr.name], ["U1"])
            tt(U2[:], Bbr[:], bc(pi_[:]), ALU.mult, ["Bbr", pi_.name], ["U2"])
            for (lo, hi, c0) in ((0, 64, 0), (64, 128, 16)):
                tt(Bpi[lo:hi, :, c0:c0 + 16], U1[lo:hi], U2[lo:hi], ALU.add, ["U1", "U2"], ["Bpi"])
            tau = 7 - s
            for k in range(8 if not (DBG & 4) else 0):
                bk = tctr[0] % 4
                tctr[0] += 1
                for ri, Bp in enumerate((Bpr, Bpi)):
                    src = Bp[:, 4 * k:4 * k + 4, :].rearrange("p a c -> p (a c)")
                    P.op(pe, lambda e, src=src, bk=bk, ri=ri: e.transpose(
                        out=ps[bk][:, ri * 128:(ri + 1) * 128], in_=src, identity=ident[:]),
                        r=[Bp.name, "ident"], w=[PSK[bk]], inc=False)
                srcr = Bpr[:, 4 * k:4 * k + 4, :].rearrange("p a c -> p (a c)")
                srci = Bpi[:, 4 * k:4 * k + 4, :].rearrange("p a c -> p (a c)")
                cr = Cpr[:, 4 * k:4 * k + 4, :].rearrange("p a c -> p (a c)")
                ci = nCpi[:, 4 * k:4 * k + 4, :].rearrange("p a c -> p (a c)")
                P.op(pe, lambda e, bk=bk, srcr=srcr, cr=cr: e.matmul(ps[bk][:, 256:384], lhsT=srcr, rhs=cr,
                                                                     start=True, stop=False),
                     r=["Bpr", "Cpr"], w=[PSK[bk]], inc=False)
                P.op(pe, lambda e, bk=bk, srci=srci, ci=ci: e.matmul(ps[bk][:, 256:384], lhsT=srci, rhs=ci,
                                                                     start=False, stop=True),
                     r=["Bpi", "nCpi"], w=[PSK[bk]], inc=True)
                P.op(act, lambda a, bk=bk, k=k, s=s: a.activation(
                    out=WBall[:, k, s, :, :], in_=ps[bk][:, 0:256].rearrange("p (r c) -> p r c", r=2), func=AF.Copy),
                    r=[PSK[bk]], w=["WBall"])
                for jq in range(4):
                    P.op(dve, lambda v, bk=bk, k=k, tau=tau, jq=jq: v.tensor_copy(
                        out=WTall[32 * jq:32 * jq + 32, k, tau, 32 * jq:32 * jq + 32],
                        in_=ps[bk][32 * jq:32 * jq + 32, 256 + 32 * jq:256 + 32 * jq + 32]),
                        r=[], w=["WTall", PSK[bk]])
            qr_, qi_ = pw[s + 1]
            tt(U1[:], C_r, bc(qr_[:]), ALU.mult, ["crt", qr_.name], ["U1"])
            tt(U2[:], C_i, bc(qi_[:]), ALU.mult, ["cit", qi_.name], ["U2"])
            for (lo, hi, c0) in ((0, 64, 0), (64, 128, 16)):
                tt(CBall[lo:hi, :, s, 0, c0:c0 + 16], U1[lo:hi], U2[lo:hi], ALU.subtract, ["U1", "U2"], ["CBall"])
            tt(U1[:], nCr[:], bc(qi_[:]), ALU.mult, ["nCr", qi_.name], ["U1"])
            tt(U2[:], nCi[:], bc(qr_[:]), ALU.mult, ["nCi", qr_.name], ["U2"])
            for (lo, hi, c0) in ((0, 64, 0), (64, 128, 16)):
                tt(CBall[lo:hi, :, s, 1, c0:c0 + 16], U1[lo:hi], U2[lo:hi], ALU.add, ["U1", "U2"], ["CBall"])
        for s_ in range(8):
            do_s(s_)
            for _ in range(6):
                next(E["ada_it"], None)
        if DBG & 8:
            return
        P.dma(sync, wbd[j].rearrange("k p n -> p k n"), scr[:, 8192:16384].bitcast(BF16).rearrange("p (k n) -> p k n", k=8),
              s_scr[0], r=["WBall"], w=[("wbd", j)])
        P.dma(sync, cbd[j].rearrange("k p n -> p k n"), scr[:, 0:8192].bitcast(BF16).rearrange("p (k n) -> p k n", k=8),
              s_scr[1], r=["CBall"], w=[("cbd", j)])
        P.dma(sync, wtd[j].rearrange("k p n -> p k n"), hb[:, :, :], s_scr[2], r=["WTall"], w=[("wtd", j)])
    for j_ in range(2):
        if 2 * j_ in E["layers"]:
            do_layer(j_)
    for sm in s_scr:
        if sm.count:
            for q in (pe, act, dve, pool, sync):
                q.wait((sm, sm.count))


def _load_tile(E, ti):
    P = E["P"]; pe, act, dve, sync = P.pe, P.act, P.dve, P.sync
    xT, xin, ps, PSK, ident = E["xT"], E["xin"], E["ps"], E["PSK"], E["ident"]
    xTv = xT[:, :, :].rearrange("p k (s c) -> p k s c", s=8)
    for blk in range(8):
        b = blk % 2
        r0 = ti * TT + blk * 128
        P.dma(sync, xin[:, b, :], E["x_d"][r0:r0 + 128, :], E["s_xin"][b], w=[("xin", b)])
        for half in range(2):
            bk = (blk * 2 + half) % 4
            for kk in range(4):
                k = half * 4 + kk
                P.op(pe, lambda e, b=b, bk=bk, k=k, kk=kk: e.transpose(
                    out=ps[bk][:, kk * 128:(kk + 1) * 128], in_=xin[:, b, k * 128:(k + 1) * 128], identity=ident[:]),
                    r=[("xin", b), "ident"], w=[PSK[bk]], inc=(kk == 3))
            q = act if half == 0 else dve
            src = ps[bk][:, :].rearrange("p (k jc s) -> p k s jc", k=4, jc=16, s=8)
            dst = xTv[:, half * 4:half * 4 + 4, :, 16 * blk:16 * blk + 16]
            if q is act:
                P.op(q, lambda a, src=src, dst=dst: a.activation(out=dst, in_=src, func=AF.Copy), r=[PSK[bk]], w=["xT"])
            else:
                P.op(q, lambda v, src=src, dst=dst: v.tensor_copy(out=dst, in_=src), r=[PSK[bk]], w=["xT"])


def _final_store(E, ti, do_final):
    P = E["P"]; pe, act, dve, sync = P.pe, P.act, P.dve, P.sync
    xT, xin, ps, PSK, ident, scr = E["xT"], E["xin"], E["ps"], E["PSK"], E["ident"], E["scr"]
    if do_final:
        _norm_mod(E, None, 2)
        src_all = scr[:, 0:8192].rearrange("p (k s c) -> p k s c", k=8, s=8)
        skey = "fout"
    else:
        src_all = xT[:, :, :].rearrange("p k (s c) -> p k s c", s=8)
        skey = "xT"
    for blk in range(8):
        b = blk % 2
        for half in range(2):
            a = (blk * 2 + half) % 2
            bk = (blk * 2 + half) % 4
            stg = scr[:, 8192 + a * 512: 8192 + (a + 1) * 512]
            src = src_all[:, half * 4:half * 4 + 4, :, 16 * blk:16 * blk + 16]
            dstv = stg.rearrange("p (k jc s) -> p k s jc", k=4, jc=16, s=8)
            if half == 0:
                P.op(act, lambda e, src=src, dstv=dstv: e.activation(out=dstv, in_=src, func=AF.Copy),
                     r=[skey], w=[("stg", a)])
            else:
                P.op(dve, lambda e, src=src, dstv=dstv: e.tensor_copy(out=dstv, in_=src), r=[skey], w=[("stg", a)])
            for kk in range(4):
                P.op(pe, lambda e, stg=stg, bk=bk, kk=kk: e.transpose(
                    out=ps[bk][:, kk * 128:(kk + 1) * 128], in_=stg[:, kk * 128:(kk + 1) * 128], identity=ident[:]),
                    r=[("stg", a), "ident"], w=[PSK[bk]], inc=(kk == 3))
            dsto = xin[:, b, half * 512:(half + 1) * 512]
            if half == 0:
                P.op(dve, lambda e, bk=bk, dsto=dsto: e.tensor_copy(out=dsto, in_=ps[bk][:, :]),
                     r=[PSK[bk]], w=[("xin", b)])
            else:
                P.op(act, lambda e, bk=bk, dsto=dsto: e.activation(out=dsto, in_=ps[bk][:, :], func=AF.Copy),
                     r=[PSK[bk]], w=[("xin", b)])
        r0 = ti * TT + blk * 128
        P.dma(sync, E["y_d"][r0:r0 + 128, :], xin[:, b, :], E["s_out"][b], r=[("xin", b)])


def _norm_mod(E, li, which):
    P = E["P"]; pe, act, dve = P.pe, P.act, P.dve
    xT, hb, sq, rstd, tmpf, ps, PSK = E["xT"], E["hb"], E["sq"], E["rstd"], E["tmpf"], E["ps"], E["PSK"]
    ones_bf, gsc, mods, fing32, scr = E["ones_bf"], E["gsc"], E["mods"], E["fing32"], E["scr"]
    fout = scr[:, 0:8192].rearrange("p (k n) -> p k n", k=8)
    for tb in range(2):
        sl = slice(tb * 512, (tb + 1) * 512)
        bk = 6 + tb
        P.op(act, lambda a, sl=sl: a.activation(out=sq[:], in_=xT[:, :, sl], func=AF.Square), r=["xT"], w=["sq"])
        for k in range(8):
            P.op(pe, lambda e, k=k, bk=bk: e.matmul(ps[bk][:, :], lhsT=ones_bf[:], rhs=sq[:, k, :],
                                                    start=(k == 0), stop=(k == 7)),
                 r=["sq", "ones"], w=[PSK[bk]], inc=(k == 7))
        P.op(act, lambda a, tb=tb, bk=bk: a.activation(out=rstd[:, tb, :], in_=ps[bk][:, :], func=AF.Sqrt,
                                                       bias=D * EPS, scale=1.0),
             r=[PSK[bk]], w=[("rstd", tb)])
        P.op(dve, lambda v, tb=tb: v.reciprocal(out=rstd[:, tb, :], in_=rstd[:, tb, :]),
             r=[("rstd", tb)], w=[("rstd", tb)])
        for k in range(8):
            if which == 2:
                P.op(dve, lambda v, k=k, sl=sl, tb=tb: v.scalar_tensor_tensor(
                    out=fout[:, k, sl], in0=xT[:, k, sl], scalar=fing32[:, k:k + 1], in1=rstd[:, tb, :],
                    op0=ALU.mult, op1=ALU.mult), r=["xT", ("rstd", tb), "fing32"], w=["fout"])
                continue
            tbuf = k % 4
            P.op(dve, lambda v, k=k, sl=sl, tb=tb, tbuf=tbuf: v.scalar_tensor_tensor(
                out=tmpf[:, tbuf, :], in0=xT[:, k, sl], scalar=gsc[:, li, which, k:k + 1], in1=rstd[:, tb, :],
                op0=ALU.mult, op1=ALU.mult), r=["xT", ("rstd", tb), "gsc"], w=[("tmpf", tbuf)])
            shc = (3 * which) * 8 + k
            P.op(act, lambda a, k=k, sl=sl, tbuf=tbuf, shc=shc: a.activation(
                out=hb[:, k, sl], in_=tmpf[:, tbuf, :], func=AF.Identity, bias=mods[:, li, shc:shc + 1], scale=1.0),
                r=[("tmpf", tbuf), "mods"], w=["hb"])


def _ffn(E, li):
    P = E["P"]; pe, act, dve, pool = P.pe, P.act, P.dve, P.pool
    xT, hb, scr, tmpf, ps, PSK, mods = E["xT"], E["hb"], E["scr"], E["tmpf"], E["ps"], E["PSK"], E["mods"]
    ringA, ringB = E["ringA"], E["ringB"]
    actb = scr[:, 0:11264].bitcast(BF16).rearrange("p (j n) -> p j n", j=FC)
    wi = E["wfi_d"][li].rearrange("(k p) n -> p k n", p=128)
    wo = E["wfo_d"][li].rearrange("(j p) n -> p j n", p=128)
    cnt = 0
    for grp in range(11):
        buf, sem, key = ringA.next()
        bv = buf[:, :].rearrange("p (g k n) -> p g k n", g=2, k=8)
        P.dma_multi(pool, [(bv[:, 0], wi[:, :, grp * 256:(grp + 1) * 256]),
                           (bv[:, 1], wi[:, :, FF + grp * 256:FF + (grp + 1) * 256])], sem, w=[key])
        for jj in range(2):
            j = grp * 2 + jj
            for tb in range(2):
                sl = slice(tb * 512, (tb + 1) * 512)
                gb = cnt % 2
                ub = 2 + cnt % 2
                cnt += 1
                for k in range(8):
                    P.op(pe, lambda e, bv=bv, jj=jj, k=k, sl=sl, gb=gb: e.matmul(
                        ps[gb][:, :], lhsT=bv[:, 0, k, jj * 128:(jj + 1) * 128], rhs=hb[:, k, sl],
                        start=(k == 0), stop=(k == 7)), r=[key, "hb"], w=[PSK[gb]], inc=(k == 7))
                for k in range(8):
                    P.op(pe, lambda e, bv=bv, jj=jj, k=k, sl=sl, ub=ub: e.matmul(
                        ps[ub][:, :], lhsT=bv[:, 1, k, jj * 128:(jj + 1) * 128], rhs=hb[:, k, sl],
                        start=(k == 0), stop=(k == 7)), r=[key, "hb"], w=[PSK[ub]], inc=(k == 7))
                tbuf = cnt % 4
                P.op(act, lambda a, gb=gb, tbuf=tbuf: a.activation(out=tmpf[:, tbuf, :], in_=ps[gb][:, :], func=AF.Silu),
                     r=[PSK[gb]], w=[("tmpf", tbuf)])
                P.op(dve, lambda v, ub=ub, tbuf=tbuf, j=j, sl=sl: v.tensor_tensor(
                    out=actb[:, j, sl], in0=ps[ub][:, :], in1=tmpf[:, tbuf, :], op=ALU.mult),
                    r=[PSK[ub], ("tmpf", tbuf)], w=[("actb", j)])
    cnt = 0
    for mg in range(4):
        buf, sem, key = ringB.next()
        bv = buf[:, :].rearrange("p (j n) -> p j n", j=FC)
        P.dma(pool, bv, wo[:, :, mg * 256:(mg + 1) * 256], sem, w=[key])
        for mm in range(2):
            m = mg * 2 + mm
            for tb in range(2):
                sl = slice(tb * 512, (tb + 1) * 512)
                ob = 4 + cnt % 2
                cnt += 1
                for j in range(FC):
                    P.op(pe, lambda e, bv=bv, mm=mm, j=j, sl=sl, ob=ob: e.matmul(
                        ps[ob][:, :], lhsT=bv[:, j, mm * 128:(mm + 1) * 128], rhs=actb[:, j, sl],
                        start=(j == 0), stop=(j == FC - 1)), r=[key, ("actb", j)], w=[PSK[ob]], inc=(j == FC - 1))
                gc = 5 * 8 + m
                P.op(dve, lambda v, ob=ob, m=m, sl=sl, gc=gc: v.scalar_tensor_tensor(
                    out=xT[:, m, sl], in0=ps[ob][:, :], scalar=mods[:, li, gc:gc + 1], in1=xT[:, m, sl],
                    op0=ALU.mult, op1=ALU.add), r=[PSK[ob], "mods", "xT"], w=["xT"])


def _conv_mixer(E, li, ti):
    P = E["P"]; pe, act, dve, pool = P.pe, P.act, P.dve, P.pool
    xT, hb, scr, tmpf, gel, ps, PSK, mods = E["xT"], E["hb"], E["scr"], E["tmpf"], E["gel"], E["ps"], E["PSK"], E["mods"]
    cw, halo = E["cw"], E["halo"]
    ringA, ringB = E["ringA"], E["ringB"]
    jl = li // 2
    mbuf = scr[:, 0:4096].bitcast(BF16).rearrange("p (k n) -> p k n", k=8)
    cvb = [scr[:, 4096 + i * 1032: 4096 + (i + 1) * 1032].rearrange("p (s c) -> p s c", s=8) for i in range(2)]
    accb = [scr[:, 6400 + i * 1024: 6400 + (i + 1) * 1024] for i in range(2)]
    wi = E["cwi_d"][jl].rearrange("(k p) n -> p k n", p=128)
    wo = E["cwo_d"][jl].rearrange("(k p) n -> p k n", p=128)
    for m in range(8):
        buf, sem, key = ringA.next()
        bv = buf[:, 0:3072].rearrange("p (g k n) -> p g k n", g=3, k=8)
        P.dma_multi(pool, [(bv[:, g], wi[:, :, g * D + m * 128: g * D + (m + 1) * 128]) for g in range(3)], sem, w=[key])
        cv = cvb[m % 2]
        acc = accb[m % 2]
        bgb = gel[:, m % 2, :]
        ck, ak, bk_ = ("cvb", m % 2), ("acc", m % 2), ("gel", m % 2)
        for tb in range(2):
            sl = slice(tb * 512, (tb + 1) * 512)
            base = 3 * ((2 * m + tb) % 2)
            for g in range(3):
                for k in range(8):
                    P.op(pe, lambda e, bv=bv, g=g, k=k, sl=sl, base=base: e.matmul(
                        ps[base + g][:, :], lhsT=bv[:, g, k, :], rhs=hb[:, k, sl], start=(k == 0), stop=(k == 7)),
                        r=[key, "hb"], w=[PSK[base + g]], inc=(k == 7))
            tbuf = (2 * m + tb) % 4
            P.op(act, lambda a, base=base, sl=sl, bgb=bgb: a.activation(out=bgb[:, sl], in_=ps[base][:, :], func=AF.Copy),
                 r=[PSK[base]], w=[bk_])
            P.op(act, lambda a, base=base, tbuf=tbuf: a.activation(out=tmpf[:, tbuf, :], in_=ps[base + 1][:, :], func=AF.Copy),
                 r=[PSK[base + 1]], w=[("tmpf", tbuf)])
            P.op(dve, lambda v, base=base, tbuf=tbuf, cv=cv, tb=tb: v.tensor_tensor(
                out=cv[:, 4 * tb:4 * tb + 4, 1:129], in0=ps[base + 2][:, :].rearrange("p (s c) -> p s c", s=4),
                in1=tmpf[:, tbuf, :].rearrange("p (s c) -> p s c", s=4), op=ALU.mult),
                r=[PSK[base + 2], ("tmpf", tbuf)], w=[ck])
        P.op(dve, lambda v, cv=cv, m=m: v.tensor_copy(out=cv[:, :, 0], in_=halo[:, jl, m, :]), r=["halo", ck], w=[ck])
        P.op(dve, lambda v, cv=cv, m=m: v.tensor_copy(out=halo[:, jl, m, :], in_=cv[:, :, 128]), r=[ck, "halo"], w=["halo"])
        a3 = acc.rearrange("p (s c) -> p s c", s=8)
        w0 = cw[:, jl, m, 0:1]; w1 = cw[:, jl, m, 1:2]; w2 = cw[:, jl, m, 2:3]
        P.op(dve, lambda v, a3=a3, cv=cv, w2=w2: v.tensor_scalar(out=a3, in0=cv[:, :, 1:129], scalar1=w2, scalar2=None,
                                                                op0=ALU.mult), r=[ck, "cw"], w=[ak])
        for (o_, i_, wv) in ((a3[:, 1:8, :], cv[:, 0:7, 1:129], w1), (a3[:, 0:1, :], cv[:, 7:8, 0:128], w1),
                             (a3[:, 2:8, :], cv[:, 0:6, 1:129], w0), (a3[:, 0:2, :], cv[:, 6:8, 0:128], w0)):
            P.op(dve, lambda v, o_=o_, i_=i_, wv=wv: v.scalar_tensor_tensor(
                out=o_, in0=i_, scalar=wv, in1=o_, op0=ALU.mult, op1=ALU.add), r=[ck, ak, "cw"], w=[ak])
        P.op(dve, lambda v, acc=acc, bgb=bgb, m=m: v.tensor_tensor(out=mbuf[:, m, :], in0=acc, in1=bgb, op=ALU.mult),
             r=[ak, bk_], w=[("mbuf", m)])
    cnt = 0
    for mg in range(4):
        buf, sem, key = ringB.next()
        bv = buf[:, 0:2048].rearrange("p (k n) -> p k n", k=8)
        P.dma(pool, bv, wo[:, :, mg * 256:(mg + 1) * 256], sem, w=[key])
        for mm in range(2):
            m = mg * 2 + mm
            for tb in range(2):
                sl = slice(tb * 512, (tb + 1) * 512)
                ob = 6 + cnt % 2
                cnt += 1
                for k in range(8):
                    P.op(pe, lambda e, bv=bv, mm=mm, k=k, sl=sl, ob=ob: e.matmul(
                        ps[ob][:, :], lhsT=bv[:, k, mm * 128:(mm + 1) * 128], rhs=mbuf[:, k, sl],
                        start=(k == 0), stop=(k == 7)), r=[key, ("mbuf", k)], w=[PSK[ob]], inc=(k == 7))
                gc = 2 * 8 + m
                P.op(dve, lambda v, ob=ob, m=m, sl=sl, gc=gc: v.scalar_tensor_tensor(
                    out=xT[:, m, sl], in0=ps[ob][:, :], scalar=mods[:, li, gc:gc + 1], in1=xT[:, m, sl],
                    op0=ALU.mult, op1=ALU.add), r=[PSK[ob], "mods", "xT"], w=["xT"])


def _ssm_mixer(E, li, ti):
    P = E["P"]; pe, act, dve, pool, sync = P.pe, P.act, P.dve, P.pool, P.sync
    xT, hb, scr, tmpf, gel, ps, PSK, mods = E["xT"], E["hb"], E["scr"], E["tmpf"], E["gel"], E["ps"], E["PSK"], E["mods"]
    hprev, sstate, Ca, Cb, dd = E["hprev"], E["sstate"], E["Ca"], E["Cb"], E["dd"]
    T1, T2, T3 = E["T1"], E["T2"], E["T3"]
    ringA = E["ringA"]
    jl = li // 2
    XS = scr[:, 0:8256].rearrange("p (r c) -> p r c", r=64)
    XSb = scr[:, 8256:12352].bitcast(BF16).rearrange("p (r c) -> p r c", r=64)
    ybuf = scr[:, 12352:16448].bitcast(BF16).rearrange("p (k n) -> p k n", k=8)
    hv = hb[:, :, :].rearrange("p k (s c) -> p k s c", s=8)
    P.op(dve, lambda v: v.tensor_copy(out=XS[:, :, 0], in_=sstate[:, jl, :]), r=["sstate"], w=["XS"])
    for k in range(8):
        buf, sem, key = ringA.next(hw=True)
        bv = buf[:, 0:2048].rearrange("p (s r c) -> p s r c", s=8, r=2)
        P.dma(sync, buf[:, 0:2048], E["wbd"][jl, k], sem, r=[("wbd", jl)], w=[key])
        for ri in range(2):
            for s in range(8):
                for jq in range(4):
                    p0 = 32 * jq
                    P.op(pe, lambda e, bv=bv, p0=p0, ri=ri, s=s, k=k, jq=jq: e.matmul(
                        ps[ri * 4 + jq][:, 0:128], lhsT=bv[p0:p0 + 32, s, ri, :],
                        rhs=hv[p0:p0 + 32, k, s, :], start=(s == 0), stop=(s == 7), tile_position=(p0, 0)),
                        r=[key, "hb"], w=[PSK[ri * 4 + jq]], inc=(s == 7 and jq == 3))
            for jq in range(4):
                pi_ = 4 * k + jq
                dst = XS[:, ri * 32 + pi_, 1:129]
                src = ps[ri * 4 + jq][:, 0:128]
                if jq % 2 == 0:
                    P.op(act, lambda a, dst=dst, src=src: a.activation(out=dst, in_=src, func=AF.Copy),
                         r=[PSK[ri * 4 + jq]], w=["XS"])
                else:
                    P.op(dve, lambda v, dst=dst, src=src: v.tensor_copy(out=dst, in_=src), r=[PSK[ri * 4 + jq]], w=["XS"])
    def mm(out, lhsT, rhs, rk, start=False, bank=0, inc=False, tp=None):
        if tp is None:
            P.op(pe, lambda e: e.matmul(out, lhsT=lhsT, rhs=rhs, start=start, stop=inc, skip_group_check=True),
                 r=rk, w=[PSK[bank]], inc=inc)
        else:
            P.op(pe, lambda e: e.matmul(out, lhsT=lhsT, rhs=rhs, start=start, stop=inc, skip_group_check=True,
                                        tile_position=tp),
                 r=rk, w=[PSK[bank]], inc=inc)

    def near(k, wtv, yb, rk):
        for hf in range(2):
            mm(ps[yb[hf]][:, :], wtv[:, 0, :], hb[:, k, hf * 512:(hf + 1) * 512], rk, start=True, bank=yb[hf])
        for tau in range(1, 8):
            lo = tau * 128
            if lo < 512:
                mm(ps[yb[0]][:, lo:512], wtv[:, tau, :], hb[:, k, 0:512 - lo], rk, bank=yb[0])
            lo2 = max(512, lo)
            mm(ps[yb[1]][:, lo2 - 512:512], wtv[:, tau, :], hb[:, k, lo2 - lo:1024 - lo], rk, bank=yb[1])
            for s in range(tau):
                hf, sc = s // 4, s % 4
                src_s = s + 8 - tau
                mm(ps[yb[hf]][:, sc * 128 + 1:sc * 128 + 128], wtv[:, tau, :],
                   hb[:, k, src_s * 128:src_s * 128 + 127], rk, bank=yb[hf])
                mm(ps[yb[hf]][:, sc * 128:sc * 128 + 1], wtv[:, tau, :], hprev[:, jl, k, src_s:src_s + 1], rk, bank=yb[hf])

    def far(k, cbv, yb, rk):
        for s in range(8):
            hf, sc = s // 4, s % 4
            for ri in range(2):
                for jq in range(4):
                    pi_ = 4 * k + jq
                    last = (jq == 3 and s in (3, 7) and ri == 1)
                    mm(ps[yb[hf]][32 * jq:32 * jq + 32, sc * 128:(sc + 1) * 128], cbv[:, jq, s, ri, :],
                       XSb[:, ri * 32 + pi_, :], rk, bank=yb[hf], inc=last, tp=(0, 32 * jq))

    def evac(k, yb):
        for hf in range(2):
            sl = slice(hf * 512, (hf + 1) * 512)
            g_ = gel[:, hf, 0:512]
            P.op(dve, lambda v, hf=hf, sl=sl, k=k, g_=g_, yb=yb: v.scalar_tensor_tensor(
                out=g_, in0=hb[:, k, sl], scalar=dd[:, jl, k:k + 1], in1=ps[yb[hf]][:, :],
                op0=ALU.mult, op1=ALU.add), r=["hb", "dd", PSK[yb[hf]]], w=[("gel", hf)])
            P.op(act, lambda a, sl=sl, k=k, g_=g_: a.activation(out=ybuf[:, k, sl], in_=g_, func=AF.Gelu_apprx_tanh),
                 r=[("gel", hf)], w=[("ybuf", k)])

    bufW, semW, keyW = ringA.next(hw=True)
    P.dma(sync, bufW[:, :].rearrange("p (k n) -> p k n", k=4), E["wtd"][jl, 0:4].rearrange("k p n -> p k n"), semW,
          r=[("wtd", jl)], w=[keyW])
    for k in range(4):
        wtv = bufW[:, k * 1024:(k + 1) * 1024].rearrange("p (t c) -> p t c", t=8)
        near(k, wtv, [2 * k, 2 * k + 1], [keyW, "hb", "hprev"])
    xs_t = XS.tensor
    pstep = XS.ap[0][0]
    for c in range(128 if not (DBG & 1) else 0):
        cur = XS[:, :, c]
        nxt = XS[:, :, c + 1]
        swp = bass.AP(xs_t, XS.offset + 32 * 129 + c, [[pstep, 128], [-32 * 129, 2], [129, 32]])
        P.op(dve, lambda v, cur=cur: v.tensor_tensor(out=T1[:], in0=cur, in1=Ca[:, jl, :], op=ALU.mult),
             r=["XS", "Ca"], w=["T1"])
        P.op(dve, lambda v, swp=swp: v.tensor_tensor(out=T2[:].rearrange("p (r c) -> p r c", r=2), in0=swp,
                                                     in1=Cb[:, jl, :].rearrange("p (r c) -> p r c", r=2), op=ALU.mult),
             r=["XS", "Cb"], w=["T2"])
        P.op(dve, lambda v: v.tensor_tensor(out=T3[:], in0=T1[:], in1=T2[:], op=ALU.add), r=["T1", "T2"], w=["T3"])
        P.op(dve, lambda v, nxt=nxt: v.tensor_tensor(out=nxt, in0=T3[:], in1=nxt, op=ALU.add), r=["T3", "XS"], w=["XS"])
    P.op(dve, lambda v: v.tensor_copy(out=sstate[:, jl, :], in_=XS[:, :, 128]), r=["XS"], w=["sstate"])
    for q4 in range(4):
        P.op(act, lambda a, q4=q4: a.activation(out=XSb[:, 16 * q4:16 * q4 + 16, :], in_=XS[:, 16 * q4:16 * q4 + 16, 0:128],
                                                 func=AF.Copy), r=["XS"], w=["XSb"])
    for k in range(8):
        buf, sem, key = ringA.next(hw=True)
        cbv = buf[:, 0:2048].rearrange("p (a s r c) -> p a s r c", a=4, s=8, r=2)
        yb = [2 * (k % 4), 2 * (k % 4) + 1]
        if k < 4:
            P.dma(sync, buf[:, 0:2048], E["cbd"][jl, k], sem, r=[("cbd", jl)], w=[key])
        else:
            wtv = buf[:, 2048:3072].rearrange("p (t c) -> p t c", t=8)
            P.dma_multi(sync, [(buf[:, 0:2048], E["cbd"][jl, k]), (buf[:, 2048:3072], E["wtd"][jl, k])], sem,
                        r=[("cbd", jl), ("wtd", jl)], w=[key])
            near(k, wtv, yb, [key, "hb", "hprev"])
        far(k, cbv, yb, [key, "XSb"])
        evac(k, yb)
    P.op(dve, lambda v: v.tensor_copy(out=hprev[:, jl, :, :], in_=hv[:, :, :, 127]), r=["hb", "hprev"], w=["hprev"])
    wo = E["swo_d"][jl].rearrange("(k p) n -> p k n", p=128)
    cnt = 0
    for mg in range(4):
        buf, sem, key = ringA.next()
        bv = buf[:, :].rearrange("p (g k n) -> p g k n", g=2, k=8)
        P.dma_multi(pool, [(bv[:, 0], wo[:, :, mg * 256:(mg + 1) * 256]),
                           (bv[:, 1], wo[:, :, D + mg * 256:D + (mg + 1) * 256])], sem, w=[key])
        for mm_ in range(2):
            m = mg * 2 + mm_
            for tb in range(2):
                sl = slice(tb * 512, (tb + 1) * 512)
                vb = 4 + cnt % 2
                gb = 6 + cnt % 2
                cnt += 1
                for g, bnk in ((0, vb), (1, gb)):
                    for k in range(8):
                        P.op(pe, lambda e, bv=bv, g=g, mm_=mm_, k=k, sl=sl, bnk=bnk: e.matmul(
                            ps[bnk][:, :], lhsT=bv[:, g, k, mm_ * 128:(mm_ + 1) * 128], rhs=ybuf[:, k, sl],
                            start=(k == 0), stop=(k == 7)), r=[key, ("ybuf", k)], w=[PSK[bnk]], inc=(k == 7))
                tbuf = cnt % 4
                P.op(act, lambda a, gb=gb, tbuf=tbuf: a.activation(out=tmpf[:, tbuf, :], in_=ps[gb][:, :], func=AF.Sigmoid),
                     r=[PSK[gb]], w=[("tmpf", tbuf)])
                P.op(dve, lambda v, vb=vb, tbuf=tbuf: v.tensor_tensor(out=tmpf[:, tbuf, :], in0=ps[vb][:, :],
                                                                      in1=tmpf[:, tbuf, :], op=ALU.mult),
                     r=[PSK[vb], ("tmpf", tbuf)], w=[("tmpf", tbuf)])
                gc = 2 * 8 + m
                P.op(dve, lambda v, tbuf=tbuf, m=m, sl=sl, gc=gc: v.scalar_tensor_tensor(
                    out=xT[:, m, sl], in0=tmpf[:, tbuf, :], scalar=mods[:, li, gc:gc + 1], in1=xT[:, m, sl],
                    op0=ALU.mult, op1=ALU.add), r=[("tmpf", tbuf), "mods", "xT"], w=["xT"])


def _prep_inputs(inp, b):
    f = np.float32
    g = lambda a: np.ascontiguousarray(np.asarray(a, dtype=f))

    def pk(v):
        v = np.asarray(v, dtype=f)
        lead = v.shape[:-1]
        return g(np.moveaxis(v.reshape(lead + (8, 128)), -1, 0))

    def gp(a):
        a = np.asarray(a, dtype=f).reshape(2, 32, 2, 64)
        return g(a.transpose(2, 3, 0, 1).reshape(128, 2, 32))
    m = {}
    m["x"] = g(inp["x"][b])
    m["c"] = pk(inp["c"][b])
    m["n1g"] = pk(inp["norm1_g"])
    m["n2g"] = pk(inp["norm2_g"])
    m["fing"] = pk(inp["final_g"])
    m["w_ada"] = g(inp["w_ada"])
    m["b_ada"] = g(np.asarray(inp["b_ada"], dtype=f).reshape(4, 48, 128).transpose(2, 0, 1))
    m["lre"] = gp(inp["ssm_a_re"])
    m["lim"] = gp(inp["ssm_a_im"])
    ls = np.broadcast_to(np.asarray(inp["ssm_log_step"], dtype=f)[:, :, None], (2, 64, 64))
    m["lst"] = gp(ls)
    for nm, src in (("bre", "ssm_b_re"), ("bim", "ssm_b_im")):
        a = np.asarray(inp[src], dtype=f).reshape(2, 32, 2, 64, 16)
        m[nm] = g(a.transpose(2, 3, 0, 1, 4).reshape(128, 2, 512))
    for nm, src in (("crt", "ssm_c_re"), ("cit", "ssm_c_im")):
        a = np.asarray(inp[src], dtype=f).reshape(2, 32, 2, 16, 64)
        m[nm] = g(a.transpose(2, 4, 0, 1, 3).reshape(128, 2, 512))
    m["dd"] = pk(inp["ssm_d"])
    m["ssm_w_out"] = g(inp["ssm_w_out"])
    m["conv_w_in"] = g(inp["conv_w_in"])
    cwv = np.asarray(inp["conv_w"], dtype=f).reshape(2, 3, 8, 128)
    m["cw"] = g(cwv.transpose(3, 0, 2, 1))
    m["conv_w_out"] = g(inp["conv_w_out"])
    m["w_ffn_in"] = g(inp["w_ffn_in"])
    m["w_ffn_out"] = g(inp["w_ffn_out"])
    return m


_NC_CACHE = {}


def kernel(**inputs):
    if "full" not in _NC_CACHE:
        _NC_CACHE["full"] = build()
    nc = _NC_CACHE["full"]
    n = 8
    in_maps = [_prep_inputs(inputs, b) for b in range(n)]
    res = run_bass_kernel_spmd(nc, in_maps, core_ids=list(range(n)))
    out = np.stack([np.asarray(r["y"], dtype=np.float32) for r in res.results], axis=0)
    return out
```

```python
import contextlib
import math
import os
DBG = int(os.environ.get('KDBG', '0'))
OPT = int(os.environ.get('KOPT', '1'))
import numpy as np
import concourse.bass as bass
import concourse.mybir as mybir
from concourse.bass_utils import run_bass_kernel_spmd

F32 = mybir.dt.float32
BF16 = mybir.dt.bfloat16
ALU = mybir.AluOpType
AF = mybir.ActivationFunctionType

D = 1024
KC = 8
TT = 1024
SEQ = 4096
FF = 2816
FC = 22
EPS = 1e-6


class Sem:
    def __init__(self, name):
        self.name = name
        self.handle = None
        self.count = 0


class Q:
    def __init__(self, prog, name):
        self.prog = prog
        self.name = name
        self.items = []
        self.sem = prog.sem("q_" + name)
        self.waited = {}
        self.pending = []
        self.last = None

    def wait(self, ev):
        if ev is None:
            return
        s, v = ev
        if self.waited.get(s, 0) >= v:
            return
        self.waited[s] = v
        self.items.append(lambda e, s=s, v=v: e.wait_ge(s.handle, v))


class Prog:
    def __init__(self, nc):
        self.nc = nc
        self.sems = []
        self.q = {}
        for n in ["sync", "scalar", "vector", "gpsimd", "tensor"]:
            self.q[n] = Q(self, n)
        self.sync = self.q["sync"]
        self.act = self.q["scalar"]
        self.dve = self.q["vector"]
        self.pool = self.q["gpsimd"]
        self.pe = self.q["tensor"]
        self.state = {}
        self.dma_events = []

    def sem(self, name):
        s = Sem(name)
        self.sems.append(s)
        return s

    def _st(self, k):
        st = self.state.get(k)
        if st is None:
            st = {"w": None, "r": []}
            self.state[k] = st
        return st

    def _deps(self, q, r, w):
        for k in r:
            q.wait(self._st(k)["w"])
        for k in w:
            st = self._st(k)
            q.wait(st["w"])
            for ev in st["r"]:
                q.wait(ev)

    def _commit(self, ev, r, w):
        for k in r:
            self._st(k)["r"].append(ev)
        for k in w:
            self.state[k] = {"w": ev, "r": []}

    def op(self, q, fn, r=(), w=(), inc=True):
        self._deps(q, r, w)
        if inc:
            s = q.sem
            s.count += 1
            ev = (s, s.count)
            q.items.append(lambda e, fn=fn, s=s: fn(e).then_inc(s.handle, 1))
            for (pr, pw) in q.pending:
                self._commit(ev, pr, pw)
            q.pending = []
            self._commit(ev, r, w)
            q.last = ev
            return ev
        q.items.append(lambda e, fn=fn: fn(e))
        q.pending.append((tuple(r), tuple(w)))
        return None

    def dma(self, q, out, in_, sem, r=(), w=()):
        self._deps(q, r, w)
        sem.count += 16
        ev = (sem, sem.count)
        q.items.append(lambda e, out=out, in_=in_, sem=sem: e.dma_start(out=out, in_=in_).then_inc(sem.handle, 16))
        self._commit(ev, r, w)
        self.dma_events.append(ev)
        return ev

    def dma_multi(self, q, pairs, sem, r=(), w=()):
        self._deps(q, r, w)
        for (out, in_) in pairs:
            sem.count += 16
            q.items.append(lambda e, out=out, in_=in_, sem=sem: e.dma_start(out=out, in_=in_).then_inc(sem.handle, 16))
        ev = (sem, sem.count)
        self._commit(ev, r, w)
        return ev

    def barrier(self, queues=None):
        qs = queues or [self.pe, self.act, self.dve, self.pool]
        evs = [q.last for q in qs if q.last is not None]
        for q in qs:
            for ev in evs:
                if ev[0] is not q.sem:
                    q.wait(ev)

    def emit(self):
        nc = self.nc
        with contextlib.ExitStack() as st:
            for s in self.sems:
                s.handle = st.enter_context(nc.semaphore(s.name))
            block = st.enter_context(nc.Block())
            for n in ["sync", "scalar", "vector", "gpsimd", "tensor"]:
                items = self.q[n].items

                def body(eng, items=items):
                    for it in items:
                        it(eng)
                getattr(block, n)(body)


class Ring:
    def __init__(self, P, name, bufs):
        self.P = P
        self.bufs = bufs
        self.sems = [P.sem(f"{name}{i}") for i in range(len(bufs))]
        self.sems_hw = [P.sem(f"{name}h{i}") for i in range(len(bufs))]
        self.keys = [(name, i) for i in range(len(bufs))]
        self.i = 0

    def next(self, hw=False):
        i = self.i % len(self.bufs)
        self.i += 1
        return self.bufs[i], (self.sems_hw[i] if hw else self.sems[i]), self.keys[i]


def build(ntiles=4, layers=(0, 1, 2, 3), do_final=True):
    nc = bass.Bass("TRN2", target_bir_lowering=False)
    P = Prog(nc)
    pe, act, dve, pool, sync = P.pe, P.act, P.dve, P.pool, P.sync

    def din(name, shape, dt=F32):
        return nc.dram_tensor(name, list(shape), dt, kind="ExternalInput").ap()

    x_d = din("x", [SEQ, D])
    c_d = din("c", [128, 8])
    n1g_d = din("n1g", [128, 4, 8])
    n2g_d = din("n2g", [128, 4, 8])
    fing_d = din("fing", [128, 8])
    wada_d = din("w_ada", [4, D, 6 * D])
    bada_d = din("b_ada", [128, 4, 48])
    lre_d = din("lre", [128, 2, 32])
    lim_d = din("lim", [128, 2, 32])
    lst_d = din("lst", [128, 2, 32])
    bre_d = din("bre", [128, 2, 512])
    bim_d = din("bim", [128, 2, 512])
    crt_d = din("crt", [128, 2, 512])
    cit_d = din("cit", [128, 2, 512])
    dd_d = din("dd", [128, 2, 8])
    swo_d = din("ssm_w_out", [2, D, 2 * D])
    cwi_d = din("conv_w_in", [2, D, 3 * D])
    cw_d = din("cw", [128, 2, 8, 3])
    cwo_d = din("conv_w_out", [2, D, D])
    wfi_d = din("w_ffn_in", [4, D, 2 * FF])
    wfo_d = din("w_ffn_out", [4, FF, D])
    y_d = nc.dram_tensor("y", [SEQ, D], F32, kind="ExternalOutput").ap()
    wbd = nc.dram_tensor("wbd", [2, 8, 128, 2048], BF16, kind="Internal").ap()
    cbd = nc.dram_tensor("cbd", [2, 8, 128, 2048], BF16, kind="Internal").ap()
    wtd = nc.dram_tensor("wtd", [2, 8, 128, 1024], BF16, kind="Internal").ap()

    with contextlib.ExitStack() as st:
        def sb(name, shape, dt=F32):
            return st.enter_context(nc.sbuf_tensor("sb_" + name, list(shape), dt))

        xT = sb("xT", [128, 8, 1024])
        hb = sb("hb", [128, 8, 1024], BF16)
        hprev = sb("hprev", [128, 2, 8, 8], BF16)
        scr = sb("scr", [128, 16448])
        wA = [sb(f"wA{i}", [128, 4096], BF16) for i in range(3)]
        wB = [sb(f"wB{i}", [128, 5632], BF16) for i in range(2)]
        xin = sb("xin", [128, 2, 1024])
        sq = sb("sq", [128, 8, 512], BF16)
        rstd = sb("rstd", [128, 2, 512])
        tmpf = sb("tmpf", [128, 4, 512])
        gel = sb("gel", [128, 2, 1024])
        ident = sb("ident", [128, 128])
        ones_bf = sb("ones_bf", [128, 128], BF16)
        mods = sb("mods", [128, 4, 48])
        gsc = sb("gsc", [128, 4, 2, 8])
        n1g = sb("n1g", [128, 4, 8])
        n2g = sb("n2g", [128, 4, 8])
        fing = sb("fing", [128, 8])
        fing32 = sb("fing32", [128, 8])
        bada = sb("bada", [128, 4, 48])
        cin = sb("cin", [128, 8])
        cact = sb("cact", [128, 8])
        cw = sb("cw", [128, 2, 8, 3])
        dd = sb("dd", [128, 2, 8])
        halo = sb("halo", [128, 2, 8, 8])
        sstate = sb("sstate", [128, 2, 64])
        Ca = sb("Ca", [128, 2, 64])
        Cb = sb("Cb", [128, 2, 64])
        T1 = sb("T1", [128, 64])
        T2 = sb("T2", [128, 64])
        T3 = sb("T3", [128, 64])
        ps = [st.enter_context(nc.psum_tensor(f"ps{i}", [128, 512], F32)) for i in range(8)]
        PSK = [("ps", i) for i in range(8)]

        ringA = Ring(P, "wA", wA)
        ringB = Ring(P, "wB", wB)
        s_small = P.sem("small")
        s_small2 = P.sem("small2")
        s_xin = [P.sem("xin0"), P.sem("xin1")]
        s_out = [P.sem("out0"), P.sem("out1")]
        s_scr = [P.sem("scrd0"), P.sem("scrd1"), P.sem("scrd2")]

        small = [(cin, c_d, "cin"), (n1g, n1g_d, "n1g"), (n2g, n2g_d, "n2g"), (fing, fing_d, "fing"),
                 (bada, bada_d, "bada"), (cw, cw_d, "cw"), (dd, dd_d, "dd")]
        P.dma_multi(sync, [(t[:], d_) for (t, d_, _) in small], s_small, w=[k_ for (_, _, k_) in small])

        P.op(pool, lambda g: g.memset(ident[:], 1.0), w=["ident"])
        P.op(pool, lambda g: g.affine_select(out=ident[:], in_=ident[:], pattern=[[-1, 128]],
                                             compare_op=ALU.is_equal, fill=0.0, base=0, channel_multiplier=1),
             r=["ident"], w=["ident"])
        P.op(pool, lambda g: g.memset(ones_bf[:], 1.0), w=["ones"])
        P.op(pool, lambda g: g.memset(hprev[:], 0.0), w=["hprev"])
        P.op(pool, lambda g: g.memset(halo[:], 0.0), w=["halo"])
        P.op(pool, lambda g: g.memset(sstate[:], 0.0), w=["sstate"])

        P.op(act, lambda a: a.activation(out=cact[:], in_=cin[:], func=AF.Silu), r=["cin"], w=["cact"])
        adab = [xin[:, :, :].rearrange("p a (k n) -> p (a k) n", k=4),
                sq[:, :, :].rearrange("p k n -> p (k n)").bitcast(F32).rearrange("p (k n) -> p k n", k=8)]
        s_ada = [P.sem("ada0"), P.sem("ada1")]

        rowsb = rstd[0:1, :, 0:256]

        def ada_gen():
            nb = 0
            deferred = None
            for li in range(4):
                wv = wada_d[li].rearrange("(k p) n -> p k n", p=128)
                mb = 6 + li % 2

                def flush(d):
                    b_, blk_, mb_ = d
                    for jj in range(2):
                        j = blk_ * 2 + jj
                        P.op(pe, lambda e, b_=b_, jj=jj, j=j, mb_=mb_: e.transpose(
                            out=ps[mb_][:, j:j + 1], in_=rowsb[0:1, b_, jj * 128:(jj + 1) * 128], identity=ident[0:1, 0:1]),
                            r=[("rowsb", b_), "ident"], w=[PSK[mb_]], inc=(jj == 1))
                for blk in range(24):
                    b = nb % 2
                    rb = 4 + nb % 2
                    nb += 1
                    akeys = [[("xin", 0), ("xin", 1)], ["sq"]][b]
                    P.dma(sync, adab[b], wv[:, :, blk * 256:(blk + 1) * 256], s_ada[b], w=akeys)
                    for k in range(8):
                        P.op(pe, lambda e, rb=rb, k=k, b=b: e.matmul(
                            ps[rb][0:1, 0:256], lhsT=cact[:, k:k + 1], rhs=adab[b][:, k, :],
                            start=(k == 0), stop=(k == 7)),
                            r=akeys + ["cact"], w=[PSK[rb]], inc=(k == 7))
                    if deferred is not None:
                        flush(deferred)
                    P.op(act, lambda a_, rb=rb, b=b: a_.activation(out=rowsb[0:1, b, :], in_=ps[rb][0:1, 0:256], func=AF.Copy),
                         r=[PSK[rb]], w=[("rowsb", b)])
                    deferred = (b, blk, mb)
                    yield
                flush(deferred)
                deferred = None
                P.op(dve, lambda v, li=li, mb=mb: v.tensor_tensor(out=mods[:, li, :], in0=ps[mb][:, 0:48], in1=bada[:, li, :],
                                                                  op=ALU.add), r=[PSK[mb], "bada"], w=["mods"])
                for which, ng, ngk in ((0, n1g, "n1g"), (1, n2g, "n2g")):
                    sc = mods[:, li, (1 + 3 * which) * 8:(2 + 3 * which) * 8]
                    P.op(dve, lambda v, li=li, which=which, ng=ng, sc=sc: v.scalar_tensor_tensor(
                        out=gsc[:, li, which, :], in0=sc, scalar=1.0, in1=ng[:, li, :], op0=ALU.add, op1=ALU.mult),
                        r=["mods", ngk], w=["gsc"])
                    P.op(dve, lambda v, li=li, which=which: v.tensor_scalar(
                        out=gsc[:, li, which, :], in0=gsc[:, li, which, :], scalar1=32.0, scalar2=None, op0=ALU.mult),
                        r=["gsc"], w=["gsc"])
            P.op(dve, lambda v: v.tensor_scalar(out=fing32[:], in0=fing[:], scalar1=32.0, scalar2=None, op0=ALU.mult),
                 r=["fing"], w=["fing32"])
        ada_it = ada_gen()
        if not (OPT & 1):
            for _ in ada_it:
                pass
            P.barrier()

        ssm_layers = [l for l in layers if l % 2 == 0]
        if ssm_layers and not (DBG & 2):
            _ssm_prologue(nc, P, st, locals())
        for _ in ada_it:
            pass
        P.barrier()

        env = dict(locals())
        for ti in range(ntiles):
            _load_tile(env, ti)
            for li in layers:
                bar = (lambda: None) if (OPT & 2) else (lambda: P.barrier([pe, act, dve]))
                _norm_mod(env, li, 0)
                bar()
                if li % 2 == 0:
                    _ssm_mixer(env, li, ti)
                else:
                    _conv_mixer(env, li, ti)
                bar()
                _norm_mod(env, li, 1)
                bar()
                _ffn(env, li)
                bar()
            _final_store(env, ti, do_final)
            bar()
        for s in s_out:
            if s.count:
                sync.wait((s, s.count))
        P.emit()
    return nc


def _ssm_prologue(nc, P, st, E):
    pe, act, dve, pool, sync = P.pe, P.act, P.dve, P.pool, P.sync
    scr, hb, ps, PSK, ident = E["scr"], E["hb"], E["ps"], E["PSK"], E["ident"]
    Ca, Cb = E["Ca"], E["Cb"]
    wbd, cbd, wtd = E["wbd"], E["cbd"], E["wtd"]
    s_scr = E["s_scr"]
    s_small = E["s_small"]

    def sb(name, shape, dt=F32):
        return st.enter_context(nc.sbuf_tensor("sp_" + name, list(shape), dt))

    class V:
        def __init__(self, name, ap):
            self.name = name
            self.ap = ap

        def __getitem__(self, k):
            return self.ap[k]

    t32c = {}

    def t32(nm):
        if nm not in t32c:
            t32c[nm] = sb("t_" + nm, [128, 32])
        return t32c[nm]

    xTf = E["xT"][:, :, :].rearrange("p k n -> p (k n)")
    gelf = E["gel"][:, :, :].rearrange("p a n -> p (a n)")
    tmpff = E["tmpf"][:, :, :].rearrange("p a n -> p (a n)")
    CBall = scr[:, 0:8192].bitcast(BF16).rearrange("p (a s r c) -> p a s r c", a=32, s=8, r=2)
    WBall = scr[:, 8192:16384].bitcast(BF16).rearrange("p (k s r c) -> p k s r c", k=8, s=8, r=2)
    WTall = hb[:, :, :].rearrange("p k (t c) -> p k t c", t=8)
    lre = sb("lre", [128, 2, 32]); lim = sb("lim", [128, 2, 32]); lst = sb("lst", [128, 2, 32])
    bre = V("bre", xTf[:, 0:1024].rearrange("p (j n) -> p j n", j=2))
    bim = V("bim", xTf[:, 1024:2048].rearrange("p (j n) -> p j n", j=2))
    crt = V("crt", xTf[:, 2048:3072].rearrange("p (j n) -> p j n", j=2))
    cit = V("cit", xTf[:, 3072:4096].rearrange("p (j n) -> p j n", j=2))
    Bpr = V("Bpr", xTf[:, 4096:5120].rearrange("p (a c) -> p a c", a=32))
    Bpi = V("Bpi", xTf[:, 5120:6144].rearrange("p (a c) -> p a c", a=32))
    Cpr = V("Cpr", xTf[:, 6144:7168].rearrange("p (a c) -> p a c", a=32))
    nCpi = V("nCpi", xTf[:, 7168:8192].rearrange("p (a c) -> p a c", a=32))
    Bbr = V("Bbr", gelf[:, 0:512].rearrange("p (a c) -> p a c", a=32))
    Bbi = V("Bbi", gelf[:, 512:1024].rearrange("p (a c) -> p a c", a=32))
    U1 = V("U1", gelf[:, 1024:1536].rearrange("p (a c) -> p a c", a=32))
    U2 = V("U2", gelf[:, 1536:2048].rearrange("p (a c) -> p a c", a=32))
    nCr = V("nCr", tmpff[:, 0:512].rearrange("p (a c) -> p a c", a=32))
    nCi = V("nCi", tmpff[:, 512:1024].rearrange("p (a c) -> p a c", a=32))
    P.dma_multi(sync, [(t[:], d_) for t, d_ in ((lre, E["lre_d"]), (lim, E["lim_d"]), (lst, E["lst_d"]), (bre, E["bre_d"]),
                                                  (bim, E["bim_d"]), (crt, E["crt_d"]), (cit, E["cit_d"]))], E["s_small2"],
                w=["lre", "lim", "lst", "bre", "bim", "crt", "cit"])

    def tt(out, a, b, op, r, w, q=None):
        return P.op(q or dve, lambda v: v.tensor_tensor(out=out, in0=a, in1=b, op=op), r=r, w=w)

    def bc(a):
        return a.unsqueeze(2).to_broadcast([128, 32, 16])

    P.op(pool, lambda g: g.memset(WTall, 0.0), w=["WTall"])
    tctr = [0]
    def do_layer(j):
        P.op(pool, lambda g: g.memset(scr[:, 0:8192], 0.0), w=["CBall"])
        for t in (Bpr, Bpi, Cpr, nCpi):
            P.op(pool, lambda g, t=t: g.memset(t[:], 0.0), w=[t.name])
        dt_ = t32("dt"); lr = t32("lr"); ph = t32("ph"); lrd = t32("lrd")
        s8 = t32("s8"); c8 = t32("c8"); m8 = t32("m8")
        P.op(act, lambda a: a.activation(out=dt_[:], in_=lst[:, j, :], func=AF.Exp), r=["lst"], w=[dt_.name])
        P.op(dve, lambda v: v.tensor_scalar(out=lr[:], in0=lre[:, j, :], scalar1=-1e-4, scalar2=None, op0=ALU.min),
             r=["lre"], w=[lr.name])
        li_ = lim[:, j, :]
        tt(ph[:], li_, dt_[:], ALU.mult, ["lim", dt_.name], [ph.name])
        tt(lrd[:], lr[:], dt_[:], ALU.mult, [lr.name, dt_.name], [lrd.name])
        P.op(act, lambda a: a.activation(out=s8[:], in_=ph[:], func=AF.Sin, scale=0.125), r=[ph.name], w=[s8.name])
        P.op(act, lambda a: a.activation(out=c8[:], in_=ph[:], func=AF.Sin, scale=-0.125, bias=math.pi / 2),
             r=[ph.name], w=[c8.name])
        P.op(act, lambda a: a.activation(out=m8[:], in_=lrd[:], func=AF.Exp, scale=0.125), r=[lrd.name], w=[m8.name])
        ar = t32("ar"); ai = t32("ai")
        tt(ar[:], m8[:], c8[:], ALU.mult, [m8.name, c8.name], [ar.name])
        tt(ai[:], m8[:], s8[:], ALU.mult, [m8.name, s8.name], [ai.name])
        for it in range(3):
            q1 = t32("q1"); q2 = t32("q2"); q3 = t32("q3"); nr = t32(f"nr{it}"); ni = t32(f"ni{it}")
            tt(q1[:], ar[:], ar[:], ALU.mult, [ar.name], [q1.name])
            tt(q2[:], ai[:], ai[:], ALU.mult, [ai.name], [q2.name])
            tt(q3[:], ar[:], ai[:], ALU.mult, [ar.name, ai.name], [q3.name])
            tt(nr[:], q1[:], q2[:], ALU.subtract, [q1.name, q2.name], [nr.name])
            tt(ni[:], q3[:], q3[:], ALU.add, [q3.name], [ni.name])
            ar, ai = nr, ni
        pw = [None] * 9
        one = t32("one"); zero = t32("zero")
        P.op(pool, lambda g: g.memset(one[:], 1.0), w=[one.name])
        P.op(pool, lambda g: g.memset(zero[:], 0.0), w=[zero.name])
        pw[0] = (one, zero)
        pw[1] = (ar, ai)

        def cmul(xr, xi, yr, yi, n):
            a1 = t32("a1"); a2 = t32("a2"); a3 = t32("a3"); a4 = t32("a4"); zr = t32(f"zr{n}"); zi = t32(f"zi{n}")
            tt(a1[:], xr[:], yr[:], ALU.mult, [xr.name, yr.name], [a1.name])
            tt(a2[:], xi[:], yi[:], ALU.mult, [xi.name, yi.name], [a2.name])
            tt(a3[:], xr[:], yi[:], ALU.mult, [xr.name, yi.name], [a3.name])
            tt(a4[:], xi[:], yr[:], ALU.mult, [xi.name, yr.name], [a4.name])
            tt(zr[:], a1[:], a2[:], ALU.subtract, [a1.name, a2.name], [zr.name])
            tt(zi[:], a3[:], a4[:], ALU.add, [a3.name, a4.name], [zi.name])
            return zr, zi
        for n in range(2, 9):
            pw[n] = cmul(pw[n - 1][0], pw[n - 1][1], ar, ai, n)
        a8r, a8i = pw[8]
        P.op(dve, lambda v: v.tensor_copy(out=Ca[:, j, 0:32], in_=a8r[:]), r=[a8r.name], w=["Ca"])
        P.op(dve, lambda v: v.tensor_copy(out=Ca[:, j, 32:64], in_=a8r[:]), r=[a8r.name], w=["Ca"])
        P.op(dve, lambda v: v.tensor_copy(out=Cb[:, j, 32:64], in_=a8i[:]), r=[a8i.name], w=["Cb"])
        P.op(dve, lambda v: v.tensor_scalar(out=Cb[:, j, 0:32], in0=a8i[:], scalar1=-1.0, scalar2=None, op0=ALU.mult),
             r=[a8i.name], w=["Cb"])
        den = t32("den"); d2 = t32("d2"); rden = t32("rden"); am1 = t32("am1")
        tt(den[:], lr[:], lr[:], ALU.mult, [lr.name], [den.name])
        tt(d2[:], li_, li_, ALU.mult, ["lim"], [d2.name])
        tt(den[:], den[:], d2[:], ALU.add, [den.name, d2.name], [den.name])
        P.op(dve, lambda v: v.reciprocal(out=rden[:], in_=den[:]), r=[den.name], w=[rden.name])
        P.op(dve, lambda v: v.tensor_scalar(out=am1[:], in0=ar[:], scalar1=-1.0, scalar2=None, op0=ALU.add),
             r=[ar.name], w=[am1.name])
        e1 = t32("e1"); e2 = t32("e2"); qr = t32("qr"); qi = t32("qi")
        tt(e1[:], am1[:], lr[:], ALU.mult, [am1.name, lr.name], [e1.name])
        tt(e2[:], ai[:], li_, ALU.mult, [ai.name, "lim"], [e2.name])
        tt(e1[:], e1[:], e2[:], ALU.add, [e1.name, e2.name], [e1.name])
        tt(qr[:], e1[:], rden[:], ALU.mult, [e1.name, rden.name], [qr.name])
        e3 = t32("e3"); e4 = t32("e4")
        tt(e3[:], ai[:], lr[:], ALU.mult, [ai.name, lr.name], [e3.name])
        tt(e4[:], am1[:], li_, ALU.mult, [am1.name, "lim"], [e4.name])
        tt(e3[:], e3[:], e4[:], ALU.subtract, [e3.name, e4.name], [e3.name])
        tt(qi[:], e3[:], rden[:], ALU.mult, [e3.name, rden.name], [qi.name])
        B_r = bre[:, j, :].rearrange("p (a h) -> p a h", h=16)
        B_i = bim[:, j, :].rearrange("p (a h) -> p a h", h=16)
        tt(U1[:], B_r, bc(qr[:]), ALU.mult, ["bre", qr.name], ["U1"])
        tt(U2[:], B_i, bc(qi[:]), ALU.mult, ["bim", qi.name], ["U2"])
        tt(Bbr[:], U1[:], U2[:], ALU.subtract, ["U1", "U2"], ["Bbr"])
        tt(U1[:], B_i, bc(qr[:]), ALU.mult, ["bim", qr.name], ["U1"])
        tt(U2[:], B_r, bc(qi[:]), ALU.mult, ["bre", qi.name], ["U2"])
        tt(Bbi[:], U1[:], U2[:], ALU.add, ["U1", "U2"], ["Bbi"])
        C_r = crt[:, j, :].rearrange("p (a h) -> p a h", h=16)
        C_i = cit[:, j, :].rearrange("p (a h) -> p a h", h=16)
        for (lo, hi, c0) in ((0, 64, 0), (64, 128, 16)):
            P.op(dve, lambda v, lo=lo, hi=hi, c0=c0: v.tensor_copy(out=Cpr[lo:hi, :, c0:c0 + 16], in_=C_r[lo:hi]),
                 r=["crt"], w=["Cpr"])
            P.op(dve, lambda v, lo=lo, hi=hi, c0=c0: v.tensor_scalar(
                out=nCpi[lo:hi, :, c0:c0 + 16], in0=C_i[lo:hi], scalar1=-1.0, scalar2=None, op0=ALU.mult),
                r=["cit"], w=["nCpi"])
        P.op(dve, lambda v: v.tensor_scalar(out=nCr[:], in0=C_r, scalar1=-1.0, scalar2=None, op0=ALU.mult),
             r=["crt"], w=["nCr"])
        P.op(dve, lambda v: v.tensor_scalar(out=nCi[:], in0=C_i, scalar1=-1.0, scalar2=None, op0=ALU.mult),
             r=["cit"], w=["nCi"])
        def do_s(s):
            pr, pi_ = pw[7 - s]
            tt(U1[:], Bbr[:], bc(pr[:]), ALU.mult, ["Bbr", pr.name], ["U1"])
            tt(U2[:], Bbi[:], bc(pi_[:]), ALU.mult, ["Bbi", pi_.name], ["U2"])
            for (lo, hi, c0) in ((0, 64, 0), (64, 128, 16)):
                tt(Bpr[lo:hi, :, c0:c0 + 16], U1[lo:hi], U2[lo:hi], ALU.subtract, ["U1", "U2"], ["Bpr"])
            tt(U1[:], Bbi[:], bc(pr[:]), ALU.mult, ["Bbi", pr.name], ["U1"])
            tt(U2[:], Bbr[:], bc(pi_[:]), ALU.mult, ["Bbr", pi_.name], ["U2"])
            for (lo, hi, c0) in ((0, 64, 0), (64, 128, 16)):
                tt(Bpi[lo:hi, :, c0:c0 + 16], U1[lo:hi], U2[lo:hi], ALU.add, ["U1", "U2"], ["Bpi"])
            tau = 7 - s
            for k in range(8 if not (DBG & 4) else 0):
                bk = tctr[0] % 4
                tctr[0] += 1
                for ri, Bp in enumerate((Bpr, Bpi)):
                    src = Bp[:, 4 * k:4 * k + 4, :].rearrange("p a c -> p (a c)")
                    P.op(pe, lambda e, src=src, bk=bk, ri=ri: e.transpose(
                        out=ps[bk][:, ri * 128:(ri + 1) * 128], in_=src, identity=ident[:]),
                        r=[Bp.name, "ident"], w=[PSK[bk]], inc=False)
                srcr = Bpr[:, 4 * k:4 * k + 4, :].rearrange("p a c -> p (a c)")
                srci = Bpi[:, 4 * k:4 * k + 4, :].rearrange("p a c -> p (a c)")
                cr = Cpr[:, 4 * k:4 * k + 4, :].rearrange("p a c -> p (a c)")
                ci = nCpi[:, 4 * k:4 * k + 4, :].rearrange("p a c -> p (a c)")
                P.op(pe, lambda e, bk=bk, srcr=srcr, cr=cr: e.matmul(ps[bk][:, 256:384], lhsT=srcr, rhs=cr,
                                                                     start=True, stop=False),
                     r=["Bpr", "Cpr"], w=[PSK[bk]], inc=False)
                P.op(pe, lambda e, bk=bk, srci=srci, ci=ci: e.matmul(ps[bk][:, 256:384], lhsT=srci, rhs=ci,
                                                                     start=False, stop=True),
                     r=["Bpi", "nCpi"], w=[PSK[bk]], inc=True)
                P.op(act, lambda a, bk=bk, k=k, s=s: a.activation(
                    out=WBall[:, k, s, :, :], in_=ps[bk][:, 0:256].rearrange("p (r c) -> p r c", r=2), func=AF.Copy),
                    r=[PSK[bk]], w=["WBall"])
                for jq in range(4):
                    P.op(dve, lambda v, bk=bk, k=k, tau=tau, jq=jq: v.tensor_copy(
                        out=WTall[32 * jq:32 * jq + 32, k, tau, 32 * jq:32 * jq + 32],
                        in_=ps[bk][32 * jq:32 * jq + 32, 256 + 32 * jq:256 + 32 * jq + 32]),
                        r=[], w=["WTall", PSK[bk]])
            qr_, qi_ = pw[s + 1]
            tt(U1[:], C_r, bc(qr_[:]), ALU.mult, ["crt", qr_.name], ["U1"])
            tt(U2[:], C_i, bc(qi_[:]), ALU.mult, ["cit", qi_.name], ["U2"])
            for (lo, hi, c0) in ((0, 64, 0), (64, 128, 16)):
                tt(CBall[lo:hi, :, s, 0, c0:c0 + 16], U1[lo:hi], U2[lo:hi], ALU.subtract, ["U1", "U2"], ["CBall"])
            tt(U1[:], nCr[:], bc(qi_[:]), ALU.mult, ["nCr", qi_.name], ["U1"])
            tt(U2[:], nCi[:], bc(qr_[:]), ALU.mult, ["nCi", qr_.name], ["U2"])
            for (lo, hi, c0) in ((0, 64, 0), (64, 128, 16)):
                tt(CBall[lo:hi, :, s, 1, c0:c0 + 16], U1[lo:hi], U2[lo:hi], ALU.add, ["U1", "U2"], ["CBall"])
        for s_ in range(8):
            do_s(s_)
            for _ in range(6):
                next(E["ada_it"], None)
        if DBG & 8:
            return
        P.dma(sync, wbd[j].rearrange("k p n -> p k n"), scr[:, 8192:16384].bitcast(BF16).rearrange("p (k n) -> p k n", k=8),
              s_scr[0], r=["WBall"], w=[("wbd", j)])
        P.dma(sync, cbd[j].rearrange("k p n -> p k n"), scr[:, 0:8192].bitcast(BF16).rearrange("p (k n) -> p k n", k=8),
              s_scr[1], r=["CBall"], w=[("cbd", j)])
        P.dma(sync, wtd[j].rearrange("k p n -> p k n"), hb[:, :, :], s_scr[2], r=["WTall"], w=[("wtd", j)])
    for j_ in range(2):
        if 2 * j_ in E["layers"]:
            do_layer(j_)
    for sm in s_scr:
        if sm.count:
            for q in (pe, act, dve, pool, sync):
                q.wait((sm, sm.count))


def _load_tile(E, ti):
    P = E["P"]; pe, act, dve, sync = P.pe, P.act, P.dve, P.sync
    xT, xin, ps, PSK, ident = E["xT"], E["xin"], E["ps"], E["PSK"], E["ident"]
    xTv = xT[:, :, :].rearrange("p k (s c) -> p k s c", s=8)
    for blk in range(8):
        b = blk % 2
        r0 = ti * TT + blk * 128
        P.dma(sync, xin[:, b, :], E["x_d"][r0:r0 + 128, :], E["s_xin"][b], w=[("xin", b)])
        for half in range(2):
            bk = (blk * 2 + half) % 4
            for kk in range(4):
                k = half * 4 + kk
                P.op(pe, lambda e, b=b, bk=bk, k=k, kk=kk: e.transpose(
                    out=ps[bk][:, kk * 128:(kk + 1) * 128], in_=xin[:, b, k * 128:(k + 1) * 128], identity=ident[:]),
                    r=[("xin", b), "ident"], w=[PSK[bk]], inc=(kk == 3))
            q = act if half == 0 else dve
            src = ps[bk][:, :].rearrange("p (k jc s) -> p k s jc", k=4, jc=16, s=8)
            dst = xTv[:, half * 4:half * 4 + 4, :, 16 * blk:16 * blk + 16]
            if q is act:
                P.op(q, lambda a, src=src, dst=dst: a.activation(out=dst, in_=src, func=AF.Copy), r=[PSK[bk]], w=["xT"])
            else:
                P.op(q, lambda v, src=src, dst=dst: v.tensor_copy(out=dst, in_=src), r=[PSK[bk]], w=["xT"])


def _final_store(E, ti, do_final):
    P = E["P"]; pe, act, dve, sync = P.pe, P.act, P.dve, P.sync
    xT, xin, ps, PSK, ident, scr = E["xT"], E["xin"], E["ps"], E["PSK"], E["ident"], E["scr"]
    if do_final:
        _norm_mod(E, None, 2)
        src_all = scr[:, 0:8192].rearrange("p (k s c) -> p k s c", k=8, s=8)
        skey = "fout"
    else:
        src_all = xT[:, :, :].rearrange("p k (s c) -> p k s c", s=8)
        skey = "xT"
    for blk in range(8):
        b = blk % 2
        for half in range(2):
            a = (blk * 2 + half) % 2
            bk = (blk * 2 + half) % 4
            stg = scr[:, 8192 + a * 512: 8192 + (a + 1) * 512]
            src = src_all[:, half * 4:half * 4 + 4, :, 16 * blk:16 * blk + 16]
            dstv = stg.rearrange("p (k jc s) -> p k s jc", k=4, jc=16, s=8)
            if half == 0:
                P.op(act, lambda e, src=src, dstv=dstv: e.activation(out=dstv, in_=src, func=AF.Copy),
                     r=[skey], w=[("stg", a)])
            else:
                P.op(dve, lambda e, src=src, dstv=dstv: e.tensor_copy(out=dstv, in_=src), r=[skey], w=[("stg", a)])
            for kk in range(4):
                P.op(pe, lambda e, stg=stg, bk=bk, kk=kk: e.transpose(
                    out=ps[bk][:, kk * 128:(kk + 1) * 128], in_=stg[:, kk * 128:(kk + 1) * 128], identity=ident[:]),
                    r=[("stg", a), "ident"], w=[PSK[bk]], inc=(kk == 3))
            dsto = xin[:, b, half * 512:(half + 1) * 512]
            if half == 0:
                P.op(dve, lambda e, bk=bk, dsto=dsto: e.tensor_copy(out=dsto, in_=ps[bk][:, :]),
                     r=[PSK[bk]], w=[("xin", b)])
            else:
                P.op(act, lambda e, bk=bk, dsto=dsto: e.activation(out=dsto, in_=ps[bk][:, :], func=AF.Copy),
                     r=[PSK[bk]], w=[("xin", b)])
        r0 = ti * TT + blk * 128
        P.dma(sync, E["y_d"][r0:r0 + 128, :], xin[:, b, :], E["s_out"][b], r=[("xin", b)])


def _norm_mod(E, li, which):
    P = E["P"]; pe, act, dve = P.pe, P.act, P.dve
    xT, hb, sq, rstd, tmpf, ps, PSK = E["xT"], E["hb"], E["sq"], E["rstd"], E["tmpf"], E["ps"], E["PSK"]
    ones_bf, gsc, mods, fing32, scr = E["ones_bf"], E["gsc"], E["mods"], E["fing32"], E["scr"]
    fout = scr[:, 0:8192].rearrange("p (k n) -> p k n", k=8)
    for tb in range(2):
        sl = slice(tb * 512, (tb + 1) * 512)
        bk = 6 + tb
        P.op(act, lambda a, sl=sl: a.activation(out=sq[:], in_=xT[:, :, sl], func=AF.Square), r=["xT"], w=["sq"])
        for k in range(8):
            P.op(pe, lambda e, k=k, bk=bk: e.matmul(ps[bk][:, :], lhsT=ones_bf[:], rhs=sq[:, k, :],
                                                    start=(k == 0), stop=(k == 7)),
                 r=["sq", "ones"], w=[PSK[bk]], inc=(k == 7))
        P.op(act, lambda a, tb=tb, bk=bk: a.activation(out=rstd[:, tb, :], in_=ps[bk][:, :], func=AF.Sqrt,
                                                       bias=D * EPS, scale=1.0),
             r=[PSK[bk]], w=[("rstd", tb)])
        P.op(dve, lambda v, tb=tb: v.reciprocal(out=rstd[:, tb, :], in_=rstd[:, tb, :]),
             r=[("rstd", tb)], w=[("rstd", tb)])
        for k in range(8):
            if which == 2:
                P.op(dve, lambda v, k=k, sl=sl, tb=tb: v.scalar_tensor_tensor(
                    out=fout[:, k, sl], in0=xT[:, k, sl], scalar=fing32[:, k:k + 1], in1=rstd[:, tb, :],
                    op0=ALU.mult, op1=ALU.mult), r=["xT", ("rstd", tb), "fing32"], w=["fout"])
                continue
            tbuf = k % 4
            P.op(dve, lambda v, k=k, sl=sl, tb=tb, tbuf=tbuf: v.scalar_tensor_tensor(
                out=tmpf[:, tbuf, :], in0=xT[:, k, sl], scalar=gsc[:, li, which, k:k + 1], in1=rstd[:, tb, :],
                op0=ALU.mult, op1=ALU.mult), r=["xT", ("rstd", tb), "gsc"], w=[("tmpf", tbuf)])
            shc = (3 * which) * 8 + k
            P.op(act, lambda a, k=k, sl=sl, tbuf=tbuf, shc=shc: a.activation(
                out=hb[:, k, sl], in_=tmpf[:, tbuf, :], func=AF.Identity, bias=mods[:, li, shc:shc + 1], scale=1.0),
                r=[("tmpf", tbuf), "mods"], w=["hb"])


def _ffn(E, li):
    P = E["P"]; pe, act, dve, pool = P.pe, P.act, P.dve, P.pool
    xT, hb, scr, tmpf, ps, PSK, mods = E["xT"], E["hb"], E["scr"], E["tmpf"], E["ps"], E["PSK"], E["mods"]
    ringA, ringB = E["ringA"], E["ringB"]
    actb = scr[:, 0:11264].bitcast(BF16).rearrange("p (j n) -> p j n", j=FC)
    wi = E["wfi_d"][li].rearrange("(k p) n -> p k n", p=128)
    wo = E["wfo_d"][li].rearrange("(j p) n -> p j n", p=128)
    cnt = 0
    for grp in range(11):
        buf, sem, key = ringA.next()
        bv = buf[:, :].rearrange("p (g k n) -> p g k n", g=2, k=8)
        P.dma_multi(pool, [(bv[:, 0], wi[:, :, grp * 256:(grp + 1) * 256]),
                           (bv[:, 1], wi[:, :, FF + grp * 256:FF + (grp + 1) * 256])], sem, w=[key])
        for jj in range(2):
            j = grp * 2 + jj
            for tb in range(2):
                sl = slice(tb * 512, (tb + 1) * 512)
                gb = cnt % 2
                ub = 2 + cnt % 2
                cnt += 1
                for k in range(8):
                    P.op(pe, lambda e, bv=bv, jj=jj, k=k, sl=sl, gb=gb: e.matmul(
                        ps[gb][:, :], lhsT=bv[:, 0, k, jj * 128:(jj + 1) * 128], rhs=hb[:, k, sl],
                        start=(k == 0), stop=(k == 7)), r=[key, "hb"], w=[PSK[gb]], inc=(k == 7))
                for k in range(8):
                    P.op(pe, lambda e, bv=bv, jj=jj, k=k, sl=sl, ub=ub: e.matmul(
                        ps[ub][:, :], lhsT=bv[:, 1, k, jj * 128:(jj + 1) * 128], rhs=hb[:, k, sl],
                        start=(k == 0), stop=(k == 7)), r=[key, "hb"], w=[PSK[ub]], inc=(k == 7))
                tbuf = cnt % 4
                P.op(act, lambda a, gb=gb, tbuf=tbuf: a.activation(out=tmpf[:, tbuf, :], in_=ps[gb][:, :], func=AF.Silu),
                     r=[PSK[gb]], w=[("tmpf", tbuf)])
                P.op(dve, lambda v, ub=ub, tbuf=tbuf, j=j, sl=sl: v.tensor_tensor(
                    out=actb[:, j, sl], in0=ps[ub][:, :], in1=tmpf[:, tbuf, :], op=ALU.mult),
                    r=[PSK[ub], ("tmpf", tbuf)], w=[("actb", j)])
    cnt = 0
    for mg in range(4):
        buf, sem, key = ringB.next()
        bv = buf[:, :].rearrange("p (j n) -> p j n", j=FC)
        P.dma(pool, bv, wo[:, :, mg * 256:(mg + 1) * 256], sem, w=[key])
        for mm in range(2):
            m = mg * 2 + mm
            for tb in range(2):
                sl = slice(tb * 512, (tb + 1) * 512)
                ob = 4 + cnt % 2
                cnt += 1
                for j in range(FC):
                    P.op(pe, lambda e, bv=bv, mm=mm, j=j, sl=sl, ob=ob: e.matmul(
                        ps[ob][:, :], lhsT=bv[:, j, mm * 128:(mm + 1) * 128], rhs=actb[:, j, sl],
                        start=(j == 0), stop=(j == FC - 1)), r=[key, ("actb", j)], w=[PSK[ob]], inc=(j == FC - 1))
                gc = 5 * 8 + m
                P.op(dve, lambda v, ob=ob, m=m, sl=sl, gc=gc: v.scalar_tensor_tensor(
                    out=xT[:, m, sl], in0=ps[ob][:, :], scalar=mods[:, li, gc:gc + 1], in1=xT[:, m, sl],
                    op0=ALU.mult, op1=ALU.add), r=[PSK[ob], "mods", "xT"], w=["xT"])


def _conv_mixer(E, li, ti):
    P = E["P"]; pe, act, dve, pool = P.pe, P.act, P.dve, P.pool
    xT, hb, scr, tmpf, gel, ps, PSK, mods = E["xT"], E["hb"], E["scr"], E["tmpf"], E["gel"], E["ps"], E["PSK"], E["mods"]
    cw, halo = E["cw"], E["halo"]
    ringA, ringB = E["ringA"], E["ringB"]
    jl = li // 2
    mbuf = scr[:, 0:4096].bitcast(BF16).rearrange("p (k n) -> p k n", k=8)
    cvb = [scr[:, 4096 + i * 1032: 4096 + (i + 1) * 1032].rearrange("p (s c) -> p s c", s=8) for i in range(2)]
    accb = [scr[:, 6400 + i * 1024: 6400 + (i + 1) * 1024] for i in range(2)]
    wi = E["cwi_d"][jl].rearrange("(k p) n -> p k n", p=128)
    wo = E["cwo_d"][jl].rearrange("(k p) n -> p k n", p=128)
    for m in range(8):
        buf, sem, key = ringA.next()
        bv = buf[:, 0:3072].rearrange("p (g k n) -> p g k n", g=3, k=8)
        P.dma_multi(pool, [(bv[:, g], wi[:, :, g * D + m * 128: g * D + (m + 1) * 128]) for g in range(3)], sem, w=[key])
        cv = cvb[m % 2]
        acc = accb[m % 2]
        bgb = gel[:, m % 2, :]
        ck, ak, bk_ = ("cvb", m % 2), ("acc", m % 2), ("gel", m % 2)
        for tb in range(2):
            sl = slice(tb * 512, (tb + 1) * 512)
            base = 3 * ((2 * m + tb) % 2)
            for g in range(3):
                for k in range(8):
                    P.op(pe, lambda e, bv=bv, g=g, k=k, sl=sl, base=base: e.matmul(
                        ps[base + g][:, :], lhsT=bv[:, g, k, :], rhs=hb[:, k, sl], start=(k == 0), stop=(k == 7)),
                        r=[key, "hb"], w=[PSK[base + g]], inc=(k == 7))
            tbuf = (2 * m + tb) % 4
            P.op(act, lambda a, base=base, sl=sl, bgb=bgb: a.activation(out=bgb[:, sl], in_=ps[base][:, :], func=AF.Copy),
                 r=[PSK[base]], w=[bk_])
            P.op(act, lambda a, base=base, tbuf=tbuf: a.activation(out=tmpf[:, tbuf, :], in_=ps[base + 1][:, :], func=AF.Copy),
                 r=[PSK[base + 1]], w=[("tmpf", tbuf)])
            P.op(dve, lambda v, base=base, tbuf=tbuf, cv=cv, tb=tb: v.tensor_tensor(
                out=cv[:, 4 * tb:4 * tb + 4, 1:129], in0=ps[base + 2][:, :].rearrange("p (s c) -> p s c", s=4),
                in1=tmpf[:, tbuf, :].rearrange("p (s c) -> p s c", s=4), op=ALU.mult),
                r=[PSK[base + 2], ("tmpf", tbuf)], w=[ck])
        P.op(dve, lambda v, cv=cv, m=m: v.tensor_copy(out=cv[:, :, 0], in_=halo[:, jl, m, :]), r=["halo", ck], w=[ck])
        P.op(dve, lambda v, cv=cv, m=m: v.tensor_copy(out=halo[:, jl, m, :], in_=cv[:, :, 128]), r=[ck, "halo"], w=["halo"])
        a3 = acc.rearrange("p (s c) -> p s c", s=8)
        w0 = cw[:, jl, m, 0:1]; w1 = cw[:, jl, m, 1:2]; w2 = cw[:, jl, m, 2:3]
        P.op(dve, lambda v, a3=a3, cv=cv, w2=w2: v.tensor_scalar(out=a3, in0=cv[:, :, 1:129], scalar1=w2, scalar2=None,
                                                                op0=ALU.mult), r=[ck, "cw"], w=[ak])
        for (o_, i_, wv) in ((a3[:, 1:8, :], cv[:, 0:7, 1:129], w1), (a3[:, 0:1, :], cv[:, 7:8, 0:128], w1),
                             (a3[:, 2:8, :], cv[:, 0:6, 1:129], w0), (a3[:, 0:2, :], cv[:, 6:8, 0:128], w0)):
            P.op(dve, lambda v, o_=o_, i_=i_, wv=wv: v.scalar_tensor_tensor(
                out=o_, in0=i_, scalar=wv, in1=o_, op0=ALU.mult, op1=ALU.add), r=[ck, ak, "cw"], w=[ak])
        P.op(dve, lambda v, acc=acc, bgb=bgb, m=m: v.tensor_tensor(out=mbuf[:, m, :], in0=acc, in1=bgb, op=ALU.mult),
             r=[ak, bk_], w=[("mbuf", m)])
    cnt = 0
    for mg in range(4):
        buf, sem, key = ringB.next()
        bv = buf[:, 0:2048].rearrange("p (k n) -> p k n", k=8)
        P.dma(pool, bv, wo[:, :, mg * 256:(mg + 1) * 256], sem, w=[key])
        for mm in range(2):
            m = mg * 2 + mm
            for tb in range(2):
                sl = slice(tb * 512, (tb + 1) * 512)
                ob = 6 + cnt % 2
                cnt += 1
                for k in range(8):
                    P.op(pe, lambda e, bv=bv, mm=mm, k=k, sl=sl, ob=ob: e.matmul(
                        ps[ob][:, :], lhsT=bv[:, k, mm * 128:(mm + 1) * 128], rhs=mbuf[:, k, sl],
                        start=(k == 0), stop=(k == 7)), r=[key, ("mbuf", k)], w=[PSK[ob]], inc=(k == 7))
                gc = 2 * 8 + m
                P.op(dve, lambda v, ob=ob, m=m, sl=sl, gc=gc: v.scalar_tensor_tensor(
                    out=xT[:, m, sl], in0=ps[ob][:, :], scalar=mods[:, li, gc:gc + 1], in1=xT[:, m, sl],
                    op0=ALU.mult, op1=ALU.add), r=[PSK[ob], "mods", "xT"], w=["xT"])


def _ssm_mixer(E, li, ti):
    P = E["P"]; pe, act, dve, pool, sync = P.pe, P.act, P.dve, P.pool, P.sync
    xT, hb, scr, tmpf, gel, ps, PSK, mods = E["xT"], E["hb"], E["scr"], E["tmpf"], E["gel"], E["ps"], E["PSK"], E["mods"]
    hprev, sstate, Ca, Cb, dd = E["hprev"], E["sstate"], E["Ca"], E["Cb"], E["dd"]
    T1, T2, T3 = E["T1"], E["T2"], E["T3"]
    ringA = E["ringA"]
    jl = li // 2
    XS = scr[:, 0:8256].rearrange("p (r c) -> p r c", r=64)
    XSb = scr[:, 8256:12352].bitcast(BF16).rearrange("p (r c) -> p r c", r=64)
    ybuf = scr[:, 12352:16448].bitcast(BF16).rearrange("p (k n) -> p k n", k=8)
    hv = hb[:, :, :].rearrange("p k (s c) -> p k s c", s=8)
    P.op(dve, lambda v: v.tensor_copy(out=XS[:, :, 0], in_=sstate[:, jl, :]), r=["sstate"], w=["XS"])
    for k in range(8):
        buf, sem, key = ringA.next(hw=True)
        bv = buf[:, 0:2048].rearrange("p (s r c) -> p s r c", s=8, r=2)
        P.dma(sync, buf[:, 0:2048], E["wbd"][jl, k], sem, r=[("wbd", jl)], w=[key])
        for ri in range(2):
            for s in range(8):
                for jq in range(4):
                    p0 = 32 * jq
                    P.op(pe, lambda e, bv=bv, p0=p0, ri=ri, s=s, k=k, jq=jq: e.matmul(
                        ps[ri * 4 + jq][:, 0:128], lhsT=bv[p0:p0 + 32, s, ri, :],
                        rhs=hv[p0:p0 + 32, k, s, :], start=(s == 0), stop=(s == 7), tile_position=(p0, 0)),
                        r=[key, "hb"], w=[PSK[ri * 4 + jq]], inc=(s == 7 and jq == 3))
            for jq in range(4):
                pi_ = 4 * k + jq
                dst = XS[:, ri * 32 + pi_, 1:129]
                src = ps[ri * 4 + jq][:, 0:128]
                if jq % 2 == 0:
                    P.op(act, lambda a, dst=dst, src=src: a.activation(out=dst, in_=src, func=AF.Copy),
                         r=[PSK[ri * 4 + jq]], w=["XS"])
                else:
                    P.op(dve, lambda v, dst=dst, src=src: v.tensor_copy(out=dst, in_=src), r=[PSK[ri * 4 + jq]], w=["XS"])
    def mm(out, lhsT, rhs, rk, start=False, bank=0, inc=False, tp=None):
        if tp is None:
            P.op(pe, lambda e: e.matmul(out, lhsT=lhsT, rhs=rhs, start=start, stop=inc, skip_group_check=True),
                 r=rk, w=[PSK[bank]], inc=inc)
        else:
            P.op(pe, lambda e: e.matmul(out, lhsT=lhsT, rhs=rhs, start=start, stop=inc, skip_group_check=True,
                                        tile_position=tp),
                 r=rk, w=[PSK[bank]], inc=inc)

    def near(k, wtv, yb, rk):
        for hf in range(2):
            mm(ps[yb[hf]][:, :], wtv[:, 0, :], hb[:, k, hf * 512:(hf + 1) * 512], rk, start=True, bank=yb[hf])
        for tau in range(1, 8):
            lo = tau * 128
            if lo < 512:
                mm(ps[yb[0]][:, lo:512], wtv[:, tau, :], hb[:, k, 0:512 - lo], rk, bank=yb[0])
            lo2 = max(512, lo)
            mm(ps[yb[1]][:, lo2 - 512:512], wtv[:, tau, :], hb[:, k, lo2 - lo:1024 - lo], rk, bank=yb[1])
            for s in range(tau):
                hf, sc = s // 4, s % 4
                src_s = s + 8 - tau
                mm(ps[yb[hf]][:, sc * 128 + 1:sc * 128 + 128], wtv[:, tau, :],
                   hb[:, k, src_s * 128:src_s * 128 + 127], rk, bank=yb[hf])
                mm(ps[yb[hf]][:, sc * 128:sc * 128 + 1], wtv[:, tau, :], hprev[:, jl, k, src_s:src_s + 1], rk, bank=yb[hf])

    def far(k, cbv, yb, rk):
        for s in range(8):
            hf, sc = s // 4, s % 4
            for ri in range(2):
                for jq in range(4):
                    pi_ = 4 * k + jq
                    last = (jq == 3 and s in (3, 7) and ri == 1)
                    mm(ps[yb[hf]][32 * jq:32 * jq + 32, sc * 128:(sc + 1) * 128], cbv[:, jq, s, ri, :],
                       XSb[:, ri * 32 + pi_, :], rk, bank=yb[hf], inc=last, tp=(0, 32 * jq))

    def evac(k, yb):
        for hf in range(2):
            sl = slice(hf * 512, (hf + 1) * 512)
            g_ = gel[:, hf, 0:512]
            P.op(dve, lambda v, hf=hf, sl=sl, k=k, g_=g_, yb=yb: v.scalar_tensor_tensor(
                out=g_, in0=hb[:, k, sl], scalar=dd[:, jl, k:k + 1], in1=ps[yb[hf]][:, :],
                op0=ALU.mult, op1=ALU.add), r=["hb", "dd", PSK[yb[hf]]], w=[("gel", hf)])
            P.op(act, lambda a, sl=sl, k=k, g_=g_: a.activation(out=ybuf[:, k, sl], in_=g_, func=AF.Gelu_apprx_tanh),
                 r=[("gel", hf)], w=[("ybuf", k)])

    bufW, semW, keyW = ringA.next(hw=True)
    P.dma(sync, bufW[:, :].rearrange("p (k n) -> p k n", k=4), E["wtd"][jl, 0:4].rearrange("k p n -> p k n"), semW,
          r=[("wtd", jl)], w=[keyW])
    for k in range(4):
        wtv = bufW[:, k * 1024:(k + 1) * 1024].rearrange("p (t c) -> p t c", t=8)
        near(k, wtv, [2 * k, 2 * k + 1], [keyW, "hb", "hprev"])
    xs_t = XS.tensor
    pstep = XS.ap[0][0]
    for c in range(128 if not (DBG & 1) else 0):
        cur = XS[:, :, c]
        nxt = XS[:, :, c + 1]
        swp = bass.AP(xs_t, XS.offset + 32 * 129 + c, [[pstep, 128], [-32 * 129, 2], [129, 32]])
        P.op(dve, lambda v, cur=cur: v.tensor_tensor(out=T1[:], in0=cur, in1=Ca[:, jl, :], op=ALU.mult),
             r=["XS", "Ca"], w=["T1"])
        P.op(dve, lambda v, swp=swp: v.tensor_tensor(out=T2[:].rearrange("p (r c) -> p r c", r=2), in0=swp,
                                                     in1=Cb[:, jl, :].rearrange("p (r c) -> p r c", r=2), op=ALU.mult),
             r=["XS", "Cb"], w=["T2"])
        P.op(dve, lambda v: v.tensor_tensor(out=T3[:], in0=T1[:], in1=T2[:], op=ALU.add), r=["T1", "T2"], w=["T3"])
        P.op(dve, lambda v, nxt=nxt: v.tensor_tensor(out=nxt, in0=T3[:], in1=nxt, op=ALU.add), r=["T3", "XS"], w=["XS"])
    P.op(dve, lambda v: v.tensor_copy(out=sstate[:, jl, :], in_=XS[:, :, 128]), r=["XS"], w=["sstate"])
    for q4 in range(4):
        P.op(act, lambda a, q4=q4: a.activation(out=XSb[:, 16 * q4:16 * q4 + 16, :], in_=XS[:, 16 * q4:16 * q4 + 16, 0:128],
                                                 func=AF.Copy), r=["XS"], w=["XSb"])
    for k in range(8):
        buf, sem, key = ringA.next(hw=True)
        cbv = buf[:, 0:2048].rearrange("p (a s r c) -> p a s r c", a=4, s=8, r=2)
        yb = [2 * (k % 4), 2 * (k % 4) + 1]
        if k < 4:
            P.dma(sync, buf[:, 0:2048], E["cbd"][jl, k], sem, r=[("cbd", jl)], w=[key])
        else:
            wtv = buf[:, 2048:3072].rearrange("p (t c) -> p t c", t=8)
            P.dma_multi(sync, [(buf[:, 0:2048], E["cbd"][jl, k]), (buf[:, 2048:3072], E["wtd"][jl, k])], sem,
                        r=[("cbd", jl), ("wtd", jl)], w=[key])
            near(k, wtv, yb, [key, "hb", "hprev"])
        far(k, cbv, yb, [key, "XSb"])
        evac(k, yb)
    P.op(dve, lambda v: v.tensor_copy(out=hprev[:, jl, :, :], in_=hv[:, :, :, 127]), r=["hb", "hprev"], w=["hprev"])
    wo = E["swo_d"][jl].rearrange("(k p) n -> p k n", p=128)
    cnt = 0
    for mg in range(4):
        buf, sem, key = ringA.next()
        bv = buf[:, :].rearrange("p (g k n) -> p g k n", g=2, k=8)
        P.dma_multi(pool, [(bv[:, 0], wo[:, :, mg * 256:(mg + 1) * 256]),
                           (bv[:, 1], wo[:, :, D + mg * 256:D + (mg + 1) * 256])], sem, w=[key])
        for mm_ in range(2):
            m = mg * 2 + mm_
            for tb in range(2):
                sl = slice(tb * 512, (tb + 1) * 512)
                vb = 4 + cnt % 2
                gb = 6 + cnt % 2
                cnt += 1
                for g, bnk in ((0, vb), (1, gb)):
                    for k in range(8):
                        P.op(pe, lambda e, bv=bv, g=g, mm_=mm_, k=k, sl=sl, bnk=bnk: e.matmul(
                            ps[bnk][:, :], lhsT=bv[:, g, k, mm_ * 128:(mm_ + 1) * 128], rhs=ybuf[:, k, sl],
                            start=(k == 0), stop=(k == 7)), r=[key, ("ybuf", k)], w=[PSK[bnk]], inc=(k == 7))
                tbuf = cnt % 4
                P.op(act, lambda a, gb=gb, tbuf=tbuf: a.activation(out=tmpf[:, tbuf, :], in_=ps[gb][:, :], func=AF.Sigmoid),
                     r=[PSK[gb]], w=[("tmpf", tbuf)])
                P.op(dve, lambda v, vb=vb, tbuf=tbuf: v.tensor_tensor(out=tmpf[:, tbuf, :], in0=ps[vb][:, :],
                                                                      in1=tmpf[:, tbuf, :], op=ALU.mult),
                     r=[PSK[vb], ("tmpf", tbuf)], w=[("tmpf", tbuf)])
                gc = 2 * 8 + m
                P.op(dve, lambda v, tbuf=tbuf, m=m, sl=sl, gc=gc: v.scalar_tensor_tensor(
                    out=xT[:, m, sl], in0=tmpf[:, tbuf, :], scalar=mods[:, li, gc:gc + 1], in1=xT[:, m, sl],
                    op0=ALU.mult, op1=ALU.add), r=[("tmpf", tbuf), "mods", "xT"], w=["xT"])


def _prep_inputs(inp, b):
    f = np.float32
    g = lambda a: np.ascontiguousarray(np.asarray(a, dtype=f))

    def pk(v):
        v = np.asarray(v, dtype=f)
        lead = v.shape[:-1]
        return g(np.moveaxis(v.reshape(lead + (8, 128)), -1, 0))

    def gp(a):
        a = np.asarray(a, dtype=f).reshape(2, 32, 2, 64)
        return g(a.transpose(2, 3, 0, 1).reshape(128, 2, 32))
    m = {}
    m["x"] = g(inp["x"][b])
    m["c"] = pk(inp["c"][b])
    m["n1g"] = pk(inp["norm1_g"])
    m["n2g"] = pk(inp["norm2_g"])
    m["fing"] = pk(inp["final_g"])
    m["w_ada"] = g(inp["w_ada"])
    m["b_ada"] = g(np.asarray(inp["b_ada"], dtype=f).reshape(4, 48, 128).transpose(2, 0, 1))
    m["lre"] = gp(inp["ssm_a_re"])
    m["lim"] = gp(inp["ssm_a_im"])
    ls = np.broadcast_to(np.asarray(inp["ssm_log_step"], dtype=f)[:, :, None], (2, 64, 64))
    m["lst"] = gp(ls)
    for nm, src in (("bre", "ssm_b_re"), ("bim", "ssm_b_im")):
        a = np.asarray(inp[src], dtype=f).reshape(2, 32, 2, 64, 16)
        m[nm] = g(a.transpose(2, 3, 0, 1, 4).reshape(128, 2, 512))
    for nm, src in (("crt", "ssm_c_re"), ("cit", "ssm_c_im")):
        a = np.asarray(inp[src], dtype=f).reshape(2, 32, 2, 16, 64)
        m[nm] = g(a.transpose(2, 4, 0, 1, 3).reshape(128, 2, 512))
    m["dd"] = pk(inp["ssm_d"])
    m["ssm_w_out"] = g(inp["ssm_w_out"])
    m["conv_w_in"] = g(inp["conv_w_in"])
    cwv = np.asarray(inp["conv_w"], dtype=f).reshape(2, 3, 8, 128)
    m["cw"] = g(cwv.transpose(3, 0, 2, 1))
    m["conv_w_out"] = g(inp["conv_w_out"])
    m["w_ffn_in"] = g(inp["w_ffn_in"])
    m["w_ffn_out"] = g(inp["w_ffn_out"])
    return m


_NC_CACHE = {}


def kernel(**inputs):
    if "full" not in _NC_CACHE:
        _NC_CACHE["full"] = build()
    nc = _NC_CACHE["full"]
    n = 8
    in_maps = [_prep_inputs(inputs, b) for b in range(n)]
    res = run_bass_kernel_spmd(nc, in_maps, core_ids=list(range(n)))
    out = np.stack([np.asarray(r["y"], dtype=np.float32) for r in res.results], axis=0)
    return out
```

```python
import contextlib
import math
import os
DBG = int(os.environ.get('KDBG', '0'))
OPT = int(os.environ.get('KOPT', '1'))
import numpy as np
import concourse.bass as bass
import concourse.mybir as mybir
from concourse.bass_utils import run_bass_kernel_spmd

F32 = mybir.dt.float32
BF16 = mybir.dt.bfloat16
ALU = mybir.AluOpType
AF = mybir.ActivationFunctionType

D = 1024
KC = 8
TT = 1024
SEQ = 4096
FF = 2816
FC = 22
EPS = 1e-6


class Sem:
    def __init__(self, name):
        self.name = name
        self.handle = None
        self.count = 0


class Q:
    def __init__(self, prog, name):
        self.prog = prog
        self.name = name
        self.items = []
        self.sem = prog.sem("q_" + name)
        self.waited = {}
        self.pending = []
        self.last = None

    def wait(self, ev):
        if ev is None:
            return
        s, v = ev
        if self.waited.get(s, 0) >= v:
            return
        self.waited[s] = v
        self.items.append(lambda e, s=s, v=v: e.wait_ge(s.handle, v))


class Prog:
    def __init__(self, nc):
        self.nc = nc
        self.sems = []
        self.q = {}
        for n in ["sync", "scalar", "vector", "gpsimd", "tensor"]:
            self.q[n] = Q(self, n)
        self.sync = self.q["sync"]
        self.act = self.q["scalar"]
        self.dve = self.q["vector"]
        self.pool = self.q["gpsimd"]
        self.pe = self.q["tensor"]
        self.state = {}
        self.dma_events = []

    def sem(self, name):
        s = Sem(name)
        self.sems.append(s)
        return s

    def _st(self, k):
        st = self.state.get(k)
        if st is None:
            st = {"w": None, "r": []}
            self.state[k] = st
        return st

    def _deps(self, q, r, w):
        for k in r:
            q.wait(self._st(k)["w"])
        for k in w:
            st = self._st(k)
            q.wait(st["w"])
            for ev in st["r"]:
                q.wait(ev)

    def _commit(self, ev, r, w):
        for k in r:
            self._st(k)["r"].append(ev)
        for k in w:
            self.state[k] = {"w": ev, "r": []}

    def op(self, q, fn, r=(), w=(), inc=True):
        self._deps(q, r, w)
        if inc:
            s = q.sem
            s.count += 1
            ev = (s, s.count)
            q.items.append(lambda e, fn=fn, s=s: fn(e).then_inc(s.handle, 1))
            for (pr, pw) in q.pending:
                self._commit(ev, pr, pw)
            q.pending = []
            self._commit(ev, r, w)
            q.last = ev
            return ev
        q.items.append(lambda e, fn=fn: fn(e))
        q.pending.append((tuple(r), tuple(w)))
        return None

    def dma(self, q, out, in_, sem, r=(), w=()):
        self._deps(q, r, w)
        sem.count += 16
        ev = (sem, sem.count)
        q.items.append(lambda e, out=out, in_=in_, sem=sem: e.dma_start(out=out, in_=in_).then_inc(sem.handle, 16))
        self._commit(ev, r, w)
        self.dma_events.append(ev)
        return ev

    def dma_multi(self, q, pairs, sem, r=(), w=()):
        self._deps(q, r, w)
        for (out, in_) in pairs:
            sem.count += 16
            q.items.append(lambda e, out=out, in_=in_, sem=sem: e.dma_start(out=out, in_=in_).then_inc(sem.handle, 16))
        ev = (sem, sem.count)
        self._commit(ev, r, w)
        return ev

    def barrier(self, queues=None):
        qs = queues or [self.pe, self.act, self.dve, self.pool]
        evs = [q.last for q in qs if q.last is not None]
        for q in qs:
            for ev in evs:
                if ev[0] is not q.sem:
                    q.wait(ev)

    def emit(self):
        nc = self.nc
        with contextlib.ExitStack() as st:
            for s in self.sems:
                s.handle = st.enter_context(nc.semaphore(s.name))
            block = st.enter_context(nc.Block())
            for n in ["sync", "scalar", "vector", "gpsimd", "tensor"]:
                items = self.q[n].items

                def body(eng, items=items):
                    for it in items:
                        it(eng)
                getattr(block, n)(body)


class Ring:
    def __init__(self, P, name, bufs):
        self.P = P
        self.bufs = bufs
        self.sems = [P.sem(f"{name}{i}") for i in range(len(bufs))]
        self.sems_hw = [P.sem(f"{name}h{i}") for i in range(len(bufs))]
        self.keys = [(name, i) for i in range(len(bufs))]
        self.i = 0

    def next(self, hw=False):
        i = self.i % len(self.bufs)
        self.i += 1
        return self.bufs[i], (self.sems_hw[i] if hw else self.sems[i]), self.keys[i]


def build(ntiles=4, layers=(0, 1, 2, 3), do_final=True):
    nc = bass.Bass("TRN2", target_bir_lowering=False)
    P = Prog(nc)
    pe, act, dve, pool, sync = P.pe, P.act, P.dve, P.pool, P.sync

    def din(name, shape, dt=F32):
        return nc.dram_tensor(name, list(shape), dt, kind="ExternalInput").ap()

    x_d = din("x", [SEQ, D])
    c_d = din("c", [128, 8])
    n1g_d = din("n1g", [128, 4, 8])
    n2g_d = din("n2g", [128, 4, 8])
    fing_d = din("fing", [128, 8])
    wada_d = din("w_ada", [4, D, 6 * D])
    bada_d = din("b_ada", [128, 4, 48])
    lre_d = din("lre", [128, 2, 32])
    lim_d = din("lim", [128, 2, 32])
    lst_d = din("lst", [128, 2, 32])
    bre_d = din("bre", [128, 2, 512])
    bim_d = din("bim", [128, 2, 512])
    crt_d = din("crt", [128, 2, 512])
    cit_d = din("cit", [128, 2, 512])
    dd_d = din("dd", [128, 2, 8])
    swo_d = din("ssm_w_out", [2, D, 2 * D])
    cwi_d = din("conv_w_in", [2, D, 3 * D])
    cw_d = din("cw", [128, 2, 8, 3])
    cwo_d = din("conv_w_out", [2, D, D])
    wfi_d = din("w_ffn_in", [4, D, 2 * FF])
    wfo_d = din("w_ffn_out", [4, FF, D])
    y_d = nc.dram_tensor("y", [SEQ, D], F32, kind="ExternalOutput").ap()
    wbd = nc.dram_tensor("wbd", [2, 8, 128, 2048], BF16, kind="Internal").ap()
    cbd = nc.dram_tensor("cbd", [2, 8, 128, 2048], BF16, kind="Internal").ap()
    wtd = nc.dram_tensor("wtd", [2, 8, 128, 1024], BF16, kind="Internal").ap()

    with contextlib.ExitStack() as st:
        def sb(name, shape, dt=F32):
            return st.enter_context(nc.sbuf_tensor("sb_" + name, list(shape), dt))

        xT = sb("xT", [128, 8, 1024])
        hb = sb("hb", [128, 8, 1024], BF16)
        hprev = sb("hprev", [128, 2, 8, 8], BF16)
        scr = sb("scr", [128, 16448])
        wA = [sb(f"wA{i}", [128, 4096], BF16) for i in range(3)]
        wB = [sb(f"wB{i}", [128, 5632], BF16) for i in range(2)]
        xin = sb("xin", [128, 2, 1024])
        sq = sb("sq", [128, 8, 512], BF16)
        rstd = sb("rstd", [128, 2, 512])
        tmpf = sb("tmpf", [128, 4, 512])
        gel = sb("gel", [128, 2, 1024])
        ident = sb("ident", [128, 128])
        ones_bf = sb("ones_bf", [128, 128], BF16)
        mods = sb("mods", [128, 4, 48])
        gsc = sb("gsc", [128, 4, 2, 8])
        n1g = sb("n1g", [128, 4, 8])
        n2g = sb("n2g", [128, 4, 8])
        fing = sb("fing", [128, 8])
        fing32 = sb("fing32", [128, 8])
        bada = sb("bada", [128, 4, 48])
        cin = sb("cin", [128, 8])
        cact = sb("cact", [128, 8])
        cw = sb("cw", [128, 2, 8, 3])
        dd = sb("dd", [128, 2, 8])
        halo = sb("halo", [128, 2, 8, 8])
        sstate = sb("sstate", [128, 2, 64])
        Ca = sb("Ca", [128, 2, 64])
        Cb = sb("Cb", [128, 2, 64])
        T1 = sb("T1", [128, 64])
        T2 = sb("T2", [128, 64])
        T3 = sb("T3", [128, 64])
        ps = [st.enter_context(nc.psum_tensor(f"ps{i}", [128, 512], F32)) for i in range(8)]
        PSK = [("ps", i) for i in range(8)]

        ringA = Ring(P, "wA", wA)
        ringB = Ring(P, "wB", wB)
        s_small = P.sem("small")
        s_small2 = P.sem("small2")
        s_xin = [P.sem("xin0"), P.sem("xin1")]
        s_out = [P.sem("out0"), P.sem("out1")]
        s_scr = [P.sem("scrd0"), P.sem("scrd1"), P.sem("scrd2")]

        small = [(cin, c_d, "cin"), (n1g, n1g_d, "n1g"), (n2g, n2g_d, "n2g"), (fing, fing_d, "fing"),
                 (bada, bada_d, "bada"), (cw, cw_d, "cw"), (dd, dd_d, "dd")]
        P.dma_multi(sync, [(t[:], d_) for (t, d_, _) in small], s_small, w=[k_ for (_, _, k_) in small])

        P.op(pool, lambda g: g.memset(ident[:], 1.0), w=["ident"])
        P.op(pool, lambda g: g.affine_select(out=ident[:], in_=ident[:], pattern=[[-1, 128]],
                                             compare_op=ALU.is_equal, fill=0.0, base=0, channel_multiplier=1),
             r=["ident"], w=["ident"])
        P.op(pool, lambda g: g.memset(ones_bf[:], 1.0), w=["ones"])
        P.op(pool, lambda g: g.memset(hprev[:], 0.0), w=["hprev"])
        P.op(pool, lambda g: g.memset(halo[:], 0.0), w=["halo"])
        P.op(pool, lambda g: g.memset(sstate[:], 0.0), w=["sstate"])

        P.op(act, lambda a: a.activation(out=cact[:], in_=cin[:], func=AF.Silu), r=["cin"], w=["cact"])
        adab = [xin[:, :, :].rearrange("p a (k n) -> p (a k) n", k=4),
                sq[:, :, :].rearrange("p k n -> p (k n)").bitcast(F32).rearrange("p (k n) -> p k n", k=8)]
        s_ada = [P.sem("ada0"), P.sem("ada1")]

        rowsb = rstd[0:1, :, 0:256]

        def ada_gen():
            nb = 0
            deferred = None
            for li in range(4):
                wv = wada_d[li].rearrange("(k p) n -> p k n", p=128)
                mb = 6 + li % 2

                def flush(d):
                    b_, blk_, mb_ = d
                    for jj in range(2):
                        j = blk_ * 2 + jj
                        P.op(pe, lambda e, b_=b_, jj=jj, j=j, mb_=mb_: e.transpose(
                            out=ps[mb_][:, j:j + 1], in_=rowsb[0:1, b_, jj * 128:(jj + 1) * 128], identity=ident[0:1, 0:1]),
                            r=[("rowsb", b_), "ident"], w=[PSK[mb_]], inc=(jj == 1))
                for blk in range(24):
                    b = nb % 2
                    rb = 4 + nb % 2
                    nb += 1
                    akeys = [[("xin", 0), ("xin", 1)], ["sq"]][b]
                    P.dma(sync, adab[b], wv[:, :, blk * 256:(blk + 1) * 256], s_ada[b], w=akeys)
                    for k in range(8):
                        P.op(pe, lambda e, rb=rb, k=k, b=b: e.matmul(
                            ps[rb][0:1, 0:256], lhsT=cact[:, k:k + 1], rhs=adab[b][:, k, :],
                            start=(k == 0), stop=(k == 7)),
                            r=akeys + ["cact"], w=[PSK[rb]], inc=(k == 7))
                    if deferred is not None:
                        flush(deferred)
                    P.op(act, lambda a_, rb=rb, b=b: a_.activation(out=rowsb[0:1, b, :], in_=ps[rb][0:1, 0:256], func=AF.Copy),
                         r=[PSK[rb]], w=[("rowsb", b)])
                    deferred = (b, blk, mb)
                    yield
                flush(deferred)
                deferred = None
                P.op(dve, lambda v, li=li, mb=mb: v.tensor_tensor(out=mods[:, li, :], in0=ps[mb][:, 0:48], in1=bada[:, li, :],
                                                                  op=ALU.add), r=[PSK[mb], "bada"], w=["mods"])
                for which, ng, ngk in ((0, n1g, "n1g"), (1, n2g, "n2g")):
                    sc = mods[:, li, (1 + 3 * which) * 8:(2 + 3 * which) * 8]
                    P.op(dve, lambda v, li=li, which=which, ng=ng, sc=sc: v.scalar_tensor_tensor(
                        out=gsc[:, li, which, :], in0=sc, scalar=1.0, in1=ng[:, li, :], op0=ALU.add, op1=ALU.mult),
                        r=["mods", ngk], w=["gsc"])
                    P.op(dve, lambda v, li=li, which=which: v.tensor_scalar(
                        out=gsc[:, li, which, :], in0=gsc[:, li, which, :], scalar1=32.0, scalar2=None, op0=ALU.mult),
                        r=["gsc"], w=["gsc"])
            P.op(dve, lambda v: v.tensor_scalar(out=fing32[:], in0=fing[:], scalar1=32.0, scalar2=None, op0=ALU.mult),
                 r=["fing"], w=["fing32"])
        ada_it = ada_gen()
        if not (OPT & 1):
            for _ in ada_it:
                pass
            P.barrier()

        ssm_layers = [l for l in layers if l % 2 == 0]
        if ssm_layers and not (DBG & 2):
            _ssm_prologue(nc, P, st, locals())
        for _ in ada_it:
            pass
        P.barrier()

        env = dict(locals())
        for ti in range(ntiles):
            _load_tile(env, ti)
            for li in layers:
                bar = (lambda: None) if (OPT & 2) else (lambda: P.barrier([pe, act, dve]))
                _norm_mod(env, li, 0)
                bar()
                if li % 2 == 0:
                    _ssm_mixer(env, li, ti)
                else:
                    _conv_mixer(env, li, ti)
                bar()
                _norm_mod(env, li, 1)
                bar()
                _ffn(env, li)
                bar()
            _final_store(env, ti, do_final)
            bar()
        for s in s_out:
            if s.count:
                sync.wait((s, s.count))
        P.emit()
    return nc


def _ssm_prologue(nc, P, st, E):
    pe, act, dve, pool, sync = P.pe, P.act, P.dve, P.pool, P.sync
    scr, hb, ps, PSK, ident = E["scr"], E["hb"], E["ps"], E["PSK"], E["ident"]
    Ca, Cb = E["Ca"], E["Cb"]
    wbd, cbd, wtd = E["wbd"], E["cbd"], E["wtd"]
    s_scr = E["s_scr"]
    s_small = E["s_small"]

    def sb(name, shape, dt=F32):
        return st.enter_context(nc.sbuf_tensor("sp_" + name, list(shape), dt))

    class V:
        def __init__(self, name, ap):
            self.name = name
            self.ap = ap

        def __getitem__(self, k):
            return self.ap[k]

    t32c = {}

    def t32(nm):
        if nm not in t32c:
            t32c[nm] = sb("t_" + nm, [128, 32])
        return t32c[nm]

    xTf = E["xT"][:, :, :].rearrange("p k n -> p (k n)")
    gelf = E["gel"][:, :, :].rearrange("p a n -> p (a n)")
    tmpff = E["tmpf"][:, :, :].rearrange("p a n -> p (a n)")
    CBall = scr[:, 0:8192].bitcast(BF16).rearrange("p (a s r c) -> p a s r c", a=32, s=8, r=2)
    WBall = scr[:, 8192:16384].bitcast(BF16).rearrange("p (k s r c) -> p k s r c", k=8, s=8, r=2)
    WTall = hb[:, :, :].rearrange("p k (t c) -> p k t c", t=8)
    lre = sb("lre", [128, 2, 32]); lim = sb("lim", [128, 2, 32]); lst = sb("lst", [128, 2, 32])
    bre = V("bre", xTf[:, 0:1024].rearrange("p (j n) -> p j n", j=2))
    bim = V("bim", xTf[:, 1024:2048].rearrange("p (j n) -> p j n", j=2))
    crt = V("crt", xTf[:, 2048:3072].rearrange("p (j n) -> p j n", j=2))
    cit = V("cit", xTf[:, 3072:4096].rearrange("p (j n) -> p j n", j=2))
    Bpr = V("Bpr", xTf[:, 4096:5120].rearrange("p (a c) -> p a c", a=32))
    Bpi = V("Bpi", xTf[:, 5120:6144].rearrange("p (a c) -> p a c", a=32))
    Cpr = V("Cpr", xTf[:, 6144:7168].rearrange("p (a c) -> p a c", a=32))
    nCpi = V("nCpi", xTf[:, 7168:8192].rearrange("p (a c) -> p a c", a=32))
    Bbr = V("Bbr", gelf[:, 0:512].rearrange("p (a c) -> p a c", a=32))
    Bbi = V("Bbi", gelf[:, 512:1024].rearrange("p (a c) -> p a c", a=32))
    U1 = V("U1", gelf[:, 1024:1536].rearrange("p (a c) -> p a c", a=32))
    U2 = V("U2", gelf[:, 1536:2048].rearrange("p (a c) -> p a c", a=32))
    nCr = V("nCr", tmpff[:, 0:512].rearrange("p (a c) -> p a c", a=32))
    nCi = V("nCi", tmpff[:, 512:1024].rearrange("p (a c) -> p a c", a=32))
    P.dma_multi(sync, [(t[:], d_) for t, d_ in ((lre, E["lre_d"]), (lim, E["lim_d"]), (lst, E["lst_d"]), (bre, E["bre_d"]),
                                                  (bim, E["bim_d"]), (crt, E["crt_d"]), (cit, E["cit_d"]))], E["s_small2"],
                w=["lre", "lim", "lst", "bre", "bim", "crt", "cit"])

    def tt(out, a, b, op, r, w, q=None):
        return P.op(q or dve, lambda v: v.tensor_tensor(out=out, in0=a, in1=b, op=op), r=r, w=w)

    def bc(a):
        return a.unsqueeze(2).to_broadcast([128, 32, 16])

    P.op(pool, lambda g: g.memset(WTall, 0.0), w=["WTall"])
    tctr = [0]
    def do_layer(j):
        P.op(pool, lambda g: g.memset(scr[:, 0:8192], 0.0), w=["CBall"])
        for t in (Bpr, Bpi, Cpr, nCpi):
            P.op(pool, lambda g, t=t: g.memset(t[:], 0.0), w=[t.name])
        dt_ = t32("dt"); lr = t32("lr"); ph = t32("ph"); lrd = t32("lrd")
        s8 = t32("s8"); c8 = t32("c8"); m8 = t32("m8")
        P.op(act, lambda a: a.activation(out=dt_[:], in_=lst[:, j, :], func=AF.Exp), r=["lst"], w=[dt_.name])
        P.op(dve, lambda v: v.tensor_scalar(out=lr[:], in0=lre[:, j, :], scalar1=-1e-4, scalar2=None, op0=ALU.min),
             r=["lre"], w=[lr.name])
        li_ = lim[:, j, :]
        tt(ph[:], li_, dt_[:], ALU.mult, ["lim", dt_.name], [ph.name])
        tt(lrd[:], lr[:], dt_[:], ALU.mult, [lr.name, dt_.name], [lrd.name])
        P.op(act, lambda a: a.activation(out=s8[:], in_=ph[:], func=AF.Sin, scale=0.125), r=[ph.name], w=[s8.name])
        P.op(act, lambda a: a.activation(out=c8[:], in_=ph[:], func=AF.Sin, scale=-0.125, bias=math.pi / 2),
             r=[ph.name], w=[c8.name])
        P.op(act, lambda a: a.activation(out=m8[:], in_=lrd[:], func=AF.Exp, scale=0.125), r=[lrd.name], w=[m8.name])
        ar = t32("ar"); ai = t32("ai")
        tt(ar[:], m8[:], c8[:], ALU.mult, [m8.name, c8.name], [ar.name])
        tt(ai[:], m8[:], s8[:], ALU.mult, [m8.name, s8.name], [ai.name])
        for it in range(3):
            q1 = t32("q1"); q2 = t32("q2"); q3 = t32("q3"); nr = t32(f"nr{it}"); ni = t32(f"ni{it}")
            tt(q1[:], ar[:], ar[:], ALU.mult, [ar.name], [q1.name])
            tt(q2[:], ai[:], ai[:], ALU.mult, [ai.name], [q2.name])
            tt(q3[:], ar[:], ai[:], ALU.mult, [ar.name, ai.name], [q3.name])
            tt(nr[:], q1[:], q2[:], ALU.subtract, [q1.name, q2.name], [nr.name])
            tt(ni[:], q3[:], q3[:], ALU.add, [q3.name], [ni.name])
            ar, ai = nr, ni
        pw = [None] * 9
        one = t32("one"); zero = t32("zero")
        P.op(pool, lambda g: g.memset(one[:], 1.0), w=[one.name])
        P.op(pool, lambda g: g.memset(zero[:], 0.0), w=[zero.name])
        pw[0] = (one, zero)
        pw[1] = (ar, ai)

        def cmul(xr, xi, yr, yi, n):
            a1 = t32("a1"); a2 = t32("a2"); a3 = t32("a3"); a4 = t32("a4"); zr = t32(f"zr{n}"); zi = t32(f"zi{n}")
            tt(a1[:], xr[:], yr[:], ALU.mult, [xr.name, yr.name], [a1.name])
            tt(a2[:], xi[:], yi[:], ALU.mult, [xi.name, yi.name], [a2.name])
            tt(a3[:], xr[:], yi[:], ALU.mult, [xr.name, yi.name], [a3.name])
            tt(a4[:], xi[:], yr[:], ALU.mult, [xi.name, yr.name], [a4.name])
            tt(zr[:], a1[:], a2[:], ALU.subtract, [a1.name, a2.name], [zr.name])
            tt(zi[:], a3[:], a4[:], ALU.add, [a3.name, a4.name], [zi.name])
            return zr, zi
        for n in range(2, 9):
            pw[n] = cmul(pw[n - 1][0], pw[n - 1][1], ar, ai, n)
        a8r, a8i = pw[8]
        P.op(dve, lambda v: v.tensor_copy(out=Ca[:, j, 0:32], in_=a8r[:]), r=[a8r.name], w=["Ca"])
        P.op(dve, lambda v: v.tensor_copy(out=Ca[:, j, 32:64], in_=a8r[:]), r=[a8r.name], w=["Ca"])
        P.op(dve, lambda v: v.tensor_copy(out=Cb[:, j, 32:64], in_=a8i[:]), r=[a8i.name], w=["Cb"])
        P.op(dve, lambda v: v.tensor_scalar(out=Cb[:, j, 0:32], in0=a8i[:], scalar1=-1.0, scalar2=None, op0=ALU.mult),
             r=[a8i.name], w=["Cb"])
        den = t32("den"); d2 = t32("d2"); rden = t32("rden"); am1 = t32("am1")
        tt(den[:], lr[:], lr[:], ALU.mult, [lr.name], [den.name])
        tt(d2[:], li_, li_, ALU.mult, ["lim"], [d2.name])
        tt(den[:], den[:], d2[:], ALU.add, [den.name, d2.name], [den.name])
        P.op(dve, lambda v: v.reciprocal(out=rden[:], in_=den[:]), r=[den.name], w=[rden.name])
        P.op(dve, lambda v: v.tensor_scalar(out=am1[:], in0=ar[:], scalar1=-1.0, scalar2=None, op0=ALU.add),
             r=[ar.name], w=[am1.name])
        e1 = t32("e1"); e2 = t32("e2"); qr = t32("qr"); qi = t32("qi")
        tt(e1[:], am1[:], lr[:], ALU.mult, [am1.name, lr.name], [e1.name])
        tt(e2[:], ai[:], li_, ALU.mult, [ai.name, "lim"], [e2.name])
        tt(e1[:], e1[:], e2[:], ALU.add, [e1.name, e2.name], [e1.name])
        tt(qr[:], e1[:], rden[:], ALU.mult, [e1.name, rden.name], [qr.name])
        e3 = t32("e3"); e4 = t32("e4")
        tt(e3[:], ai[:], lr[:], ALU.mult, [ai.name, lr.name], [e3.name])
        tt(e4[:], am1[:], li_, ALU.mult, [am1.name, "lim"], [e4.name])
        tt(e3[:], e3[:], e4[:], ALU.subtract, [e3.name, e4.name], [e3.name])
        tt(qi[:], e3[:], rden[:], ALU.mult, [e3.name, rden.name], [qi.name])
        B_r = bre[:, j, :].rearrange("p (a h) -> p a h", h=16)
        B_i = bim[:, j, :].rearrange("p (a h) -> p a h", h=16)
        tt(U1[:], B_r, bc(qr[:]), ALU.mult, ["bre", qr.name], ["U1"])
        tt(U2[:], B_i, bc(qi[:]), ALU.mult, ["bim", qi.name], ["U2"])
        tt(Bbr[:], U1[:], U2[:], ALU.subtract, ["U1", "U2"], ["Bbr"])
        tt(U1[:], B_i, bc(qr[:]), ALU.mult, ["bim", qr.name], ["U1"])
        tt(U2[:], B_r, bc(qi[:]), ALU.mult, ["bre", qi.name], ["U2"])
        tt(Bbi[:], U1[:], U2[:], ALU.add, ["U1", "U2"], ["Bbi"])
        C_r = crt[:, j, :].rearrange("p (a h) -> p a h", h=16)
        C_i = cit[:, j, :].rearrange("p (a h) -> p a h", h=16)
        for (lo, hi, c0) in ((0, 64, 0), (64, 128, 16)):
            P.op(dve, lambda v, lo=lo, hi=hi, c0=c0: v.tensor_copy(out=Cpr[lo:hi, :, c0:c0 + 16], in_=C_r[lo:hi]),
                 r=["crt"], w=["Cpr"])
            P.op(dve, lambda v, lo=lo, hi=hi, c0=c0: v.tensor_scalar(
                out=nCpi[lo:hi, :, c0:c0 + 16], in0=C_i[lo:hi], scalar1=-1.0, scalar2=None, op0=ALU.mult),
                r=["cit"], w=["nCpi"])
        P.op(dve, lambda v: v.tensor_scalar(out=nCr[:], in0=C_r, scalar1=-1.0, scalar2=None, op0=ALU.mult),
             r=["crt"], w=["nCr"])
        P.op(dve, lambda v: v.tensor_scalar(out=nCi[:], in0=C_i, scalar1=-1.0, scalar2=None, op0=ALU.mult),
             r=["cit"], w=["nCi"])
        def do_s(s):
            pr, pi_ = pw[7 - s]
            tt(U1[:], Bbr[:], bc(pr[:]), ALU.mult, ["Bbr", pr.name], ["U1"])
            tt(U2[:], Bbi[:], bc(pi_[:]), ALU.mult, ["Bbi", pi_.name], ["U2"])
            for (lo, hi, c0) in ((0, 64, 0), (64, 128, 16)):
                tt(Bpr[lo:hi, :, c0:c0 + 16], U1[lo:hi], U2[lo:hi], ALU.subtract, ["U1", "U2"], ["Bpr"])
            tt(U1[:], Bbi[:], bc(pr[:]), ALU.mult, ["Bbi", pr.name], ["U1"])
            tt(U2[:], Bbr[:], bc(pi_[:]), ALU.mult, ["Bbr", pi_.name], ["U2"])
            for (lo, hi, c0) in ((0, 64, 0), (64, 128, 16)):
                tt(Bpi[lo:hi, :, c0:c0 + 16], U1[lo:hi], U2[lo:hi], ALU.add, ["U1", "U2"], ["Bpi"])
            tau = 7 - s
            for k in range(8 if not (DBG & 4) else 0):
                bk = tctr[0] % 4
                tctr[0] += 1
                for ri, Bp in enumerate((Bpr, Bpi)):
                    src = Bp[:, 4 * k:4 * k + 4, :].rearrange("p a c -> p (a c)")
                    P.op(pe, lambda e, src=src, bk=bk, ri=ri: e.transpose(
                        out=ps[bk][:, ri * 128:(ri + 1) * 128], in_=src, identity=ident[:]),
                        r=[Bp.name, "ident"], w=[PSK[bk]], inc=False)
                srcr = Bpr[:, 4 * k:4 * k + 4, :].rearrange("p a c -> p (a c)")
                srci = Bpi[:, 4 * k:4 * k + 4, :].rearrange("p a c -> p (a c)")
                cr = Cpr[:, 4 * k:4 * k + 4, :].rearrange("p a c -> p (a c)")
                ci = nCpi[:, 4 * k:4 * k + 4, :].rearrange("p a c -> p (a c)")
                P.op(pe, lambda e, bk=bk, srcr=srcr, cr=cr: e.matmul(ps[bk][:, 256:384], lhsT=srcr, rhs=cr,
                                                                     start=True, stop=False),
                     r=["Bpr", "Cpr"], w=[PSK[bk]], inc=False)
                P.op(pe, lambda e, bk=bk, srci=srci, ci=ci: e.matmul(ps[bk][:, 256:384], lhsT=srci, rhs=ci,
                                                                     start=False, stop=True),
                     r=["Bpi", "nCpi"], w=[PSK[bk]], inc=True)
                P.op(act, lambda a, bk=bk, k=k, s=s: a.activation(
                    out=WBall[:, k, s, :, :], in_=ps[bk][:, 0:256].rearrange("p (r c) -> p r c", r=2), func=AF.Copy),
                    r=[PSK[bk]], w=["WBall"])
                for jq in range(4):
                    P.op(dve, lambda v, bk=bk, k=k, tau=tau, jq=jq: v.tensor_copy(
                        out=WTall[32 * jq:32 * jq + 32, k, tau, 32 * jq:32 * jq + 32],
                        in_=ps[bk][32 * jq:32 * jq + 32, 256 + 32 * jq:256 + 32 * jq + 32]),
                        r=[], w=["WTall", PSK[bk]])
            qr_, qi_ = pw[s + 1]
            tt(U1[:], C_r, bc(qr_[:]), ALU.mult, ["crt", qr_.name], ["U1"])
            tt(U2[:], C_i, bc(qi_[:]), ALU.mult, ["cit", qi_.name], ["U2"])
            for (lo, hi, c0) in ((0, 64, 0), (64, 128, 16)):
                tt(CBall[lo:hi, :, s, 0, c0:c0 + 16], U1[lo:hi], U2[lo:hi], ALU.subtract, ["U1", "U2"], ["CBall"])
            tt(U1[:], nCr[:], bc(qi_[:]), ALU.mult, ["nCr", qi_.name], ["U1"])
            tt(U2[:], nCi[:], bc(qr_[:]), ALU.mult, ["nCi", qr_.name], ["U2"])
            for (lo, hi, c0) in ((0, 64, 0), (64, 128, 16)):
                tt(CBall[lo:hi, :, s, 1, c0:c0 + 16], U1[lo:hi], U2[lo:hi], ALU.add, ["U1", "U2"], ["CBall"])
        for s_ in range(8):
            do_s(s_)
            for _ in range(6):
                next(E["ada_it"], None)
        if DBG & 8:
            return
        P.dma(sync, wbd[j].rearrange("k p n -> p k n"), scr[:, 8192:16384].bitcast(BF16).rearrange("p (k n) -> p k n", k=8),
              s_scr[0], r=["WBall"], w=[("wbd", j)])
        P.dma(sync, cbd[j].rearrange("k p n -> p k n"), scr[:, 0:8192].bitcast(BF16).rearrange("p (k n) -> p k n", k=8),
              s_scr[1], r=["CBall"], w=[("cbd", j)])
        P.dma(sync, wtd[j].rearrange("k p n -> p k n"), hb[:, :, :], s_scr[2], r=["WTall"], w=[("wtd", j)])
    for j_ in range(2):
        if 2 * j_ in E["layers"]:
            do_layer(j_)
    for sm in s_scr:
        if sm.count:
            for q in (pe, act, dve, pool, sync):
                q.wait((sm, sm.count))


def _load_tile(E, ti):
    P = E["P"]; pe, act, dve, sync = P.pe, P.act, P.dve, P.sync
    xT, xin, ps, PSK, ident = E["xT"], E["xin"], E["ps"], E["PSK"], E["ident"]
    xTv = xT[:, :, :].rearrange("p k (s c) -> p k s c", s=8)
    for blk in range(8):
        b = blk % 2
        r0 = ti * TT + blk * 128
        P.dma(sync, xin[:, b, :], E["x_d"][r0:r0 + 128, :], E["s_xin"][b], w=[("xin", b)])
        for half in range(2):
            bk = (blk * 2 + half) % 4
            for kk in range(4):
                k = half * 4 + kk
                P.op(pe, lambda e, b=b, bk=bk, k=k, kk=kk: e.transpose(
                    out=ps[bk][:, kk * 128:(kk + 1) * 128], in_=xin[:, b, k * 128:(k + 1) * 128], identity=ident[:]),
                    r=[("xin", b), "ident"], w=[PSK[bk]], inc=(kk == 3))
            q = act if half == 0 else dve
            src = ps[bk][:, :].rearrange("p (k jc s) -> p k s jc", k=4, jc=16, s=8)
            dst = xTv[:, half * 4:half * 4 + 4, :, 16 * blk:16 * blk + 16]
            if q is act:
                P.op(q, lambda a, src=src, dst=dst: a.activation(out=dst, in_=src, func=AF.Copy), r=[PSK[bk]], w=["xT"])
            else:
                P.op(q, lambda v, src=src, dst=dst: v.tensor_copy(out=dst, in_=src), r=[PSK[bk]], w=["xT"])


def _final_store(E, ti, do_final):
    P = E["P"]; pe, act, dve, sync = P.pe, P.act, P.dve, P.sync
    xT, xin, ps, PSK, ident, scr = E["xT"], E["xin"], E["ps"], E["PSK"], E["ident"], E["scr"]
    if do_final:
        _norm_mod(E, None, 2)
        src_all = scr[:, 0:8192].rearrange("p (k s c) -> p k s c", k=8, s=8)
        skey = "fout"
    else:
        src_all = xT[:, :, :].rearrange("p k (s c) -> p k s c", s=8)
        skey = "xT"
    for blk in range(8):
        b = blk % 2
        for half in range(2):
            a = (blk * 2 + half) % 2
            bk = (blk * 2 + half) % 4
            stg = scr[:, 8192 + a * 512: 8192 + (a + 1) * 512]
            src = src_all[:, half * 4:half * 4 + 4, :, 16 * blk:16 * blk + 16]
            dstv = stg.rearrange("p (k jc s) -> p k s jc", k=4, jc=16, s=8)
            if half == 0:
                P.op(act, lambda e, src=src, dstv=dstv: e.activation(out=dstv, in_=src, func=AF.Copy),
                     r=[skey], w=[("stg", a)])
            else:
                P.op(dve, lambda e, src=src, dstv=dstv: e.tensor_copy(out=dstv, in_=src), r=[skey], w=[("stg", a)])
            for kk in range(4):
                P.op(pe, lambda e, stg=stg, bk=bk, kk=kk: e.transpose(
                    out=ps[bk][:, kk * 128:(kk + 1) * 128], in_=stg[:, kk * 128:(kk + 1) * 128], identity=ident[:]),
                    r=[("stg", a), "ident"], w=[PSK[bk]], inc=(kk == 3))
            dsto = xin[:, b, half * 512:(half + 1) * 512]
            if half == 0:
                P.op(dve, lambda e, bk=bk, dsto=dsto: e.tensor_copy(out=dsto, in_=ps[bk][:, :]),
                     r=[PSK[bk]], w=[("xin", b)])
            else:
                P.op(act, lambda e, bk=bk, dsto=dsto: e.activation(out=dsto, in_=ps[bk][:, :], func=AF.Copy),
                     r=[PSK[bk]], w=[("xin", b)])
        r0 = ti * TT + blk * 128
        P.dma(sync, E["y_d"][r0:r0 + 128, :], xin[:, b, :], E["s_out"][b], r=[("xin", b)])


def _norm_mod(E, li, which):
    P = E["P"]; pe, act, dve = P.pe, P.act, P.dve
    xT, hb, sq, rstd, tmpf, ps, PSK = E["xT"], E["hb"], E["sq"], E["rstd"], E["tmpf"], E["ps"], E["PSK"]
    ones_bf, gsc, mods, fing32, scr = E["ones_bf"], E["gsc"], E["mods"], E["fing32"], E["scr"]
    fout = scr[:, 0:8192].rearrange("p (k n) -> p k n", k=8)
    sqs = [sq[:, :, :], hb[:, :, 512:1024]]
    sqk = ["sq", ("hb", 1)]
    P.op(act, lambda a: a.activation(out=sqs[0], in_=xT[:, :, 0:512], func=AF.Square), r=["xT"], w=[sqk[0]])
    P.op(dve, lambda v: v.tensor_tensor(out=sqs[1], in0=xT[:, :, 512:1024], in1=xT[:, :, 512:1024], op=ALU.mult),
         r=["xT"], w=[sqk[1]])
    for tb in range(2):
        bk = 6 + tb
        for k in range(8):
            P.op(pe, lambda e, k=k, bk=bk, tb=tb: e.matmul(ps[bk][:, :], lhsT=ones_bf[:], rhs=sqs[tb][:, k, :],
                                                           start=(k == 0), stop=(k == 7)),
                 r=[sqk[tb], "ones"], w=[PSK[bk]], inc=(k == 7))
    for tb in range(2):
        bk = 6 + tb
        P.op(act, lambda a, tb=tb, bk=bk: a.activation(out=rstd[:, tb, :], in_=ps[bk][:, :], func=AF.Sqrt,
                                                       bias=D * EPS, scale=1.0),
             r=[PSK[bk]], w=[("rstd", tb)])
        P.op(dve, lambda v, tb=tb: v.reciprocal(out=rstd[:, tb, :], in_=rstd[:, tb, :]),
             r=[("rstd", tb)], w=[("rstd", tb)])
    for tb in range(2):
        sl = slice(tb * 512, (tb + 1) * 512)
        for k in range(8):
            if which == 2:
                P.op(dve, lambda v, k=k, sl=sl, tb=tb: v.scalar_tensor_tensor(
                    out=fout[:, k, sl], in0=xT[:, k, sl], scalar=fing32[:, k:k + 1], in1=rstd[:, tb, :],
                    op0=ALU.mult, op1=ALU.mult), r=["xT", ("rstd", tb), "fing32"], w=["fout"])
                continue
            tbuf = k % 4
            P.op(dve, lambda v, k=k, sl=sl, tb=tb, tbuf=tbuf: v.scalar_tensor_tensor(
                out=tmpf[:, tbuf, :], in0=xT[:, k, sl], scalar=gsc[:, li, which, k:k + 1], in1=rstd[:, tb, :],
                op0=ALU.mult, op1=ALU.mult), r=["xT", ("rstd", tb), "gsc"], w=[("tmpf", tbuf)])
            shc = (3 * which) * 8 + k
            P.op(act, lambda a, k=k, sl=sl, tbuf=tbuf, shc=shc: a.activation(
                out=hb[:, k, sl], in_=tmpf[:, tbuf, :], func=AF.Identity, bias=mods[:, li, shc:shc + 1], scale=1.0),
                r=[("tmpf", tbuf), "mods"], w=[("hb", tb)])


def _ffn(E, li):
    P = E["P"]; pe, act, dve, pool = P.pe, P.act, P.dve, P.pool
    xT, hb, scr, tmpf, ps, PSK, mods = E["xT"], E["hb"], E["scr"], E["tmpf"], E["ps"], E["PSK"], E["mods"]
    ringA, ringB = E["ringA"], E["ringB"]
    actb = scr[:, 0:11264].bitcast(BF16).rearrange("p (j n) -> p j n", j=FC)
    wi = E["wfi_d"][li].rearrange("(k p) n -> p k n", p=128)
    wo = E["wfo_d"][li].rearrange("(j p) n -> p j n", p=128)
    cnt = 0
    for grp in range(11):
        buf, sem, key = ringA.next()
        bv = buf[:, :].rearrange("p (g k n) -> p g k n", g=2, k=8)
        P.dma_multi(pool, [(bv[:, 0], wi[:, :, grp * 256:(grp + 1) * 256]),
                           (bv[:, 1], wi[:, :, FF + grp * 256:FF + (grp + 1) * 256])], sem, w=[key])
        for jj in range(2):
            j = grp * 2 + jj
            for tb in range(2):
                sl = slice(tb * 512, (tb + 1) * 512)
                gb = cnt % 2
                ub = 2 + cnt % 2
                cnt += 1
                for k in range(8):
                    P.op(pe, lambda e, bv=bv, jj=jj, k=k, sl=sl, gb=gb: e.matmul(
                        ps[gb][:, :], lhsT=bv[:, 0, k, jj * 128:(jj + 1) * 128], rhs=hb[:, k, sl],
                        start=(k == 0), stop=(k == 7)), r=[key, ("hb", tb)], w=[PSK[gb]], inc=(k == 7))
                for k in range(8):
                    P.op(pe, lambda e, bv=bv, jj=jj, k=k, sl=sl, ub=ub: e.matmul(
                        ps[ub][:, :], lhsT=bv[:, 1, k, jj * 128:(jj + 1) * 128], rhs=hb[:, k, sl],
                        start=(k == 0), stop=(k == 7)), r=[key, ("hb", tb)], w=[PSK[ub]], inc=(k == 7))
                tbuf = cnt % 4
                P.op(act, lambda a, gb=gb, tbuf=tbuf: a.activation(out=tmpf[:, tbuf, :], in_=ps[gb][:, :], func=AF.Silu),
                     r=[PSK[gb]], w=[("tmpf", tbuf)])
                P.op(dve, lambda v, ub=ub, tbuf=tbuf, j=j, sl=sl: v.tensor_tensor(
                    out=actb[:, j, sl], in0=ps[ub][:, :], in1=tmpf[:, tbuf, :], op=ALU.mult),
                    r=[PSK[ub], ("tmpf", tbuf)], w=[("actb", j)])
    cnt = 0
    for mg in range(4):
        buf, sem, key = ringB.next()
        bv = buf[:, :].rearrange("p (j n) -> p j n", j=FC)
        P.dma(pool, bv, wo[:, :, mg * 256:(mg + 1) * 256], sem, w=[key])
        for mm in range(2):
            m = mg * 2 + mm
            for tb in range(2):
                sl = slice(tb * 512, (tb + 1) * 512)
                ob = 4 + cnt % 2
                cnt += 1
                for j in range(FC):
                    P.op(pe, lambda e, bv=bv, mm=mm, j=j, sl=sl, ob=ob: e.matmul(
                        ps[ob][:, :], lhsT=bv[:, j, mm * 128:(mm + 1) * 128], rhs=actb[:, j, sl],
                        start=(j == 0), stop=(j == FC - 1)), r=[key, ("actb", j)], w=[PSK[ob]], inc=(j == FC - 1))
                gc = 5 * 8 + m
                P.op(dve, lambda v, ob=ob, m=m, sl=sl, gc=gc: v.scalar_tensor_tensor(
                    out=xT[:, m, sl], in0=ps[ob][:, :], scalar=mods[:, li, gc:gc + 1], in1=xT[:, m, sl],
                    op0=ALU.mult, op1=ALU.add), r=[PSK[ob], "mods", "xT"], w=["xT"])


def _conv_mixer(E, li, ti):
    P = E["P"]; pe, act, dve, pool = P.pe, P.act, P.dve, P.pool
    xT, hb, scr, tmpf, gel, ps, PSK, mods = E["xT"], E["hb"], E["scr"], E["tmpf"], E["gel"], E["ps"], E["PSK"], E["mods"]
    cw, halo = E["cw"], E["halo"]
    ringA, ringB = E["ringA"], E["ringB"]
    jl = li // 2
    mbuf = scr[:, 0:4096].bitcast(BF16).rearrange("p (k n) -> p k n", k=8)
    cvb = [scr[:, 4096 + i * 1032: 4096 + (i + 1) * 1032].rearrange("p (s c) -> p s c", s=8) for i in range(2)]
    accb = [scr[:, 6400 + i * 1024: 6400 + (i + 1) * 1024] for i in range(2)]
    wi = E["cwi_d"][jl].rearrange("(k p) n -> p k n", p=128)
    wo = E["cwo_d"][jl].rearrange("(k p) n -> p k n", p=128)
    for m in range(8):
        buf, sem, key = ringA.next()
        bv = buf[:, 0:3072].rearrange("p (g k n) -> p g k n", g=3, k=8)
        P.dma_multi(pool, [(bv[:, g], wi[:, :, g * D + m * 128: g * D + (m + 1) * 128]) for g in range(3)], sem, w=[key])
        cv = cvb[m % 2]
        acc = accb[m % 2]
        bgb = gel[:, m % 2, :]
        ck, ak, bk_ = ("cvb", m % 2), ("acc", m % 2), ("gel", m % 2)
        for tb in range(2):
            sl = slice(tb * 512, (tb + 1) * 512)
            base = 3 * ((2 * m + tb) % 2)
            for g in range(3):
                for k in range(8):
                    P.op(pe, lambda e, bv=bv, g=g, k=k, sl=sl, base=base: e.matmul(
                        ps[base + g][:, :], lhsT=bv[:, g, k, :], rhs=hb[:, k, sl], start=(k == 0), stop=(k == 7)),
                        r=[key, ("hb", tb)], w=[PSK[base + g]], inc=(k == 7))
            tbuf = (2 * m + tb) % 4
            P.op(act, lambda a, base=base, sl=sl, bgb=bgb: a.activation(out=bgb[:, sl], in_=ps[base][:, :], func=AF.Copy),
                 r=[PSK[base]], w=[bk_])
            P.op(act, lambda a, base=base, tbuf=tbuf: a.activation(out=tmpf[:, tbuf, :], in_=ps[base + 1][:, :], func=AF.Copy),
                 r=[PSK[base + 1]], w=[("tmpf", tbuf)])
            P.op(dve, lambda v, base=base, tbuf=tbuf, cv=cv, tb=tb: v.tensor_tensor(
                out=cv[:, 4 * tb:4 * tb + 4, 1:129], in0=ps[base + 2][:, :].rearrange("p (s c) -> p s c", s=4),
                in1=tmpf[:, tbuf, :].rearrange("p (s c) -> p s c", s=4), op=ALU.mult),
                r=[PSK[base + 2], ("tmpf", tbuf)], w=[ck])
        P.op(dve, lambda v, cv=cv, m=m: v.tensor_copy(out=cv[:, :, 0], in_=halo[:, jl, m, :]), r=["halo", ck], w=[ck])
        P.op(dve, lambda v, cv=cv, m=m: v.tensor_copy(out=halo[:, jl, m, :], in_=cv[:, :, 128]), r=[ck, "halo"], w=["halo"])
        a3 = acc.rearrange("p (s c) -> p s c", s=8)
        w0 = cw[:, jl, m, 0:1]; w1 = cw[:, jl, m, 1:2]; w2 = cw[:, jl, m, 2:3]
        P.op(dve, lambda v, a3=a3, cv=cv, w2=w2: v.tensor_scalar(out=a3, in0=cv[:, :, 1:129], scalar1=w2, scalar2=None,
                                                                op0=ALU.mult), r=[ck, "cw"], w=[ak])
        for (o_, i_, wv) in ((a3[:, 1:8, :], cv[:, 0:7, 1:129], w1), (a3[:, 0:1, :], cv[:, 7:8, 0:128], w1),
                             (a3[:, 2:8, :], cv[:, 0:6, 1:129], w0), (a3[:, 0:2, :], cv[:, 6:8, 0:128], w0)):
            P.op(dve, lambda v, o_=o_, i_=i_, wv=wv: v.scalar_tensor_tensor(
                out=o_, in0=i_, scalar=wv, in1=o_, op0=ALU.mult, op1=ALU.add), r=[ck, ak, "cw"], w=[ak])
        P.op(dve, lambda v, acc=acc, bgb=bgb, m=m: v.tensor_tensor(out=mbuf[:, m, :], in0=acc, in1=bgb, op=ALU.mult),
             r=[ak, bk_], w=[("mbuf", m)])
    cnt = 0
    for mg in range(4):
        buf, sem, key = ringB.next()
        bv = buf[:, 0:2048].rearrange("p (k n) -> p k n", k=8)
        P.dma(pool, bv, wo[:, :, mg * 256:(mg + 1) * 256], sem, w=[key])
        for mm in range(2):
            m = mg * 2 + mm
            for tb in range(2):
                sl = slice(tb * 512, (tb + 1) * 512)
                ob = 6 + cnt % 2
                cnt += 1
                for k in range(8):
                    P.op(pe, lambda e, bv=bv, mm=mm, k=k, sl=sl, ob=ob: e.matmul(
                        ps[ob][:, :], lhsT=bv[:, k, mm * 128:(mm + 1) * 128], rhs=mbuf[:, k, sl],
                        start=(k == 0), stop=(k == 7)), r=[key, ("mbuf", k)], w=[PSK[ob]], inc=(k == 7))
                gc = 2 * 8 + m
                P.op(dve, lambda v, ob=ob, m=m, sl=sl, gc=gc: v.scalar_tensor_tensor(
                    out=xT[:, m, sl], in0=ps[ob][:, :], scalar=mods[:, li, gc:gc + 1], in1=xT[:, m, sl],
                    op0=ALU.mult, op1=ALU.add), r=[PSK[ob], "mods", "xT"], w=["xT"])


def _ssm_mixer(E, li, ti):
    P = E["P"]; pe, act, dve, pool, sync = P.pe, P.act, P.dve, P.pool, P.sync
    xT, hb, scr, tmpf, gel, ps, PSK, mods = E["xT"], E["hb"], E["scr"], E["tmpf"], E["gel"], E["ps"], E["PSK"], E["mods"]
    hprev, sstate, Ca, Cb, dd = E["hprev"], E["sstate"], E["Ca"], E["Cb"], E["dd"]
    T1, T2, T3 = E["T1"], E["T2"], E["T3"]
    ringA = E["ringA"]
    jl = li // 2
    XS = scr[:, 0:8256].rearrange("p (r c) -> p r c", r=64)
    XSb = scr[:, 8256:12352].bitcast(BF16).rearrange("p (r c) -> p r c", r=64)
    ybuf = scr[:, 12352:16448].bitcast(BF16).rearrange("p (k n) -> p k n", k=8)
    hv = hb[:, :, :].rearrange("p k (s c) -> p k s c", s=8)
    P.op(dve, lambda v: v.tensor_copy(out=XS[:, :, 0], in_=sstate[:, jl, :]), r=["sstate"], w=["XS"])
    for k in range(8):
        buf, sem, key = ringA.next(hw=True)
        bv = buf[:, 0:2048].rearrange("p (s r c) -> p s r c", s=8, r=2)
        P.dma(sync, buf[:, 0:2048], E["wbd"][jl, k], sem, r=[("wbd", jl)], w=[key])
        for ri in range(2):
            for s in range(8):
                for jq in range(4):
                    p0 = 32 * jq
                    P.op(pe, lambda e, bv=bv, p0=p0, ri=ri, s=s, k=k, jq=jq: e.matmul(
                        ps[ri * 4 + jq][:, 0:128], lhsT=bv[p0:p0 + 32, s, ri, :],
                        rhs=hv[p0:p0 + 32, k, s, :], start=(s == 0), stop=(s == 7), tile_position=(p0, 0)),
                        r=[key, ("hb", s // 4)], w=[PSK[ri * 4 + jq]], inc=(s == 7 and jq == 3))
            for jq in range(4):
                pi_ = 4 * k + jq
                dst = XS[:, ri * 32 + pi_, 1:129]
                src = ps[ri * 4 + jq][:, 0:128]
                if jq % 2 == 0:
                    P.op(act, lambda a, dst=dst, src=src: a.activation(out=dst, in_=src, func=AF.Copy),
                         r=[PSK[ri * 4 + jq]], w=["XS"])
                else:
                    P.op(dve, lambda v, dst=dst, src=src: v.tensor_copy(out=dst, in_=src), r=[PSK[ri * 4 + jq]], w=["XS"])
    def mm(out, lhsT, rhs, rk, start=False, bank=0, inc=False, tp=None):
        if tp is None:
            P.op(pe, lambda e: e.matmul(out, lhsT=lhsT, rhs=rhs, start=start, stop=inc, skip_group_check=True),
                 r=rk, w=[PSK[bank]], inc=inc)
        else:
            P.op(pe, lambda e: e.matmul(out, lhsT=lhsT, rhs=rhs, start=start, stop=inc, skip_group_check=True,
                                        tile_position=tp),
                 r=rk, w=[PSK[bank]], inc=inc)

    def near(k, wtv, yb, rk):
        for hf in range(2):
            mm(ps[yb[hf]][:, :], wtv[:, 0, :], hb[:, k, hf * 512:(hf + 1) * 512], rk, start=True, bank=yb[hf])
        for tau in range(1, 8):
            lo = tau * 128
            if lo < 512:
                mm(ps[yb[0]][:, lo:512], wtv[:, tau, :], hb[:, k, 0:512 - lo], rk, bank=yb[0])
            lo2 = max(512, lo)
            mm(ps[yb[1]][:, lo2 - 512:512], wtv[:, tau, :], hb[:, k, lo2 - lo:1024 - lo], rk, bank=yb[1])
            for s in range(tau):
                hf, sc = s // 4, s % 4
                src_s = s + 8 - tau
                mm(ps[yb[hf]][:, sc * 128 + 1:sc * 128 + 128], wtv[:, tau, :],
                   hb[:, k, src_s * 128:src_s * 128 + 127], rk, bank=yb[hf])
                mm(ps[yb[hf]][:, sc * 128:sc * 128 + 1], wtv[:, tau, :], hprev[:, jl, k, src_s:src_s + 1], rk, bank=yb[hf])

    def far(k, cbv, yb, rk):
        for s in range(8):
            hf, sc = s // 4, s % 4
            for ri in range(2):
                for jq in range(4):
                    pi_ = 4 * k + jq
                    last = (jq == 3 and s in (3, 7) and ri == 1)
                    mm(ps[yb[hf]][32 * jq:32 * jq + 32, sc * 128:(sc + 1) * 128], cbv[:, jq, s, ri, :],
                       XSb[:, ri * 32 + pi_, :], rk, bank=yb[hf], inc=last, tp=(0, 32 * jq))

    def evac(k, yb):
        for hf in range(2):
            sl = slice(hf * 512, (hf + 1) * 512)
            g_ = gel[:, hf, 0:512]
            P.op(dve, lambda v, hf=hf, sl=sl, k=k, g_=g_, yb=yb: v.scalar_tensor_tensor(
                out=g_, in0=hb[:, k, sl], scalar=dd[:, jl, k:k + 1], in1=ps[yb[hf]][:, :],
                op0=ALU.mult, op1=ALU.add), r=[("hb", hf), "dd", PSK[yb[hf]]], w=[("gel", hf)])
            P.op(act, lambda a, sl=sl, k=k, g_=g_: a.activation(out=ybuf[:, k, sl], in_=g_, func=AF.Gelu_apprx_tanh),
                 r=[("gel", hf)], w=[("ybuf", k)])

    bufW, semW, keyW = ringA.next(hw=True)
    P.dma(sync, bufW[:, :].rearrange("p (k n) -> p k n", k=4), E["wtd"][jl, 0:4].rearrange("k p n -> p k n"), semW,
          r=[("wtd", jl)], w=[keyW])
    for k in range(4):
        wtv = bufW[:, k * 1024:(k + 1) * 1024].rearrange("p (t c) -> p t c", t=8)
        near(k, wtv, [2 * k, 2 * k + 1], [keyW, ("hb", 0), ("hb", 1), "hprev"])
    xs_t = XS.tensor
    pstep = XS.ap[0][0]
    for c in range(128 if not (DBG & 1) else 0):
        cur = XS[:, :, c]
        nxt = XS[:, :, c + 1]
        swp = bass.AP(xs_t, XS.offset + 32 * 129 + c, [[pstep, 128], [-32 * 129, 2], [129, 32]])
        P.op(dve, lambda v, cur=cur: v.tensor_tensor(out=T1[:], in0=cur, in1=Ca[:, jl, :], op=ALU.mult),
             r=["XS", "Ca"], w=["T1"])
        P.op(dve, lambda v, swp=swp: v.tensor_tensor(out=T2[:].rearrange("p (r c) -> p r c", r=2), in0=swp,
                                                     in1=Cb[:, jl, :].rearrange("p (r c) -> p r c", r=2), op=ALU.mult),
             r=["XS", "Cb"], w=["T2"])
        P.op(dve, lambda v: v.tensor_tensor(out=T3[:], in0=T1[:], in1=T2[:], op=ALU.add), r=["T1", "T2"], w=["T3"])
        P.op(dve, lambda v, nxt=nxt: v.tensor_tensor(out=nxt, in0=T3[:], in1=nxt, op=ALU.add), r=["T3", "XS"], w=["XS"])
    P.op(dve, lambda v: v.tensor_copy(out=sstate[:, jl, :], in_=XS[:, :, 128]), r=["XS"], w=["sstate"])
    for q4 in range(4):
        P.op(act, lambda a, q4=q4: a.activation(out=XSb[:, 16 * q4:16 * q4 + 16, :], in_=XS[:, 16 * q4:16 * q4 + 16, 0:128],
                                                 func=AF.Copy), r=["XS"], w=["XSb"])
    for k in range(8):
        buf, sem, key = ringA.next(hw=True)
        cbv = buf[:, 0:2048].rearrange("p (a s r c) -> p a s r c", a=4, s=8, r=2)
        yb = [2 * (k % 4), 2 * (k % 4) + 1]
        if k < 4:
            P.dma(sync, buf[:, 0:2048], E["cbd"][jl, k], sem, r=[("cbd", jl)], w=[key])
        else:
            wtv = buf[:, 2048:3072].rearrange("p (t c) -> p t c", t=8)
            P.dma_multi(sync, [(buf[:, 0:2048], E["cbd"][jl, k]), (buf[:, 2048:3072], E["wtd"][jl, k])], sem,
                        r=[("cbd", jl), ("wtd", jl)], w=[key])
            near(k, wtv, yb, [key, ("hb", 0), ("hb", 1), "hprev"])
        far(k, cbv, yb, [key, "XSb"])
        evac(k, yb)
    P.op(dve, lambda v: v.tensor_copy(out=hprev[:, jl, :, :], in_=hv[:, :, :, 127]), r=[("hb", 0), ("hb", 1), "hprev"], w=["hprev"])
    wo = E["swo_d"][jl].rearrange("(k p) n -> p k n", p=128)
    cnt = 0
    for mg in range(4):
        buf, sem, key = ringA.next()
        bv = buf[:, :].rearrange("p (g k n) -> p g k n", g=2, k=8)
        P.dma_multi(pool, [(bv[:, 0], wo[:, :, mg * 256:(mg + 1) * 256]),
                           (bv[:, 1], wo[:, :, D + mg * 256:D + (mg + 1) * 256])], sem, w=[key])
        for mm_ in range(2):
            m = mg * 2 + mm_
            for tb in range(2):
                sl = slice(tb * 512, (tb + 1) * 512)
                vb = 4 + cnt % 2
                gb = 6 + cnt % 2
                cnt += 1
                for g, bnk in ((0, vb), (1, gb)):
                    for k in range(8):
                        P.op(pe, lambda e, bv=bv, g=g, mm_=mm_, k=k, sl=sl, bnk=bnk: e.matmul(
                            ps[bnk][:, :], lhsT=bv[:, g, k, mm_ * 128:(mm_ + 1) * 128], rhs=ybuf[:, k, sl],
                            start=(k == 0), stop=(k == 7)), r=[key, ("ybuf", k)], w=[PSK[bnk]], inc=(k == 7))
                tbuf = cnt % 4
                P.op(act, lambda a, gb=gb, tbuf=tbuf: a.activation(out=tmpf[:, tbuf, :], in_=ps[gb][:, :], func=AF.Sigmoid),
                     r=[PSK[gb]], w=[("tmpf", tbuf)])
                P.op(dve, lambda v, vb=vb, tbuf=tbuf: v.tensor_tensor(out=tmpf[:, tbuf, :], in0=ps[vb][:, :],
                                                                      in1=tmpf[:, tbuf, :], op=ALU.mult),
                     r=[PSK[vb], ("tmpf", tbuf)], w=[("tmpf", tbuf)])
                gc = 2 * 8 + m
                P.op(dve, lambda v, tbuf=tbuf, m=m, sl=sl, gc=gc: v.scalar_tensor_tensor(
                    out=xT[:, m, sl], in0=tmpf[:, tbuf, :], scalar=mods[:, li, gc:gc + 1], in1=xT[:, m, sl],
                    op0=ALU.mult, op1=ALU.add), r=[("tmpf", tbuf), "mods", "xT"], w=["xT"])


def _prep_inputs(inp, b):
    f = np.float32
    g = lambda a: np.ascontiguousarray(np.asarray(a, dtype=f))

    def pk(v):
        v = np.asarray(v, dtype=f)
        lead = v.shape[:-1]
        return g(np.moveaxis(v.reshape(lead + (8, 128)), -1, 0))

    def gp(a):
        a = np.asarray(a, dtype=f).reshape(2, 32, 2, 64)
        return g(a.transpose(2, 3, 0, 1).reshape(128, 2, 32))
    m = {}
    m["x"] = g(inp["x"][b])
    m["c"] = pk(inp["c"][b])
    m["n1g"] = pk(inp["norm1_g"])
    m["n2g"] = pk(inp["norm2_g"])
    m["fing"] = pk(inp["final_g"])
    m["w_ada"] = g(inp["w_ada"])
    m["b_ada"] = g(np.asarray(inp["b_ada"], dtype=f).reshape(4, 48, 128).transpose(2, 0, 1))
    m["lre"] = gp(inp["ssm_a_re"])
    m["lim"] = gp(inp["ssm_a_im"])
    ls = np.broadcast_to(np.asarray(inp["ssm_log_step"], dtype=f)[:, :, None], (2, 64, 64))
    m["lst"] = gp(ls)
    for nm, src in (("bre", "ssm_b_re"), ("bim", "ssm_b_im")):
        a = np.asarray(inp[src], dtype=f).reshape(2, 32, 2, 64, 16)
        m[nm] = g(a.transpose(2, 3, 0, 1, 4).reshape(128, 2, 512))
    for nm, src in (("crt", "ssm_c_re"), ("cit", "ssm_c_im")):
        a = np.asarray(inp[src], dtype=f).reshape(2, 32, 2, 16, 64)
        m[nm] = g(a.transpose(2, 4, 0, 1, 3).reshape(128, 2, 512))
    m["dd"] = pk(inp["ssm_d"])
    m["ssm_w_out"] = g(inp["ssm_w_out"])
    m["conv_w_in"] = g(inp["conv_w_in"])
    cwv = np.asarray(inp["conv_w"], dtype=f).reshape(2, 3, 8, 128)
    m["cw"] = g(cwv.transpose(3, 0, 2, 1))
    m["conv_w_out"] = g(inp["conv_w_out"])
    m["w_ffn_in"] = g(inp["w_ffn_in"])
    m["w_ffn_out"] = g(inp["w_ffn_out"])
    return m


_NC_CACHE = {}


def kernel(**inputs):
    if "full" not in _NC_CACHE:
        _NC_CACHE["full"] = build()
    nc = _NC_CACHE["full"]
    n = 8
    in_maps = [_prep_inputs(inputs, b) for b in range(n)]
    res = run_bass_kernel_spmd(nc, in_maps, core_ids=list(range(n)))
    out = np.stack([np.asarray(r["y"], dtype=np.float32) for r in res.results], axis=0)
    return out
```

```python
import contextlib
import math
import os
DBG = int(os.environ.get('KDBG', '0'))
OPT = int(os.environ.get('KOPT', '1'))
import numpy as np
import concourse.bass as bass
import concourse.mybir as mybir
from concourse.bass_utils import run_bass_kernel_spmd

F32 = mybir.dt.float32
BF16 = mybir.dt.bfloat16
ALU = mybir.AluOpType
AF = mybir.ActivationFunctionType

D = 1024
KC = 8
TT = 1024
SEQ = 4096
FF = 2816
FC = 22
EPS = 1e-6


class Sem:
    def __init__(self, name):
        self.name = name
        self.handle = None
        self.count = 0


class Q:
    def __init__(self, prog, name):
        self.prog = prog
        self.name = name
        self.items = []
        self.sem = prog.sem("q_" + name)
        self.waited = {}
        self.pending = []
        self.last = None

    def wait(self, ev):
        if ev is None:
            return
        s, v = ev
        if self.waited.get(s, 0) >= v:
            return
        self.waited[s] = v
        self.items.append(lambda e, s=s, v=v: e.wait_ge(s.handle, v))


class Prog:
    def __init__(self, nc):
        self.nc = nc
        self.sems = []
        self.q = {}
        for n in ["sync", "scalar", "vector", "gpsimd", "tensor"]:
            self.q[n] = Q(self, n)
        self.sync = self.q["sync"]
        self.act = self.q["scalar"]
        self.dve = self.q["vector"]
        self.pool = self.q["gpsimd"]
        self.pe = self.q["tensor"]
        self.state = {}
        self.dma_events = []

    def sem(self, name):
        s = Sem(name)
        self.sems.append(s)
        return s

    def _st(self, k):
        st = self.state.get(k)
        if st is None:
            st = {"w": None, "r": []}
            self.state[k] = st
        return st

    def _deps(self, q, r, w):
        for k in r:
            q.wait(self._st(k)["w"])
        for k in w:
            st = self._st(k)
            q.wait(st["w"])
            for ev in st["r"]:
                q.wait(ev)

    def _commit(self, ev, r, w):
        for k in r:
            self._st(k)["r"].append(ev)
        for k in w:
            self.state[k] = {"w": ev, "r": []}

    def op(self, q, fn, r=(), w=(), inc=True):
        self._deps(q, r, w)
        if inc:
            s = q.sem
            s.count += 1
            ev = (s, s.count)
            q.items.append(lambda e, fn=fn, s=s: fn(e).then_inc(s.handle, 1))
            for (pr, pw) in q.pending:
                self._commit(ev, pr, pw)
            q.pending = []
            self._commit(ev, r, w)
            q.last = ev
            return ev
        q.items.append(lambda e, fn=fn: fn(e))
        q.pending.append((tuple(r), tuple(w)))
        return None

    def dma(self, q, out, in_, sem, r=(), w=()):
        self._deps(q, r, w)
        sem.count += 16
        ev = (sem, sem.count)
        q.items.append(lambda e, out=out, in_=in_, sem=sem: e.dma_start(out=out, in_=in_).then_inc(sem.handle, 16))
        self._commit(ev, r, w)
        self.dma_events.append(ev)
        return ev

    def dma_multi(self, q, pairs, sem, r=(), w=()):
        self._deps(q, r, w)
        for (out, in_) in pairs:
            sem.count += 16
            q.items.append(lambda e, out=out, in_=in_, sem=sem: e.dma_start(out=out, in_=in_).then_inc(sem.handle, 16))
        ev = (sem, sem.count)
        self._commit(ev, r, w)
        return ev

    def barrier(self, queues=None):
        qs = queues or [self.pe, self.act, self.dve, self.pool]
        evs = [q.last for q in qs if q.last is not None]
        for q in qs:
            for ev in evs:
                if ev[0] is not q.sem:
                    q.wait(ev)

    def emit(self):
        nc = self.nc
        with contextlib.ExitStack() as st:
            for s in self.sems:
                s.handle = st.enter_context(nc.semaphore(s.name))
            block = st.enter_context(nc.Block())
            for n in ["sync", "scalar", "vector", "gpsimd", "tensor"]:
                items = self.q[n].items

                def body(eng, items=items):
                    for it in items:
                        it(eng)
                getattr(block, n)(body)


class Ring:
    def __init__(self, P, name, bufs):
        self.P = P
        self.bufs = bufs
        self.sems = [P.sem(f"{name}{i}") for i in range(len(bufs))]
        self.sems_hw = [P.sem(f"{name}h{i}") for i in range(len(bufs))]
        self.keys = [(name, i) for i in range(len(bufs))]
        self.i = 0

    def next(self, hw=False):
        i = self.i % len(self.bufs)
        self.i += 1
        return self.bufs[i], (self.sems_hw[i] if hw else self.sems[i]), self.keys[i]


def build(ntiles=4, layers=(0, 1, 2, 3), do_final=True):
    nc = bass.Bass("TRN2", target_bir_lowering=False)
    P = Prog(nc)
    pe, act, dve, pool, sync = P.pe, P.act, P.dve, P.pool, P.sync

    def din(name, shape, dt=F32):
        return nc.dram_tensor(name, list(shape), dt, kind="ExternalInput").ap()

    x_d = din("x", [SEQ, D])
    c_d = din("c", [128, 8])
    n1g_d = din("n1g", [128, 4, 8])
    n2g_d = din("n2g", [128, 4, 8])
    fing_d = din("fing", [128, 8])
    wada_d = din("w_ada", [4, D, 6 * D])
    bada_d = din("b_ada", [128, 4, 48])
    lre_d = din("lre", [128, 2, 32])
    lim_d = din("lim", [128, 2, 32])
    lst_d = din("lst", [128, 2, 32])
    bre_d = din("bre", [128, 2, 512])
    bim_d = din("bim", [128, 2, 512])
    crt_d = din("crt", [128, 2, 512])
    cit_d = din("cit", [128, 2, 512])
    dd_d = din("dd", [128, 2, 8])
    swo_d = din("ssm_w_out", [2, D, 2 * D])
    cwi_d = din("conv_w_in", [2, D, 3 * D])
    cw_d = din("cw", [128, 2, 8, 3])
    cwo_d = din("conv_w_out", [2, D, D])
    wfi_d = din("w_ffn_in", [4, D, 2 * FF])
    wfo_d = din("w_ffn_out", [4, FF, D])
    y_d = nc.dram_tensor("y", [SEQ, D], F32, kind="ExternalOutput").ap()
    wbd = nc.dram_tensor("wbd", [2, 8, 128, 2048], BF16, kind="Internal").ap()
    cbd = nc.dram_tensor("cbd", [2, 8, 128, 2048], BF16, kind="Internal").ap()
    wtd = nc.dram_tensor("wtd", [2, 8, 128, 1024], BF16, kind="Internal").ap()

    with contextlib.ExitStack() as st:
        def sb(name, shape, dt=F32):
            return st.enter_context(nc.sbuf_tensor("sb_" + name, list(shape), dt))

        xT = sb("xT", [128, 8, 1024])
        hb = sb("hb", [128, 8, 1024], BF16)
        hprev = sb("hprev", [128, 2, 8, 8], BF16)
        scr = sb("scr", [128, 16448])
        wA = [sb(f"wA{i}", [128, 4096], BF16) for i in range(3)]
        wB = [sb(f"wB{i}", [128, 5632], BF16) for i in range(2)]
        xin = sb("xin", [128, 2, 1024])
        sq = sb("sq", [128, 8, 512], BF16)
        rstd = sb("rstd", [128, 2, 512])
        tmpf = sb("tmpf", [128, 4, 512])
        gel = sb("gel", [128, 2, 1024])
        ident = sb("ident", [128, 128])
        ones_bf = sb("ones_bf", [128, 128], BF16)
        mods = sb("mods", [128, 4, 48])
        gsc = sb("gsc", [128, 4, 2, 8])
        n1g = sb("n1g", [128, 4, 8])
        n2g = sb("n2g", [128, 4, 8])
        fing = sb("fing", [128, 8])
        fing32 = sb("fing32", [128, 8])
        bada = sb("bada", [128, 4, 48])
        cin = sb("cin", [128, 8])
        cact = sb("cact", [128, 8])
        cw = sb("cw", [128, 2, 8, 3])
        dd = sb("dd", [128, 2, 8])
        halo = sb("halo", [128, 2, 8, 8])
        sstate = sb("sstate", [128, 2, 64])
        Ca = sb("Ca", [128, 2, 64])
        Cb = sb("Cb", [128, 2, 64])
        T1 = sb("T1", [128, 64])
        T2 = sb("T2", [128, 64])
        T3 = sb("T3", [128, 64])
        ps = [st.enter_context(nc.psum_tensor(f"ps{i}", [128, 512], F32)) for i in range(8)]
        PSK = [("ps", i) for i in range(8)]

        ringA = Ring(P, "wA", wA)
        ringB = Ring(P, "wB", wB)
        s_small = P.sem("small")
        s_small2 = P.sem("small2")
        s_xin = [P.sem("xin0"), P.sem("xin1")]
        s_out = [P.sem("out0"), P.sem("out1")]
        s_scr = [P.sem("scrd0"), P.sem("scrd1"), P.sem("scrd2")]

        small = [(cin, c_d, "cin"), (n1g, n1g_d, "n1g"), (n2g, n2g_d, "n2g"), (fing, fing_d, "fing"),
                 (bada, bada_d, "bada"), (cw, cw_d, "cw"), (dd, dd_d, "dd")]
        P.dma_multi(sync, [(t[:], d_) for (t, d_, _) in small], s_small, w=[k_ for (_, _, k_) in small])

        P.op(pool, lambda g: g.memset(ident[:], 1.0), w=["ident"])
        P.op(pool, lambda g: g.affine_select(out=ident[:], in_=ident[:], pattern=[[-1, 128]],
                                             compare_op=ALU.is_equal, fill=0.0, base=0, channel_multiplier=1),
             r=["ident"], w=["ident"])
        P.op(pool, lambda g: g.memset(ones_bf[:], 1.0), w=["ones"])
        P.op(pool, lambda g: g.memset(hprev[:], 0.0), w=["hprev"])
        P.op(pool, lambda g: g.memset(halo[:], 0.0), w=["halo"])
        P.op(pool, lambda g: g.memset(sstate[:], 0.0), w=["sstate"])

        P.op(act, lambda a: a.activation(out=cact[:], in_=cin[:], func=AF.Silu), r=["cin"], w=["cact"])
        adab = [xin[:, :, :].rearrange("p a (k n) -> p (a k) n", k=4),
                sq[:, :, :].rearrange("p k n -> p (k n)").bitcast(F32).rearrange("p (k n) -> p k n", k=8)]
        adak = [[("xin", 0), ("xin", 1)], ["sq"]]
        for i_ in range(3):
            adab.append(wA[i_][:, :].bitcast(F32).rearrange("p (k n) -> p k n", k=8))
            adak.append([ringA.keys[i_]])
        NAB = len(adab)
        s_ada = [P.sem(f"ada{i_}") for i_ in range(NAB)]

        rowsb = rstd[0:1, :, 0:256]

        def ada_gen():
            nb = 0
            deferred = None
            for li in range(4):
                wv = wada_d[li].rearrange("(k p) n -> p k n", p=128)
                mb = 6 + li % 2

                def flush(d):
                    b_, blk_, mb_ = d
                    for jj in range(2):
                        j = blk_ * 2 + jj
                        P.op(pe, lambda e, b_=b_, jj=jj, j=j, mb_=mb_: e.transpose(
                            out=ps[mb_][:, j:j + 1], in_=rowsb[0:1, b_, jj * 128:(jj + 1) * 128], identity=ident[0:1, 0:1]),
                            r=[("rowsb", b_), "ident"], w=[PSK[mb_]], inc=(jj == 1))
                for blk in range(24):
                    ab = nb % NAB
                    b = nb % 2
                    rb = 4 + nb % 2
                    nb += 1
                    akeys = adak[ab]
                    P.dma(sync, adab[ab], wv[:, :, blk * 256:(blk + 1) * 256], s_ada[ab], w=akeys)
                    for k in range(8):
                        P.op(pe, lambda e, rb=rb, k=k, ab=ab: e.matmul(
                            ps[rb][0:1, 0:256], lhsT=cact[:, k:k + 1], rhs=adab[ab][:, k, :],
                            start=(k == 0), stop=(k == 7)),
                            r=akeys + ["cact"], w=[PSK[rb]], inc=(k == 7))
                    if deferred is not None:
                        flush(deferred)
                    P.op(act, lambda a_, rb=rb, b=b: a_.activation(out=rowsb[0:1, b, :], in_=ps[rb][0:1, 0:256], func=AF.Copy),
                         r=[PSK[rb]], w=[("rowsb", b)])
                    deferred = (b, blk, mb)
                    yield
                flush(deferred)
                deferred = None
                P.op(dve, lambda v, li=li, mb=mb: v.tensor_tensor(out=mods[:, li, :], in0=ps[mb][:, 0:48], in1=bada[:, li, :],
                                                                  op=ALU.add), r=[PSK[mb], "bada"], w=["mods"])
                for which, ng, ngk in ((0, n1g, "n1g"), (1, n2g, "n2g")):
                    sc = mods[:, li, (1 + 3 * which) * 8:(2 + 3 * which) * 8]
                    P.op(dve, lambda v, li=li, which=which, ng=ng, sc=sc: v.scalar_tensor_tensor(
                        out=gsc[:, li, which, :], in0=sc, scalar=1.0, in1=ng[:, li, :], op0=ALU.add, op1=ALU.mult),
                        r=["mods", ngk], w=["gsc"])
                    P.op(dve, lambda v, li=li, which=which: v.tensor_scalar(
                        out=gsc[:, li, which, :], in0=gsc[:, li, which, :], scalar1=32.0, scalar2=None, op0=ALU.mult),
                        r=["gsc"], w=["gsc"])
            P.op(dve, lambda v: v.tensor_scalar(out=fing32[:], in0=fing[:], scalar1=32.0, scalar2=None, op0=ALU.mult),
                 r=["fing"], w=["fing32"])
        ada_it = ada_gen()
        if not (OPT & 1):
            for _ in ada_it:
                pass
            P.barrier()

        ssm_layers = [l for l in layers if l % 2 == 0]
        if ssm_layers and not (DBG & 2):
            _ssm_prologue(nc, P, st, locals())
        for _ in ada_it:
            pass
        P.barrier()

        env = dict(locals())
        for ti in range(ntiles):
            _load_tile(env, ti)
            for li in layers:
                bar = (lambda: None) if (OPT & 2) else (lambda: P.barrier([pe, act, dve]))
                _norm_mod(env, li, 0)
                bar()
                if li % 2 == 0:
                    _ssm_mixer(env, li, ti)
                else:
                    _conv_mixer(env, li, ti)
                bar()
                _norm_mod(env, li, 1)
                bar()
                _ffn(env, li)
                bar()
            _final_store(env, ti, do_final)
            bar()
        for s in s_out:
            if s.count:
                sync.wait((s, s.count))
        P.emit()
    return nc


def _ssm_prologue(nc, P, st, E):
    pe, act, dve, pool, sync = P.pe, P.act, P.dve, P.pool, P.sync
    scr, hb, ps, PSK, ident = E["scr"], E["hb"], E["ps"], E["PSK"], E["ident"]
    Ca, Cb = E["Ca"], E["Cb"]
    wbd, cbd, wtd = E["wbd"], E["cbd"], E["wtd"]
    s_scr = E["s_scr"]
    s_small = E["s_small"]

    def sb(name, shape, dt=F32):
        return st.enter_context(nc.sbuf_tensor("sp_" + name, list(shape), dt))

    class V:
        def __init__(self, name, ap):
            self.name = name
            self.ap = ap

        def __getitem__(self, k):
            return self.ap[k]

    t32c = {}

    def t32(nm):
        if nm not in t32c:
            t32c[nm] = sb("t_" + nm, [128, 32])
        return t32c[nm]

    xTf = E["xT"][:, :, :].rearrange("p k n -> p (k n)")
    gelf = E["gel"][:, :, :].rearrange("p a n -> p (a n)")
    tmpff = E["tmpf"][:, :, :].rearrange("p a n -> p (a n)")
    CBall = scr[:, 0:8192].bitcast(BF16).rearrange("p (a s r c) -> p a s r c", a=32, s=8, r=2)
    WBall = scr[:, 8192:16384].bitcast(BF16).rearrange("p (k s r c) -> p k s r c", k=8, s=8, r=2)
    WTall = hb[:, :, :].rearrange("p k (t c) -> p k t c", t=8)
    lre = sb("lre", [128, 2, 32]); lim = sb("lim", [128, 2, 32]); lst = sb("lst", [128, 2, 32])
    bre = V("bre", xTf[:, 0:1024].rearrange("p (j n) -> p j n", j=2))
    bim = V("bim", xTf[:, 1024:2048].rearrange("p (j n) -> p j n", j=2))
    crt = V("crt", xTf[:, 2048:3072].rearrange("p (j n) -> p j n", j=2))
    cit = V("cit", xTf[:, 3072:4096].rearrange("p (j n) -> p j n", j=2))
    Bpr = V("Bpr", xTf[:, 4096:5120].rearrange("p (a c) -> p a c", a=32))
    Bpi = V("Bpi", xTf[:, 5120:6144].rearrange("p (a c) -> p a c", a=32))
    Cpr = V("Cpr", xTf[:, 6144:7168].rearrange("p (a c) -> p a c", a=32))
    nCpi = V("nCpi", xTf[:, 7168:8192].rearrange("p (a c) -> p a c", a=32))
    Bbr = V("Bbr", gelf[:, 0:512].rearrange("p (a c) -> p a c", a=32))
    Bbi = V("Bbi", gelf[:, 512:1024].rearrange("p (a c) -> p a c", a=32))
    U1 = V("U1", gelf[:, 1024:1536].rearrange("p (a c) -> p a c", a=32))
    U2 = V("U2", gelf[:, 1536:2048].rearrange("p (a c) -> p a c", a=32))
    nCr = V("nCr", tmpff[:, 0:512].rearrange("p (a c) -> p a c", a=32))
    nCi = V("nCi", tmpff[:, 512:1024].rearrange("p (a c) -> p a c", a=32))
    P.dma_multi(sync, [(t[:], d_) for t, d_ in ((lre, E["lre_d"]), (lim, E["lim_d"]), (lst, E["lst_d"]), (bre, E["bre_d"]),
                                                  (bim, E["bim_d"]), (crt, E["crt_d"]), (cit, E["cit_d"]))], E["s_small2"],
                w=["lre", "lim", "lst", "bre", "bim", "crt", "cit"])

    def tt(out, a, b, op, r, w, q=None):
        return P.op(q or dve, lambda v: v.tensor_tensor(out=out, in0=a, in1=b, op=op), r=r, w=w)

    def bc(a):
        return a.unsqueeze(2).to_broadcast([128, 32, 16])

    P.op(pool, lambda g: g.memset(WTall, 0.0), w=["WTall"])
    tctr = [0]
    def do_layer(j):
        P.op(pool, lambda g: g.memset(scr[:, 0:8192], 0.0), w=["CBall"])
        for t in (Bpr, Bpi, Cpr, nCpi):
            P.op(pool, lambda g, t=t: g.memset(t[:], 0.0), w=[t.name])
        dt_ = t32("dt"); lr = t32("lr"); ph = t32("ph"); lrd = t32("lrd")
        s8 = t32("s8"); c8 = t32("c8"); m8 = t32("m8")
        P.op(act, lambda a: a.activation(out=dt_[:], in_=lst[:, j, :], func=AF.Exp), r=["lst"], w=[dt_.name])
        P.op(dve, lambda v: v.tensor_scalar(out=lr[:], in0=lre[:, j, :], scalar1=-1e-4, scalar2=None, op0=ALU.min),
             r=["lre"], w=[lr.name])
        li_ = lim[:, j, :]
        tt(ph[:], li_, dt_[:], ALU.mult, ["lim", dt_.name], [ph.name])
        tt(lrd[:], lr[:], dt_[:], ALU.mult, [lr.name, dt_.name], [lrd.name])
        P.op(act, lambda a: a.activation(out=s8[:], in_=ph[:], func=AF.Sin, scale=0.125), r=[ph.name], w=[s8.name])
        P.op(act, lambda a: a.activation(out=c8[:], in_=ph[:], func=AF.Sin, scale=-0.125, bias=math.pi / 2),
             r=[ph.name], w=[c8.name])
        P.op(act, lambda a: a.activation(out=m8[:], in_=lrd[:], func=AF.Exp, scale=0.125), r=[lrd.name], w=[m8.name])
        ar = t32("ar"); ai = t32("ai")
        tt(ar[:], m8[:], c8[:], ALU.mult, [m8.name, c8.name], [ar.name])
        tt(ai[:], m8[:], s8[:], ALU.mult, [m8.name, s8.name], [ai.name])
        for it in range(3):
            q1 = t32("q1"); q2 = t32("q2"); q3 = t32("q3"); nr = t32(f"nr{it}"); ni = t32(f"ni{it}")
            tt(q1[:], ar[:], ar[:], ALU.mult, [ar.name], [q1.name])
            tt(q2[:], ai[:], ai[:], ALU.mult, [ai.name], [q2.name])
            tt(q3[:], ar[:], ai[:], ALU.mult, [ar.name, ai.name], [q3.name])
            tt(nr[:], q1[:], q2[:], ALU.subtract, [q1.name, q2.name], [nr.name])
            tt(ni[:], q3[:], q3[:], ALU.add, [q3.name], [ni.name])
            ar, ai = nr, ni
        pw = [None] * 9
        one = t32("one"); zero = t32("zero")
        P.op(pool, lambda g: g.memset(one[:], 1.0), w=[one.name])
        P.op(pool, lambda g: g.memset(zero[:], 0.0), w=[zero.name])
        pw[0] = (one, zero)
        pw[1] = (ar, ai)

        def cmul(xr, xi, yr, yi, n):
            a1 = t32("a1"); a2 = t32("a2"); a3 = t32("a3"); a4 = t32("a4"); zr = t32(f"zr{n}"); zi = t32(f"zi{n}")
            tt(a1[:], xr[:], yr[:], ALU.mult, [xr.name, yr.name], [a1.name])
            tt(a2[:], xi[:], yi[:], ALU.mult, [xi.name, yi.name], [a2.name])
            tt(a3[:], xr[:], yi[:], ALU.mult, [xr.name, yi.name], [a3.name])
            tt(a4[:], xi[:], yr[:], ALU.mult, [xi.name, yr.name], [a4.name])
            tt(zr[:], a1[:], a2[:], ALU.subtract, [a1.name, a2.name], [zr.name])
            tt(zi[:], a3[:], a4[:], ALU.add, [a3.name, a4.name], [zi.name])
            return zr, zi
        for n in range(2, 9):
            pw[n] = cmul(pw[n - 1][0], pw[n - 1][1], ar, ai, n)
        a8r, a8i = pw[8]
        P.op(dve, lambda v: v.tensor_copy(out=Ca[:, j, 0:32], in_=a8r[:]), r=[a8r.name], w=["Ca"])
        P.op(dve, lambda v: v.tensor_copy(out=Ca[:, j, 32:64], in_=a8r[:]), r=[a8r.name], w=["Ca"])
        P.op(dve, lambda v: v.tensor_copy(out=Cb[:, j, 32:64], in_=a8i[:]), r=[a8i.name], w=["Cb"])
        P.op(dve, lambda v: v.tensor_scalar(out=Cb[:, j, 0:32], in0=a8i[:], scalar1=-1.0, scalar2=None, op0=ALU.mult),
             r=[a8i.name], w=["Cb"])
        den = t32("den"); d2 = t32("d2"); rden = t32("rden"); am1 = t32("am1")
        tt(den[:], lr[:], lr[:], ALU.mult, [lr.name], [den.name])
        tt(d2[:], li_, li_, ALU.mult, ["lim"], [d2.name])
        tt(den[:], den[:], d2[:], ALU.add, [den.name, d2.name], [den.name])
        P.op(dve, lambda v: v.reciprocal(out=rden[:], in_=den[:]), r=[den.name], w=[rden.name])
        P.op(dve, lambda v: v.tensor_scalar(out=am1[:], in0=ar[:], scalar1=-1.0, scalar2=None, op0=ALU.add),
             r=[ar.name], w=[am1.name])
        e1 = t32("e1"); e2 = t32("e2"); qr = t32("qr"); qi = t32("qi")
        tt(e1[:], am1[:], lr[:], ALU.mult, [am1.name, lr.name], [e1.name])
        tt(e2[:], ai[:], li_, ALU.mult, [ai.name, "lim"], [e2.name])
        tt(e1[:], e1[:], e2[:], ALU.add, [e1.name, e2.name], [e1.name])
        tt(qr[:], e1[:], rden[:], ALU.mult, [e1.name, rden.name], [qr.name])
        e3 = t32("e3"); e4 = t32("e4")
        tt(e3[:], ai[:], lr[:], ALU.mult, [ai.name, lr.name], [e3.name])
        tt(e4[:], am1[:], li_, ALU.mult, [am1.name, "lim"], [e4.name])
        tt(e3[:], e3[:], e4[:], ALU.subtract, [e3.name, e4.name], [e3.name])
        tt(qi[:], e3[:], rden[:], ALU.mult, [e3.name, rden.name], [qi.name])
        B_r = bre[:, j, :].rearrange("p (a h) -> p a h", h=16)
        B_i = bim[:, j, :].rearrange("p (a h) -> p a h", h=16)
        tt(U1[:], B_r, bc(qr[:]), ALU.mult, ["bre", qr.name], ["U1"])
        tt(U2[:], B_i, bc(qi[:]), ALU.mult, ["bim", qi.name], ["U2"])
        tt(Bbr[:], U1[:], U2[:], ALU.subtract, ["U1", "U2"], ["Bbr"])
        tt(U1[:], B_i, bc(qr[:]), ALU.mult, ["bim", qr.name], ["U1"])
        tt(U2[:], B_r, bc(qi[:]), ALU.mult, ["bre", qi.name], ["U2"])
        tt(Bbi[:], U1[:], U2[:], ALU.add, ["U1", "U2"], ["Bbi"])
        C_r = crt[:, j, :].rearrange("p (a h) -> p a h", h=16)
        C_i = cit[:, j, :].rearrange("p (a h) -> p a h", h=16)
        for (lo, hi, c0) in ((0, 64, 0), (64, 128, 16)):
            P.op(dve, lambda v, lo=lo, hi=hi, c0=c0: v.tensor_copy(out=Cpr[lo:hi, :, c0:c0 + 16], in_=C_r[lo:hi]),
                 r=["crt"], w=["Cpr"])
            P.op(dve, lambda v, lo=lo, hi=hi, c0=c0: v.tensor_scalar(
                out=nCpi[lo:hi, :, c0:c0 + 16], in0=C_i[lo:hi], scalar1=-1.0, scalar2=None, op0=ALU.mult),
                r=["cit"], w=["nCpi"])
        P.op(dve, lambda v: v.tensor_scalar(out=nCr[:], in0=C_r, scalar1=-1.0, scalar2=None, op0=ALU.mult),
             r=["crt"], w=["nCr"])
        P.op(dve, lambda v: v.tensor_scalar(out=nCi[:], in0=C_i, scalar1=-1.0, scalar2=None, op0=ALU.mult),
             r=["cit"], w=["nCi"])
        def do_s(s):
            pr, pi_ = pw[7 - s]
            tt(U1[:], Bbr[:], bc(pr[:]), ALU.mult, ["Bbr", pr.name], ["U1"])
            tt(U2[:], Bbi[:], bc(pi_[:]), ALU.mult, ["Bbi", pi_.name], ["U2"])
            for (lo, hi, c0) in ((0, 64, 0), (64, 128, 16)):
                tt(Bpr[lo:hi, :, c0:c0 + 16], U1[lo:hi], U2[lo:hi], ALU.subtract, ["U1", "U2"], ["Bpr"])
            tt(U1[:], Bbi[:], bc(pr[:]), ALU.mult, ["Bbi", pr.name], ["U1"])
            tt(U2[:], Bbr[:], bc(pi_[:]), ALU.mult, ["Bbr", pi_.name], ["U2"])
            for (lo, hi, c0) in ((0, 64, 0), (64, 128, 16)):
                tt(Bpi[lo:hi, :, c0:c0 + 16], U1[lo:hi], U2[lo:hi], ALU.add, ["U1", "U2"], ["Bpi"])
            tau = 7 - s
            for k in range(8 if not (DBG & 4) else 0):
                bk = tctr[0] % 4
                tctr[0] += 1
                for ri, Bp in enumerate((Bpr, Bpi)):
                    src = Bp[:, 4 * k:4 * k + 4, :].rearrange("p a c -> p (a c)")
                    P.op(pe, lambda e, src=src, bk=bk, ri=ri: e.transpose(
                        out=ps[bk][:, ri * 128:(ri + 1) * 128], in_=src, identity=ident[:]),
                        r=[Bp.name, "ident"], w=[PSK[bk]], inc=False)
                srcr = Bpr[:, 4 * k:4 * k + 4, :].rearrange("p a c -> p (a c)")
                srci = Bpi[:, 4 * k:4 * k + 4, :].rearrange("p a c -> p (a c)")
                cr = Cpr[:, 4 * k:4 * k + 4, :].rearrange("p a c -> p (a c)")
                ci = nCpi[:, 4 * k:4 * k + 4, :].rearrange("p a c -> p (a c)")
                P.op(pe, lambda e, bk=bk, srcr=srcr, cr=cr: e.matmul(ps[bk][:, 256:384], lhsT=srcr, rhs=cr,
                                                                     start=True, stop=False),
                     r=["Bpr", "Cpr"], w=[PSK[bk]], inc=False)
                P.op(pe, lambda e, bk=bk, srci=srci, ci=ci: e.matmul(ps[bk][:, 256:384], lhsT=srci, rhs=ci,
                                                                     start=False, stop=True),
                     r=["Bpi", "nCpi"], w=[PSK[bk]], inc=True)
                P.op(act, lambda a, bk=bk, k=k, s=s: a.activation(
                    out=WBall[:, k, s, :, :], in_=ps[bk][:, 0:256].rearrange("p (r c) -> p r c", r=2), func=AF.Copy),
                    r=[PSK[bk]], w=["WBall"])
                for jq in range(4):
                    P.op(dve, lambda v, bk=bk, k=k, tau=tau, jq=jq: v.tensor_copy(
                        out=WTall[32 * jq:32 * jq + 32, k, tau, 32 * jq:32 * jq + 32],
                        in_=ps[bk][32 * jq:32 * jq + 32, 256 + 32 * jq:256 + 32 * jq + 32]),
                        r=[], w=["WTall", PSK[bk]])
            qr_, qi_ = pw[s + 1]
            tt(U1[:], C_r, bc(qr_[:]), ALU.mult, ["crt", qr_.name], ["U1"])
            tt(U2[:], C_i, bc(qi_[:]), ALU.mult, ["cit", qi_.name], ["U2"])
            for (lo, hi, c0) in ((0, 64, 0), (64, 128, 16)):
                tt(CBall[lo:hi, :, s, 0, c0:c0 + 16], U1[lo:hi], U2[lo:hi], ALU.subtract, ["U1", "U2"], ["CBall"])
            tt(U1[:], nCr[:], bc(qi_[:]), ALU.mult, ["nCr", qi_.name], ["U1"])
            tt(U2[:], nCi[:], bc(qr_[:]), ALU.mult, ["nCi", qr_.name], ["U2"])
            for (lo, hi, c0) in ((0, 64, 0), (64, 128, 16)):
                tt(CBall[lo:hi, :, s, 1, c0:c0 + 16], U1[lo:hi], U2[lo:hi], ALU.add, ["U1", "U2"], ["CBall"])
        for s_ in range(8):
            do_s(s_)
            for _ in range(6):
                next(E["ada_it"], None)
        if DBG & 8:
            return
        P.dma(sync, wbd[j].rearrange("k p n -> p k n"), scr[:, 8192:16384].bitcast(BF16).rearrange("p (k n) -> p k n", k=8),
              s_scr[0], r=["WBall"], w=[("wbd", j)])
        P.dma(sync, cbd[j].rearrange("k p n -> p k n"), scr[:, 0:8192].bitcast(BF16).rearrange("p (k n) -> p k n", k=8),
              s_scr[1], r=["CBall"], w=[("cbd", j)])
        P.dma(sync, wtd[j].rearrange("k p n -> p k n"), hb[:, :, :], s_scr[2], r=["WTall"], w=[("wtd", j)])
    for j_ in range(2):
        if 2 * j_ in E["layers"]:
            do_layer(j_)
    for sm in s_scr:
        if sm.count:
            for q in (pe, act, dve, pool, sync):
                q.wait((sm, sm.count))


def _load_tile(E, ti):
    P = E["P"]; pe, act, dve, sync = P.pe, P.act, P.dve, P.sync
    xT, xin, ps, PSK, ident = E["xT"], E["xin"], E["ps"], E["PSK"], E["ident"]
    xTv = xT[:, :, :].rearrange("p k (s c) -> p k s c", s=8)
    for blk in range(8):
        b = blk % 2
        r0 = ti * TT + blk * 128
        P.dma(sync, xin[:, b, :], E["x_d"][r0:r0 + 128, :], E["s_xin"][b], w=[("xin", b)])
        for half in range(2):
            bk = (blk * 2 + half) % 4
            for kk in range(4):
                k = half * 4 + kk
                P.op(pe, lambda e, b=b, bk=bk, k=k, kk=kk: e.transpose(
                    out=ps[bk][:, kk * 128:(kk + 1) * 128], in_=xin[:, b, k * 128:(k + 1) * 128], identity=ident[:]),
                    r=[("xin", b), "ident"], w=[PSK[bk]], inc=(kk == 3))
            q = act if half == 0 else dve
            src = ps[bk][:, :].rearrange("p (k jc s) -> p k s jc", k=4, jc=16, s=8)
            dst = xTv[:, half * 4:half * 4 + 4, :, 16 * blk:16 * blk + 16]
            if q is act:
                P.op(q, lambda a, src=src, dst=dst: a.activation(out=dst, in_=src, func=AF.Copy), r=[PSK[bk]], w=["xT"])
            else:
                P.op(q, lambda v, src=src, dst=dst: v.tensor_copy(out=dst, in_=src), r=[PSK[bk]], w=["xT"])


def _final_store(E, ti, do_final):
    P = E["P"]; pe, act, dve, sync = P.pe, P.act, P.dve, P.sync
    xT, xin, ps, PSK, ident, scr = E["xT"], E["xin"], E["ps"], E["PSK"], E["ident"], E["scr"]
    if do_final:
        _norm_mod(E, None, 2)
        src_all = scr[:, 0:8192].rearrange("p (k s c) -> p k s c", k=8, s=8)
        skey = "fout"
    else:
        src_all = xT[:, :, :].rearrange("p k (s c) -> p k s c", s=8)
        skey = "xT"
    for blk in range(8):
        b = blk % 2
        for half in range(2):
            a = (blk * 2 + half) % 2
            bk = (blk * 2 + half) % 4
            stg = scr[:, 8192 + a * 512: 8192 + (a + 1) * 512]
            src = src_all[:, half * 4:half * 4 + 4, :, 16 * blk:16 * blk + 16]
            dstv = stg.rearrange("p (k jc s) -> p k s jc", k=4, jc=16, s=8)
            if half == 0:
                P.op(act, lambda e, src=src, dstv=dstv: e.activation(out=dstv, in_=src, func=AF.Copy),
                     r=[skey], w=[("stg", a)])
            else:
                P.op(dve, lambda e, src=src, dstv=dstv: e.tensor_copy(out=dstv, in_=src), r=[skey], w=[("stg", a)])
            for kk in range(4):
                P.op(pe, lambda e, stg=stg, bk=bk, kk=kk: e.transpose(
                    out=ps[bk][:, kk * 128:(kk + 1) * 128], in_=stg[:, kk * 128:(kk + 1) * 128], identity=ident[:]),
                    r=[("stg", a), "ident"], w=[PSK[bk]], inc=(kk == 3))
            dsto = xin[:, b, half * 512:(half + 1) * 512]
            if half == 0:
                P.op(dve, lambda e, bk=bk, dsto=dsto: e.tensor_copy(out=dsto, in_=ps[bk][:, :]),
                     r=[PSK[bk]], w=[("xin", b)])
            else:
                P.op(act, lambda e, bk=bk, dsto=dsto: e.activation(out=dsto, in_=ps[bk][:, :], func=AF.Copy),
                     r=[PSK[bk]], w=[("xin", b)])
        r0 = ti * TT + blk * 128
        P.dma(sync, E["y_d"][r0:r0 + 128, :], xin[:, b, :], E["s_out"][b], r=[("xin", b)])


def _norm_mod(E, li, which):
    P = E["P"]; pe, act, dve = P.pe, P.act, P.dve
    xT, hb, sq, rstd, tmpf, ps, PSK = E["xT"], E["hb"], E["sq"], E["rstd"], E["tmpf"], E["ps"], E["PSK"]
    ones_bf, gsc, mods, fing32, scr = E["ones_bf"], E["gsc"], E["mods"], E["fing32"], E["scr"]
    fout = scr[:, 0:8192].rearrange("p (k n) -> p k n", k=8)
    sqs = [sq[:, :, :], hb[:, :, 512:1024]]
    sqk = ["sq", ("hb", 1)]
    P.op(act, lambda a: a.activation(out=sqs[0], in_=xT[:, :, 0:512], func=AF.Square), r=["xT"], w=[sqk[0]])
    P.op(dve, lambda v: v.tensor_tensor(out=sqs[1], in0=xT[:, :, 512:1024], in1=xT[:, :, 512:1024], op=ALU.mult),
         r=["xT"], w=[sqk[1]])
    for tb in range(2):
        bk = 6 + tb
        for k in range(8):
            P.op(pe, lambda e, k=k, bk=bk, tb=tb: e.matmul(ps[bk][:, :], lhsT=ones_bf[:], rhs=sqs[tb][:, k, :],
                                                           start=(k == 0), stop=(k == 7)),
                 r=[sqk[tb], "ones"], w=[PSK[bk]], inc=(k == 7))
    for tb in range(2):
        bk = 6 + tb
        P.op(act, lambda a, tb=tb, bk=bk: a.activation(out=rstd[:, tb, :], in_=ps[bk][:, :], func=AF.Sqrt,
                                                       bias=D * EPS, scale=1.0),
             r=[PSK[bk]], w=[("rstd", tb)])
        P.op(dve, lambda v, tb=tb: v.reciprocal(out=rstd[:, tb, :], in_=rstd[:, tb, :]),
             r=[("rstd", tb)], w=[("rstd", tb)])
    for tb in range(2):
        sl = slice(tb * 512, (tb + 1) * 512)
        for k in range(8):
            if which == 2:
                P.op(dve, lambda v, k=k, sl=sl, tb=tb: v.scalar_tensor_tensor(
                    out=fout[:, k, sl], in0=xT[:, k, sl], scalar=fing32[:, k:k + 1], in1=rstd[:, tb, :],
                    op0=ALU.mult, op1=ALU.mult), r=["xT", ("rstd", tb), "fing32"], w=["fout"])
                continue
            tbuf = k % 4
            P.op(dve, lambda v, k=k, sl=sl, tb=tb, tbuf=tbuf: v.scalar_tensor_tensor(
                out=tmpf[:, tbuf, :], in0=xT[:, k, sl], scalar=gsc[:, li, which, k:k + 1], in1=rstd[:, tb, :],
                op0=ALU.mult, op1=ALU.mult), r=["xT", ("rstd", tb), "gsc"], w=[("tmpf", tbuf)])
            shc = (3 * which) * 8 + k
            P.op(act, lambda a, k=k, sl=sl, tbuf=tbuf, shc=shc: a.activation(
                out=hb[:, k, sl], in_=tmpf[:, tbuf, :], func=AF.Identity, bias=mods[:, li, shc:shc + 1], scale=1.0),
                r=[("tmpf", tbuf), "mods"], w=[("hb", tb)])


def _ffn(E, li):
    P = E["P"]; pe, act, dve, pool = P.pe, P.act, P.dve, P.pool
    xT, hb, scr, tmpf, ps, PSK, mods = E["xT"], E["hb"], E["scr"], E["tmpf"], E["ps"], E["PSK"], E["mods"]
    ringA, ringB = E["ringA"], E["ringB"]
    actb = scr[:, 0:11264].bitcast(BF16).rearrange("p (j n) -> p j n", j=FC)
    wi = E["wfi_d"][li].rearrange("(k p) n -> p k n", p=128)
    wo = E["wfo_d"][li].rearrange("(j p) n -> p j n", p=128)
    cnt = 0
    for grp in range(11):
        buf, sem, key = ringA.next()
        bv = buf[:, :].rearrange("p (g k n) -> p g k n", g=2, k=8)
        P.dma_multi(pool, [(bv[:, 0], wi[:, :, grp * 256:(grp + 1) * 256]),
                           (bv[:, 1], wi[:, :, FF + grp * 256:FF + (grp + 1) * 256])], sem, w=[key])
        for jj in range(2):
            j = grp * 2 + jj
            for tb in range(2):
                sl = slice(tb * 512, (tb + 1) * 512)
                gb = cnt % 2
                ub = 2 + cnt % 2
                cnt += 1
                for k in range(8):
                    P.op(pe, lambda e, bv=bv, jj=jj, k=k, sl=sl, gb=gb: e.matmul(
                        ps[gb][:, :], lhsT=bv[:, 0, k, jj * 128:(jj + 1) * 128], rhs=hb[:, k, sl],
                        start=(k == 0), stop=(k == 7)), r=[key, ("hb", tb)], w=[PSK[gb]], inc=(k == 7))
                for k in range(8):
                    P.op(pe, lambda e, bv=bv, jj=jj, k=k, sl=sl, ub=ub: e.matmul(
                        ps[ub][:, :], lhsT=bv[:, 1, k, jj * 128:(jj + 1) * 128], rhs=hb[:, k, sl],
                        start=(k == 0), stop=(k == 7)), r=[key, ("hb", tb)], w=[PSK[ub]], inc=(k == 7))
                tbuf = cnt % 4
                P.op(act, lambda a, gb=gb, tbuf=tbuf: a.activation(out=tmpf[:, tbuf, :], in_=ps[gb][:, :], func=AF.Silu),
                     r=[PSK[gb]], w=[("tmpf", tbuf)])
                P.op(dve, lambda v, ub=ub, tbuf=tbuf, j=j, sl=sl: v.tensor_tensor(
                    out=actb[:, j, sl], in0=ps[ub][:, :], in1=tmpf[:, tbuf, :], op=ALU.mult),
                    r=[PSK[ub], ("tmpf", tbuf)], w=[("actb", j)])
    cnt = 0
    for mg in range(4):
        buf, sem, key = ringB.next()
        bv = buf[:, :].rearrange("p (j n) -> p j n", j=FC)
        P.dma(pool, bv, wo[:, :, mg * 256:(mg + 1) * 256], sem, w=[key])
        for mm in range(2):
            m = mg * 2 + mm
            for tb in range(2):
                sl = slice(tb * 512, (tb + 1) * 512)
                ob = 4 + cnt % 2
                cnt += 1
                for j in range(FC):
                    P.op(pe, lambda e, bv=bv, mm=mm, j=j, sl=sl, ob=ob: e.matmul(
                        ps[ob][:, :], lhsT=bv[:, j, mm * 128:(mm + 1) * 128], rhs=actb[:, j, sl],
                        start=(j == 0), stop=(j == FC - 1)), r=[key, ("actb", j)], w=[PSK[ob]], inc=(j == FC - 1))
                gc = 5 * 8 + m
                P.op(dve, lambda v, ob=ob, m=m, sl=sl, gc=gc: v.scalar_tensor_tensor(
                    out=xT[:, m, sl], in0=ps[ob][:, :], scalar=mods[:, li, gc:gc + 1], in1=xT[:, m, sl],
                    op0=ALU.mult, op1=ALU.add), r=[PSK[ob], "mods", "xT"], w=["xT"])


def _conv_mixer(E, li, ti):
    P = E["P"]; pe, act, dve, pool = P.pe, P.act, P.dve, P.pool
    xT, hb, scr, tmpf, gel, ps, PSK, mods = E["xT"], E["hb"], E["scr"], E["tmpf"], E["gel"], E["ps"], E["PSK"], E["mods"]
    cw, halo = E["cw"], E["halo"]
    ringA, ringB = E["ringA"], E["ringB"]
    jl = li // 2
    mbuf = scr[:, 0:4096].bitcast(BF16).rearrange("p (k n) -> p k n", k=8)
    cvb = [scr[:, 4096 + i * 1032: 4096 + (i + 1) * 1032].rearrange("p (s c) -> p s c", s=8) for i in range(2)]
    accb = [scr[:, 6400 + i * 1024: 6400 + (i + 1) * 1024] for i in range(2)]
    wi = E["cwi_d"][jl].rearrange("(k p) n -> p k n", p=128)
    wo = E["cwo_d"][jl].rearrange("(k p) n -> p k n", p=128)
    for m in range(8):
        buf, sem, key = ringA.next()
        bv = buf[:, 0:3072].rearrange("p (g k n) -> p g k n", g=3, k=8)
        P.dma_multi(pool, [(bv[:, g], wi[:, :, g * D + m * 128: g * D + (m + 1) * 128]) for g in range(3)], sem, w=[key])
        cv = cvb[m % 2]
        acc = accb[m % 2]
        bgb = gel[:, m % 2, :]
        ck, ak, bk_ = ("cvb", m % 2), ("acc", m % 2), ("gel", m % 2)
        for tb in range(2):
            sl = slice(tb * 512, (tb + 1) * 512)
            base = 3 * ((2 * m + tb) % 2)
            for g in range(3):
                for k in range(8):
                    P.op(pe, lambda e, bv=bv, g=g, k=k, sl=sl, base=base: e.matmul(
                        ps[base + g][:, :], lhsT=bv[:, g, k, :], rhs=hb[:, k, sl], start=(k == 0), stop=(k == 7)),
                        r=[key, ("hb", tb)], w=[PSK[base + g]], inc=(k == 7))
            tbuf = (2 * m + tb) % 4
            P.op(act, lambda a, base=base, sl=sl, bgb=bgb: a.activation(out=bgb[:, sl], in_=ps[base][:, :], func=AF.Copy),
                 r=[PSK[base]], w=[bk_])
            P.op(act, lambda a, base=base, tbuf=tbuf: a.activation(out=tmpf[:, tbuf, :], in_=ps[base + 1][:, :], func=AF.Copy),
                 r=[PSK[base + 1]], w=[("tmpf", tbuf)])
            P.op(dve, lambda v, base=base, tbuf=tbuf, cv=cv, tb=tb: v.tensor_tensor(
                out=cv[:, 4 * tb:4 * tb + 4, 1:129], in0=ps[base + 2][:, :].rearrange("p (s c) -> p s c", s=4),
                in1=tmpf[:, tbuf, :].rearrange("p (s c) -> p s c", s=4), op=ALU.mult),
                r=[PSK[base + 2], ("tmpf", tbuf)], w=[ck])
        P.op(dve, lambda v, cv=cv, m=m: v.tensor_copy(out=cv[:, :, 0], in_=halo[:, jl, m, :]), r=["halo", ck], w=[ck])
        P.op(dve, lambda v, cv=cv, m=m: v.tensor_copy(out=halo[:, jl, m, :], in_=cv[:, :, 128]), r=[ck, "halo"], w=["halo"])
        a3 = acc.rearrange("p (s c) -> p s c", s=8)
        w0 = cw[:, jl, m, 0:1]; w1 = cw[:, jl, m, 1:2]; w2 = cw[:, jl, m, 2:3]
        P.op(dve, lambda v, a3=a3, cv=cv, w2=w2: v.tensor_scalar(out=a3, in0=cv[:, :, 1:129], scalar1=w2, scalar2=None,
                                                                op0=ALU.mult), r=[ck, "cw"], w=[ak])
        for (o_, i_, wv) in ((a3[:, 1:8, :], cv[:, 0:7, 1:129], w1), (a3[:, 0:1, :], cv[:, 7:8, 0:128], w1),
                             (a3[:, 2:8, :], cv[:, 0:6, 1:129], w0), (a3[:, 0:2, :], cv[:, 6:8, 0:128], w0)):
            P.op(dve, lambda v, o_=o_, i_=i_, wv=wv: v.scalar_tensor_tensor(
                out=o_, in0=i_, scalar=wv, in1=o_, op0=ALU.mult, op1=ALU.add), r=[ck, ak, "cw"], w=[ak])
        P.op(dve, lambda v, acc=acc, bgb=bgb, m=m: v.tensor_tensor(out=mbuf[:, m, :], in0=acc, in1=bgb, op=ALU.mult),
             r=[ak, bk_], w=[("mbuf", m)])
    cnt = 0
    for mg in range(4):
        buf, sem, key = ringB.next()
        bv = buf[:, 0:2048].rearrange("p (k n) -> p k n", k=8)
        P.dma(pool, bv, wo[:, :, mg * 256:(mg + 1) * 256], sem, w=[key])
        for mm in range(2):
            m = mg * 2 + mm
            for tb in range(2):
                sl = slice(tb * 512, (tb + 1) * 512)
                ob = 6 + cnt % 2
                cnt += 1
                for k in range(8):
                    P.op(pe, lambda e, bv=bv, mm=mm, k=k, sl=sl, ob=ob: e.matmul(
                        ps[ob][:, :], lhsT=bv[:, k, mm * 128:(mm + 1) * 128], rhs=mbuf[:, k, sl],
                        start=(k == 0), stop=(k == 7)), r=[key, ("mbuf", k)], w=[PSK[ob]], inc=(k == 7))
                gc = 2 * 8 + m
                P.op(dve, lambda v, ob=ob, m=m, sl=sl, gc=gc: v.scalar_tensor_tensor(
                    out=xT[:, m, sl], in0=ps[ob][:, :], scalar=mods[:, li, gc:gc + 1], in1=xT[:, m, sl],
                    op0=ALU.mult, op1=ALU.add), r=[PSK[ob], "mods", "xT"], w=["xT"])


def _ssm_mixer(E, li, ti):
    P = E["P"]; pe, act, dve, pool, sync = P.pe, P.act, P.dve, P.pool, P.sync
    xT, hb, scr, tmpf, gel, ps, PSK, mods = E["xT"], E["hb"], E["scr"], E["tmpf"], E["gel"], E["ps"], E["PSK"], E["mods"]
    hprev, sstate, Ca, Cb, dd = E["hprev"], E["sstate"], E["Ca"], E["Cb"], E["dd"]
    T1, T2, T3 = E["T1"], E["T2"], E["T3"]
    ringA = E["ringA"]
    jl = li // 2
    XS = scr[:, 0:8256].rearrange("p (r c) -> p r c", r=64)
    XSb = scr[:, 8256:12352].bitcast(BF16).rearrange("p (r c) -> p r c", r=64)
    ybuf = scr[:, 12352:16448].bitcast(BF16).rearrange("p (k n) -> p k n", k=8)
    hv = hb[:, :, :].rearrange("p k (s c) -> p k s c", s=8)
    P.op(dve, lambda v: v.tensor_copy(out=XS[:, :, 0], in_=sstate[:, jl, :]), r=["sstate"], w=["XS"])
    for k in range(8):
        buf, sem, key = ringA.next(hw=True)
        bv = buf[:, 0:2048].rearrange("p (s r c) -> p s r c", s=8, r=2)
        P.dma(sync, buf[:, 0:2048], E["wbd"][jl, k], sem, r=[("wbd", jl)], w=[key])
        for ri in range(2):
            for s in range(8):
                for jq in range(4):
                    p0 = 32 * jq
                    P.op(pe, lambda e, bv=bv, p0=p0, ri=ri, s=s, k=k, jq=jq: e.matmul(
                        ps[ri * 4 + jq][:, 0:128], lhsT=bv[p0:p0 + 32, s, ri, :],
                        rhs=hv[p0:p0 + 32, k, s, :], start=(s == 0), stop=(s == 7), tile_position=(p0, 0)),
                        r=[key, ("hb", s // 4)], w=[PSK[ri * 4 + jq]], inc=(s == 7 and jq == 3))
            for jq in range(4):
                pi_ = 4 * k + jq
                dst = XS[:, ri * 32 + pi_, 1:129]
                src = ps[ri * 4 + jq][:, 0:128]
                if jq % 2 == 0:
                    P.op(act, lambda a, dst=dst, src=src: a.activation(out=dst, in_=src, func=AF.Copy),
                         r=[PSK[ri * 4 + jq]], w=["XS"])
                else:
                    P.op(dve, lambda v, dst=dst, src=src: v.tensor_copy(out=dst, in_=src), r=[PSK[ri * 4 + jq]], w=["XS"])
    def mm(out, lhsT, rhs, rk, start=False, bank=0, inc=False, tp=None):
        if tp is None:
            P.op(pe, lambda e: e.matmul(out, lhsT=lhsT, rhs=rhs, start=start, stop=inc, skip_group_check=True),
                 r=rk, w=[PSK[bank]], inc=inc)
        else:
            P.op(pe, lambda e: e.matmul(out, lhsT=lhsT, rhs=rhs, start=start, stop=inc, skip_group_check=True,
                                        tile_position=tp),
                 r=rk, w=[PSK[bank]], inc=inc)

    def near(k, wtv, yb, rk):
        for hf in range(2):
            mm(ps[yb[hf]][:, :], wtv[:, 0, :], hb[:, k, hf * 512:(hf + 1) * 512], rk, start=True, bank=yb[hf])
        for tau in range(1, 8):
            lo = tau * 128
            if lo < 512:
                mm(ps[yb[0]][:, lo:512], wtv[:, tau, :], hb[:, k, 0:512 - lo], rk, bank=yb[0])
            lo2 = max(512, lo)
            mm(ps[yb[1]][:, lo2 - 512:512], wtv[:, tau, :], hb[:, k, lo2 - lo:1024 - lo], rk, bank=yb[1])
            for s in range(tau):
                hf, sc = s // 4, s % 4
                src_s = s + 8 - tau
                mm(ps[yb[hf]][:, sc * 128 + 1:sc * 128 + 128], wtv[:, tau, :],
                   hb[:, k, src_s * 128:src_s * 128 + 127], rk, bank=yb[hf])
                mm(ps[yb[hf]][:, sc * 128:sc * 128 + 1], wtv[:, tau, :], hprev[:, jl, k, src_s:src_s + 1], rk, bank=yb[hf])

    def far(k, cbv, yb, rk):
        for s in range(8):
            hf, sc = s // 4, s % 4
            for ri in range(2):
                for jq in range(4):
                    pi_ = 4 * k + jq
                    last = (jq == 3 and s in (3, 7) and ri == 1)
                    mm(ps[yb[hf]][32 * jq:32 * jq + 32, sc * 128:(sc + 1) * 128], cbv[:, jq, s, ri, :],
                       XSb[:, ri * 32 + pi_, :], rk, bank=yb[hf], inc=last, tp=(0, 32 * jq))

    def evac(k, yb):
        for hf in range(2):
            sl = slice(hf * 512, (hf + 1) * 512)
            g_ = gel[:, hf, 0:512]
            P.op(dve, lambda v, hf=hf, sl=sl, k=k, g_=g_, yb=yb: v.scalar_tensor_tensor(
                out=g_, in0=hb[:, k, sl], scalar=dd[:, jl, k:k + 1], in1=ps[yb[hf]][:, :],
                op0=ALU.mult, op1=ALU.add), r=[("hb", hf), "dd", PSK[yb[hf]]], w=[("gel", hf)])
            P.op(act, lambda a, sl=sl, k=k, g_=g_: a.activation(out=ybuf[:, k, sl], in_=g_, func=AF.Gelu_apprx_tanh),
                 r=[("gel", hf)], w=[("ybuf", k)])

    bufW, semW, keyW = ringA.next(hw=True)
    P.dma(sync, bufW[:, :].rearrange("p (k n) -> p k n", k=4), E["wtd"][jl, 0:4].rearrange("k p n -> p k n"), semW,
          r=[("wtd", jl)], w=[keyW])
    for k in range(4):
        wtv = bufW[:, k * 1024:(k + 1) * 1024].rearrange("p (t c) -> p t c", t=8)
        near(k, wtv, [2 * k, 2 * k + 1], [keyW, ("hb", 0), ("hb", 1), "hprev"])
    xs_t = XS.tensor
    pstep = XS.ap[0][0]
    for c in range(128 if not (DBG & 1) else 0):
        cur = XS[:, :, c]
        nxt = XS[:, :, c + 1]
        swp = bass.AP(xs_t, XS.offset + 32 * 129 + c, [[pstep, 128], [-32 * 129, 2], [129, 32]])
        P.op(dve, lambda v, cur=cur: v.tensor_tensor(out=T1[:], in0=cur, in1=Ca[:, jl, :], op=ALU.mult),
             r=["XS", "Ca"], w=["T1"])
        P.op(dve, lambda v, swp=swp: v.tensor_tensor(out=T2[:].rearrange("p (r c) -> p r c", r=2), in0=swp,
                                                     in1=Cb[:, jl, :].rearrange("p (r c) -> p r c", r=2), op=ALU.mult),
             r=["XS", "Cb"], w=["T2"])
        P.op(dve, lambda v: v.tensor_tensor(out=T3[:], in0=T1[:], in1=T2[:], op=ALU.add), r=["T1", "T2"], w=["T3"])
        P.op(dve, lambda v, nxt=nxt: v.tensor_tensor(out=nxt, in0=T3[:], in1=nxt, op=ALU.add), r=["T3", "XS"], w=["XS"])
    P.op(dve, lambda v: v.tensor_copy(out=sstate[:, jl, :], in_=XS[:, :, 128]), r=["XS"], w=["sstate"])
    for q4 in range(4):
        P.op(act, lambda a, q4=q4: a.activation(out=XSb[:, 16 * q4:16 * q4 + 16, :], in_=XS[:, 16 * q4:16 * q4 + 16, 0:128],
                                                 func=AF.Copy), r=["XS"], w=["XSb"])
    for k in range(8):
        buf, sem, key = ringA.next(hw=True)
        cbv = buf[:, 0:2048].rearrange("p (a s r c) -> p a s r c", a=4, s=8, r=2)
        yb = [2 * (k % 4), 2 * (k % 4) + 1]
        if k < 4:
            P.dma(sync, buf[:, 0:2048], E["cbd"][jl, k], sem, r=[("cbd", jl)], w=[key])
        else:
            wtv = buf[:, 2048:3072].rearrange("p (t c) -> p t c", t=8)
            P.dma_multi(sync, [(buf[:, 0:2048], E["cbd"][jl, k]), (buf[:, 2048:3072], E["wtd"][jl, k])], sem,
                        r=[("cbd", jl), ("wtd", jl)], w=[key])
            near(k, wtv, yb, [key, ("hb", 0), ("hb", 1), "hprev"])
        far(k, cbv, yb, [key, "XSb"])
        evac(k, yb)
    P.op(dve, lambda v: v.tensor_copy(out=hprev[:, jl, :, :], in_=hv[:, :, :, 127]), r=[("hb", 0), ("hb", 1), "hprev"], w=["hprev"])
    wo = E["swo_d"][jl].rearrange("(k p) n -> p k n", p=128)
    cnt = 0
    for mg in range(4):
        buf, sem, key = ringA.next()
        bv = buf[:, :].rearrange("p (g k n) -> p g k n", g=2, k=8)
        P.dma_multi(pool, [(bv[:, 0], wo[:, :, mg * 256:(mg + 1) * 256]),
                           (bv[:, 1], wo[:, :, D + mg * 256:D + (mg + 1) * 256])], sem, w=[key])
        for mm_ in range(2):
            m = mg * 2 + mm_
            for tb in range(2):
                sl = slice(tb * 512, (tb + 1) * 512)
                vb = 4 + cnt % 2
                gb = 6 + cnt % 2
                cnt += 1
                for g, bnk in ((0, vb), (1, gb)):
                    for k in range(8):
                        P.op(pe, lambda e, bv=bv, g=g, mm_=mm_, k=k, sl=sl, bnk=bnk: e.matmul(
                            ps[bnk][:, :], lhsT=bv[:, g, k, mm_ * 128:(mm_ + 1) * 128], rhs=ybuf[:, k, sl],
                            start=(k == 0), stop=(k == 7)), r=[key, ("ybuf", k)], w=[PSK[bnk]], inc=(k == 7))
                tbuf = cnt % 4
                P.op(act, lambda a, gb=gb, tbuf=tbuf: a.activation(out=tmpf[:, tbuf, :], in_=ps[gb][:, :], func=AF.Sigmoid),
                     r=[PSK[gb]], w=[("tmpf", tbuf)])
                P.op(dve, lambda v, vb=vb, tbuf=tbuf: v.tensor_tensor(out=tmpf[:, tbuf, :], in0=ps[vb][:, :],
                                                                      in1=tmpf[:, tbuf, :], op=ALU.mult),
                     r=[PSK[vb], ("tmpf", tbuf)], w=[("tmpf", tbuf)])
                gc = 2 * 8 + m
                P.op(dve, lambda v, tbuf=tbuf, m=m, sl=sl, gc=gc: v.scalar_tensor_tensor(
                    out=xT[:, m, sl], in0=tmpf[:, tbuf, :], scalar=mods[:, li, gc:gc + 1], in1=xT[:, m, sl],
                    op0=ALU.mult, op1=ALU.add), r=[("tmpf", tbuf), "mods", "xT"], w=["xT"])


def _prep_inputs(inp, b):
    f = np.float32
    g = lambda a: np.ascontiguousarray(np.asarray(a, dtype=f))

    def pk(v):
        v = np.asarray(v, dtype=f)
        lead = v.shape[:-1]
        return g(np.moveaxis(v.reshape(lead + (8, 128)), -1, 0))

    def gp(a):
        a = np.asarray(a, dtype=f).reshape(2, 32, 2, 64)
        return g(a.transpose(2, 3, 0, 1).reshape(128, 2, 32))
    m = {}
    m["x"] = g(inp["x"][b])
    m["c"] = pk(inp["c"][b])
    m["n1g"] = pk(inp["norm1_g"])
    m["n2g"] = pk(inp["norm2_g"])
    m["fing"] = pk(inp["final_g"])
    m["w_ada"] = g(inp["w_ada"])
    m["b_ada"] = g(np.asarray(inp["b_ada"], dtype=f).reshape(4, 48, 128).transpose(2, 0, 1))
    m["lre"] = gp(inp["ssm_a_re"])
    m["lim"] = gp(inp["ssm_a_im"])
    ls = np.broadcast_to(np.asarray(inp["ssm_log_step"], dtype=f)[:, :, None], (2, 64, 64))
    m["lst"] = gp(ls)
    for nm, src in (("bre", "ssm_b_re"), ("bim", "ssm_b_im")):
        a = np.asarray(inp[src], dtype=f).reshape(2, 32, 2, 64, 16)
        m[nm] = g(a.transpose(2, 3, 0, 1, 4).reshape(128, 2, 512))
    for nm, src in (("crt", "ssm_c_re"), ("cit", "ssm_c_im")):
        a = np.asarray(inp[src], dtype=f).reshape(2, 32, 2, 16, 64)
        m[nm] = g(a.transpose(2, 4, 0, 1, 3).reshape(128, 2, 512))
    m["dd"] = pk(inp["ssm_d"])
    m["ssm_w_out"] = g(inp["ssm_w_out"])
    m["conv_w_in"] = g(inp["conv_w_in"])
    cwv = np.asarray(inp["conv_w"], dtype=f).reshape(2, 3, 8, 128)
    m["cw"] = g(cwv.transpose(3, 0, 2, 1))
    m["conv_w_out"] = g(inp["conv_w_out"])
    m["w_ffn_in"] = g(inp["w_ffn_in"])
    m["w_ffn_out"] = g(inp["w_ffn_out"])
    return m


_NC_CACHE = {}


def kernel(**inputs):
    if "full" not in _NC_CACHE:
        _NC_CACHE["full"] = build()
    nc = _NC_CACHE["full"]
    n = 8
    in_maps = [_prep_inputs(inputs, b) for b in range(n)]
    res = run_bass_kernel_spmd(nc, in_maps, core_ids=list(range(n)))
    out = np.stack([np.asarray(r["y"], dtype=np.float32) for r in res.results], axis=0)
    return out
```
